# Optimizing a Trainium2 kernel written in Bass

```python
import math
import jax, jax.numpy as jnp
from jax import lax
import numpy as np

D_MODEL = 1024
BATCH = 16
SEQ = 2048
DEPTH = 4
DEC_BATCH = 128
DEC_SEQ = 4
PAST_LEN = 8192
PAGE_SIZE = 128

N_A_LAYERS = DEPTH // 2
N_B_LAYERS = DEPTH - N_A_LAYERS
SSM_EXPAND = 2
SSM_WIDTH = SSM_EXPAND * D_MODEL
GROUP_CH = 16
N_GROUPS = SSM_WIDTH // GROUP_CH
STATE_DIM = 64
SCAN_CHUNK = 128
N_HEADS = D_MODEL // 128
QK_NOPE = 128
QK_ROPE = 64
V_HEAD = 128
KV_LORA = D_MODEL // 4
Q_LORA = 3 * D_MODEL // 8
ATTN_WIDTH = N_HEADS * V_HEAD
Q_BLOCK = 128
ROPE_THETA = 10000.0
RMS_EPS = 1e-6
SOFTMAX_SCALE = 1.0 / math.sqrt(QK_NOPE + QK_ROPE)
NEG_INF = -1e30

kernel_name = 'yoco_s5_mla_hybrid_step'


def rms_norm(x, g):
    x32 = x.astype(jnp.float32)
    y = x32 * lax.rsqrt(jnp.mean(x32 * x32, axis=-1, keepdims=True) + RMS_EPS) * g.astype(jnp.float32)
    return y.astype(x.dtype)


def rope(x, pos):
    half = x.shape[-1] // 2
    inv = ROPE_THETA ** (-jnp.arange(half, dtype=jnp.float32) / half)
    ang = pos.astype(jnp.float32)[:, None] * inv[None, :]
    shape = (1, pos.shape[0]) + (1,) * (x.ndim - 3) + (half,)
    cos = jnp.cos(ang).reshape(shape)
    sin = jnp.sin(ang).reshape(shape)
    x1 = x[..., :half].astype(jnp.float32)
    x2 = x[..., half:].astype(jnp.float32)
    return jnp.concatenate([x1 * cos - x2 * sin, x1 * sin + x2 * cos], axis=-1).astype(x.dtype)


def s5_scan(u, h0_re, h0_im, a_re, a_im, log_dt, b_re, b_im, c_re, c_im, d_skip):
    bsz, L, _ = u.shape
    u32 = u.astype(jnp.float32).reshape(bsz, L, N_GROUPS, GROUP_CH)
    lam_re = a_re.astype(jnp.float32)
    lam_im = a_im.astype(jnp.float32)
    dt = jnp.exp(log_dt.astype(jnp.float32))[:, None]
    mag = jnp.exp(lam_re * dt)
    lb_re = mag * jnp.cos(lam_im * dt)
    lb_im = mag * jnp.sin(lam_im * dt)
    den = lam_re * lam_re + lam_im * lam_im
    nr = lb_re - 1.0
    f_re = (nr * lam_re + lb_im * lam_im) / den
    f_im = (lb_im * lam_re - nr * lam_im) / den
    br = b_re.astype(jnp.float32)
    bi = b_im.astype(jnp.float32)
    bb_re = f_re[..., None] * br - f_im[..., None] * bi
    bb_im = f_re[..., None] * bi + f_im[..., None] * br
    cr = c_re.astype(jnp.float32)
    ci = c_im.astype(jnp.float32)
    ch = SCAN_CHUNK if L % SCAN_CHUNK == 0 else L
    nc = L // ch
    u_chunks = u32.reshape(bsz, nc, ch, N_GROUPS, GROUP_CH).swapaxes(0, 1)

    def combine(e1, e2):
        a1r, a1i, b1r, b1i = e1
        a2r, a2i, b2r, b2i = e2
        return (a1r * a2r - a1i * a2i, a1r * a2i + a1i * a2r,
                a2r * b1r - a2i * b1i + b2r, a2r * b1i + a2i * b1r + b2i)

    def step(carry, uc):
        h_re, h_im = carry
        bu_re = jnp.einsum('gpc,blgc->blgp', bb_re, uc)
        bu_im = jnp.einsum('gpc,blgc->blgp', bb_im, uc)
        a_r = jnp.broadcast_to(lb_re, bu_re.shape)
        a_i = jnp.broadcast_to(lb_im, bu_im.shape)
        cum_r, cum_i, loc_r, loc_i = lax.associative_scan(combine, (a_r, a_i, bu_re, bu_im), axis=1)
        xr = cum_r * h_re[:, None] - cum_i * h_im[:, None] + loc_r
        xi = cum_r * h_im[:, None] + cum_i * h_re[:, None] + loc_i
        y = jnp.einsum('gcp,blgp->blgc', cr, xr) - jnp.einsum('gcp,blgp->blgc', ci, xi)
        return (xr[:, -1], xi[:, -1]), y

    (h_re, h_im), ys = lax.scan(step, (h0_re.astype(jnp.float32), h0_im.astype(jnp.float32)), u_chunks)
    y = ys.swapaxes(0, 1).reshape(bsz, L, SSM_WIDTH) + d_skip.astype(jnp.float32) * u32.reshape(bsz, L, SSM_WIDTH)
    return y.astype(u.dtype), h_re, h_im


def s5_block(x, h0_re, h0_im, g, w_in, a_re, a_im, log_dt, b_re, b_im, c_re, c_im, d_skip, w_glu, b_glu, w_out):
    h = rms_norm(x, g)
    u, z = jnp.split(h @ w_in, 2, axis=-1)
    y, h_re, h_im = s5_scan(u, h0_re, h0_im, a_re, a_im, log_dt, b_re, b_im, c_re, c_im, d_skip)
    ga, gb = jnp.split(jax.nn.gelu(y) @ w_glu + b_glu, 2, axis=-1)
    v = ga * jax.nn.sigmoid(gb) * jax.nn.silu(z)
    return x + v @ w_out, h_re, h_im


def shared_latent(x, pos, g_in, w_dkv, g_lat):
    h = rms_norm(x, g_in)
    ckv = h @ w_dkv
    return rms_norm(ckv[..., :KV_LORA], g_lat), rope(ckv[..., KV_LORA:], pos)


def latent_attention(q_lat, q_rope, keys_lat, keys_kr, q_pos, k_pos):
    bsz, L, H, C = q_lat.shape
    qb = Q_BLOCK if L % Q_BLOCK == 0 else L
    nb = L // qb

    def to_blocks(t):
        return t.reshape((bsz, nb, qb) + t.shape[2:]).swapaxes(0, 1)

    def attend_block(args):
        ql, qr, qp = args
        s = (jnp.einsum('bqhc,bkc->bhqk', ql, keys_lat, preferred_element_type=jnp.float32)
             + jnp.einsum('bqhr,bkr->bhqk', qr, keys_kr, preferred_element_type=jnp.float32))
        s = jnp.where(k_pos[None, :] <= qp[:, None], s * SOFTMAX_SCALE, NEG_INF)
        p = jax.nn.softmax(s, axis=-1).astype(keys_lat.dtype)
        return jnp.einsum('bhqk,bkc->bqhc', p, keys_lat)

    o = lax.map(attend_block, (to_blocks(q_lat), to_blocks(q_rope), q_pos.reshape(nb, qb)))
    return o.swapaxes(0, 1).reshape(bsz, L, H, C)


def mla_block(x, q_pos, keys_lat, keys_kr, k_pos, w_uk, w_uv, g, w_in, g_q, w_uq, w_out):
    bsz, L, _ = x.shape
    h = rms_norm(x, g)
    cq_gate = h @ w_in
    cq = rms_norm(cq_gate[..., :Q_LORA], g_q)
    gate = cq_gate[..., Q_LORA:]
    q = (cq @ w_uq).reshape(bsz, L, N_HEADS, QK_NOPE + QK_ROPE)
    q_nope = q[..., :QK_NOPE]
    q_rope = rope(q[..., QK_NOPE:], q_pos)
    q_lat = jnp.einsum('blhn,chn->blhc', q_nope, w_uk.reshape(KV_LORA, N_HEADS, QK_NOPE))
    o_lat = latent_attention(q_lat, q_rope, keys_lat, keys_kr, q_pos, k_pos)
    o = jnp.einsum('blhc,chv->blhv', o_lat, w_uv.reshape(KV_LORA, N_HEADS, V_HEAD)).reshape(bsz, L, ATTN_WIDTH)
    return x + (o * jax.nn.silu(gate)) @ w_out


def setup_inputs(seed: int = 0) -> dict:
    key = jax.random.key(seed)
    ks = iter(jax.random.split(key, 48))

    def nrm(shape, scale):
        return jax.random.normal(next(ks), shape, jnp.float32) * scale

    n_pages = PAST_LEN // PAGE_SIZE
    n_used = DEC_BATCH * n_pages
    n_pool = n_used + n_used // 4
    na, nb = N_A_LAYERS, N_B_LAYERS
    x_prompt = nrm((BATCH, SEQ, D_MODEL), 1.0)
    x_sample = nrm((DEC_BATCH, DEC_SEQ, D_MODEL), 1.0)
    cache_latent = nrm((n_pool, PAGE_SIZE, KV_LORA), 1.0)
    cache_krope = nrm((n_pool, PAGE_SIZE, QK_ROPE), 1.0)
    page_table = jax.random.permutation(next(ks), n_pool)[:n_used].reshape(DEC_BATCH, n_pages).astype(jnp.int32)
    state_ssm_re = nrm((na, DEC_BATCH, N_GROUPS, STATE_DIM), 0.5)
    state_ssm_im = nrm((na, DEC_BATCH, N_GROUPS, STATE_DIM), 0.5)
    norm_a = 1.0 + nrm((na, D_MODEL), 0.02)
    w_in_a = nrm((na, D_MODEL, 2 * SSM_WIDTH), D_MODEL ** -0.5)
    a_re = -0.5 + nrm((na, N_GROUPS, STATE_DIM), 0.01)
    a_im = jnp.pi * jnp.arange(STATE_DIM, dtype=jnp.float32) + nrm((na, N_GROUPS, STATE_DIM), 0.01)
    log_dt = jax.random.uniform(next(ks), (na, N_GROUPS), jnp.float32, math.log(0.001), math.log(0.1))
    b_re = nrm((na, N_GROUPS, STATE_DIM, GROUP_CH), (2 * GROUP_CH) ** -0.5)
    b_im = nrm((na, N_GROUPS, STATE_DIM, GROUP_CH), (2 * GROUP_CH) ** -0.5)
    c_re = nrm((na, N_GROUPS, GROUP_CH, STATE_DIM), (2 * STATE_DIM) ** -0.5)
    c_im = nrm((na, N_GROUPS, GROUP_CH, STATE_DIM), (2 * STATE_DIM) ** -0.5)
    d_skip = nrm((na, SSM_WIDTH), 0.5)
    w_glu = nrm((na, SSM_WIDTH, 2 * SSM_WIDTH), SSM_WIDTH ** -0.5)
    b_glu = nrm((na, 2 * SSM_WIDTH), 0.01)
    w_out_a = nrm((na, SSM_WIDTH, D_MODEL), SSM_WIDTH ** -0.5)
    norm_kv = 1.0 + nrm((D_MODEL,), 0.02)
    w_dkv = nrm((D_MODEL, KV_LORA + QK_ROPE), D_MODEL ** -0.5)
    norm_latent = 1.0 + nrm((KV_LORA,), 0.02)
    w_uk = nrm((KV_LORA, N_HEADS * QK_NOPE), KV_LORA ** -0.5)
    w_uv = nrm((KV_LORA, N_HEADS * V_HEAD), KV_LORA ** -0.5)
    norm_b = 1.0 + nrm((nb, D_MODEL), 0.02)
    w_in_b = nrm((nb, D_MODEL, Q_LORA + ATTN_WIDTH), D_MODEL ** -0.5)
    norm_q = 1.0 + nrm((nb, Q_LORA), 0.02)
    w_uq = nrm((nb, Q_LORA, N_HEADS * (QK_NOPE + QK_ROPE)), Q_LORA ** -0.5)
    w_out_b = nrm((nb, ATTN_WIDTH, D_MODEL), ATTN_WIDTH ** -0.5)
    norm_f = 1.0 + nrm((D_MODEL,), 0.02)
    return {'x_prompt': x_prompt, 'x_sample': x_sample, 'cache_latent': cache_latent, 'cache_krope': cache_krope,
            'page_table': page_table, 'state_ssm_re': state_ssm_re, 'state_ssm_im': state_ssm_im,
            'norm_a': norm_a, 'w_in_a': w_in_a, 'a_re': a_re, 'a_im': a_im, 'log_dt': log_dt,
            'b_re': b_re, 'b_im': b_im, 'c_re': c_re, 'c_im': c_im, 'd_skip': d_skip,
            'w_glu': w_glu, 'b_glu': b_glu, 'w_out_a': w_out_a,
            'norm_kv': norm_kv, 'w_dkv': w_dkv, 'norm_latent': norm_latent, 'w_uk': w_uk, 'w_uv': w_uv,
            'norm_b': norm_b, 'w_in_b': w_in_b, 'norm_q': norm_q, 'w_uq': w_uq, 'w_out_b': w_out_b,
            'norm_f': norm_f}


def reference(x_prompt, x_sample, cache_latent, cache_krope, page_table, state_ssm_re, state_ssm_im,
              norm_a, w_in_a, a_re, a_im, log_dt, b_re, b_im, c_re, c_im, d_skip, w_glu, b_glu, w_out_a,
              norm_kv, w_dkv, norm_latent, w_uk, w_uv,
              norm_b, w_in_b, norm_q, w_uq, w_out_b, norm_f):
    n_seq, n_pages = page_table.shape
    past_len = n_pages * PAGE_SIZE
    pos_p = jnp.arange(x_prompt.shape[1], dtype=jnp.int32)
    pos_s = past_len + jnp.arange(x_sample.shape[1], dtype=jnp.int32)
    kpos_s = jnp.arange(past_len + x_sample.shape[1], dtype=jnp.int32)
    zeros_p = jnp.zeros((x_prompt.shape[0], N_GROUPS, STATE_DIM), jnp.float32)
    xp, xs = x_prompt, x_sample
    hp_re, hp_im, hs_re, hs_im = [], [], [], []
    for i in range(DEPTH):
        if i < N_A_LAYERS:
            a_par = (norm_a[i], w_in_a[i], a_re[i], a_im[i], log_dt[i], b_re[i], b_im[i],
                     c_re[i], c_im[i], d_skip[i], w_glu[i], b_glu[i], w_out_a[i])
            xp, r, m = s5_block(xp, zeros_p, zeros_p, *a_par)
            hp_re.append(r)
            hp_im.append(m)
            xs, r, m = s5_block(xs, state_ssm_re[i], state_ssm_im[i], *a_par)
            hs_re.append(r)
            hs_im.append(m)
        else:
            if i == N_A_LAYERS:
                lat_p, kr_p = shared_latent(xp, pos_p, norm_kv, w_dkv, norm_latent)
                lat_s, kr_s = shared_latent(xs, pos_s, norm_kv, w_dkv, norm_latent)
                past_lat = cache_latent[page_table].reshape(n_seq, past_len, KV_LORA)
                past_kr = cache_krope[page_table].reshape(n_seq, past_len, QK_ROPE)
                keys_lat_s = jnp.concatenate([past_lat, lat_s.astype(past_lat.dtype)], axis=1)
                keys_kr_s = jnp.concatenate([past_kr, kr_s.astype(past_kr.dtype)], axis=1)
            j = i - N_A_LAYERS
            b_par = (norm_b[j], w_in_b[j], norm_q[j], w_uq[j], w_out_b[j])
            xp = mla_block(xp, pos_p, lat_p, kr_p, pos_p, w_uk, w_uv, *b_par)
            xs = mla_block(xs, pos_s, keys_lat_s, keys_kr_s, kpos_s, w_uk, w_uv, *b_par)
    y_prompt = rms_norm(xp, norm_f)
    y_sample = rms_norm(xs, norm_f)
    return (y_prompt, y_sample, lat_p, kr_p, lat_s, kr_s,
            jnp.stack(hp_re), jnp.stack(hp_im), jnp.stack(hs_re), jnp.stack(hs_im))
```

```python
import contextlib
import math
import numpy as np
import concourse.bass as bass
import concourse.mybir as mybir
from concourse.bass_utils import run_bass_kernel_spmd

F32 = mybir.dt.float32
BF16 = mybir.dt.bfloat16
I32 = mybir.dt.int32
AF = mybir.ActivationFunctionType
ALU = mybir.AluOpType

NCORES = 8
D = 1024
SEQ = 2048
NSEQ_C = 2
NS_C = 16
DSEQ = 4
NTS = NS_C * DSEQ
W = 2048
NPAIR = 64
KVL = 256
ROPE = 64
QL = 384
NH = 8
NPOOL = 10240
NPAGES = 64
EPS = 1e-6
SCALE = 1.0 / math.sqrt(192.0)
TWO_PI = 2.0 * math.pi
GRAN = 256


class Sched:
    NDS = 24

    def __init__(self, nc):
        self.nc = nc
        self.ops = []
        self.stack = contextlib.ExitStack()
        self.last_w = {}
        self.readers = {}
        self.n_dma = 0

    def sb(self, name, shape, dtype):
        return self.stack.enter_context(self.nc.sbuf_tensor(name, list(shape), dtype))

    def ps(self, name, shape, dtype):
        return self.stack.enter_context(self.nc.psum_tensor(name, list(shape), dtype))

    def _add(self, eng, fn, reads, writes, dma):
        idx = len(self.ops)
        deps = set()
        for k in reads:
            d = self.last_w.get(k)
            if d is not None:
                deps.add(d)
        for k in writes:
            d = self.last_w.get(k)
            if d is not None:
                deps.add(d)
            deps.update(self.readers.get(k, ()))
        deps.discard(idx)
        for k in reads:
            self.readers.setdefault(k, []).append(idx)
        for k in writes:
            self.last_w[k] = idx
            self.readers[k] = []
        o = dict(eng=eng, fn=fn, deps=deps, dma=dma)
        if dma:
            o["dma_i"] = self.n_dma
            self.n_dma += 1
        self.ops.append(o)
        return idx

    def emit(self, final_keys=()):
        nc = self.nc
        ops = self.ops
        engs = ["pe", "act", "dve", "pool", "sp"]
        need_sig = [False] * len(ops)
        for i, o in enumerate(ops):
            for d in o["deps"]:
                od = ops[d]
                if od["dma"]:
                    continue
                if od["eng"] == "pe" and o["eng"] == "pe" and not o["dma"]:
                    continue
                need_sig[d] = True
        final_deps = set()
        for k in final_keys:
            if k in self.last_w:
                final_deps.add(self.last_w[k])
        for d in final_deps:
            if not ops[d]["dma"]:
                need_sig[d] = True
        cnt = {e: 0 for e in engs}
        sig_val = [None] * len(ops)
        nds = self.NDS
        dma_uses = [0] * nds
        for i, o in enumerate(ops):
            if o["dma"]:
                s = o["dma_i"] % nds
                dma_uses[s] += 1
                sig_val[i] = ("d", s, 16 * dma_uses[s])
            elif need_sig[i]:
                cnt[o["eng"]] += 1
                sig_val[i] = ("e", o["eng"], cnt[o["eng"]])
        with contextlib.ExitStack() as st:
            esem = {e: st.enter_context(nc.semaphore("es_" + e)) for e in engs}
            dsem = [st.enter_context(nc.semaphore("ds_%d" % j)) for j in range(nds)]
            block = st.enter_context(nc.Block())
            per_eng = {e: [] for e in engs}
            for i, o in enumerate(ops):
                per_eng[o["eng"]].append(i)

            def semof(sv):
                return dsem[sv[1]] if sv[0] == "d" else esem[sv[1]]

            def body(ename):
                def _f(eng):
                    waited = {}
                    for i in per_eng[ename]:
                        o = ops[i]
                        waits = {}
                        for d in o["deps"]:
                            od = ops[d]
                            if (not od["dma"]) and od["eng"] == "pe" and ename == "pe" and not o["dma"]:
                                continue
                            sv = sig_val[d]
                            key = (sv[0], sv[1])
                            if waits.get(key, 0) < sv[2]:
                                waits[key] = sv[2]
                        if o["dma"]:
                            sv = sig_val[i]
                            if sv[2] > 16:
                                key = (sv[0], sv[1])
                                if waits.get(key, 0) < sv[2] - 16:
                                    waits[key] = sv[2] - 16
                        for key, v in waits.items():
                            if waited.get(key, 0) >= v:
                                continue
                            waited[key] = v
                            eng.wait_ge(dsem[key[1]] if key[0] == "d" else esem[key[1]], v)
                        ins = o["fn"](eng)
                        sv = sig_val[i]
                        if sv is not None:
                            ins.then_inc(semof(sv), 16 if sv[0] == "d" else 1)
                    if ename == "sp":
                        for d in sorted(final_deps):
                            sv = sig_val[d]
                            key = (sv[0], sv[1])
                            if waited.get(key, 0) >= sv[2]:
                                continue
                            waited[key] = sv[2]
                            eng.wait_ge(semof(sv), sv[2])
                return _f

            block.tensor(body("pe"))
            block.scalar(body("act"))
            block.vector(body("dve"))
            block.gpsimd(body("pool"))
            block.sync(body("sp"))
        self.stack.close()


class T:
    __slots__ = ("ap", "k")

    def __init__(self, ap, k):
        self.ap = ap
        self.k = k


class ABuf:
    def __init__(self, arena, off, shape, dtype, tag="A"):
        es = 2 if dtype == BF16 else 4
        n = int(np.prod(shape))
        assert off % 4 == 0
        ap = arena[:, off // 2: off // 2 + n * es // 2]
        if dtype != BF16:
            ap = ap.bitcast(dtype)
        if len(shape) == 2:
            ap = ap.rearrange("p (a b) -> p a b", a=shape[0])
        elif len(shape) == 3:
            ap = ap.rearrange("p (a b c) -> p a b c", a=shape[0], b=shape[1])
        elif len(shape) == 4:
            ap = ap.rearrange("p (a b c d) -> p a b c d", a=shape[0], b=shape[1], c=shape[2])
        self.ap = ap
        self.off = off
        self.shape = tuple(shape)
        self.es = es
        self.tag = tag
        self.nbytes = n * es
        self._offs = None

    def keys(self, index=()):
        if self._offs is None:
            self._offs = (np.arange(int(np.prod(self.shape)), dtype=np.int64).reshape(self.shape) * self.es + self.off)
        sel = self._offs[index] if index else self._offs
        sel = np.asarray(sel).ravel()
        g = np.unique(np.concatenate([sel // GRAN, (sel + self.es - 1) // GRAN]))
        return ["%s%d" % (self.tag, int(x)) for x in g]

    def __getitem__(self, index):
        if not isinstance(index, tuple):
            index = (index,)
        return T(self.ap[(slice(None),) + index], self.keys(index))

    def all(self):
        return T(self.ap, self.keys())

    def rows(self, lo, hi, index=()):
        if not isinstance(index, tuple):
            index = (index,)
        return T(self.ap[(slice(lo, hi),) + index], self.keys(index))


def _flat(xs):
    out = []
    for x in xs:
        if isinstance(x, T):
            out.extend(x.k)
        elif isinstance(x, str):
            out.append(x)
        else:
            out.extend(_flat(x))
    return out


def build_program(stages=("s5", "lat", "mla", "fin"), units=(0, 1, 2), npool=NPOOL):
    nc = bass.Bass("TRN2", target_bir_lowering=False)

    def din(name, shape, dt=F32):
        return nc.dram_tensor(name, list(shape), dt, kind="ExternalInput").ap()

    def dout(name, shape, dt=F32):
        return nc.dram_tensor(name, list(shape), dt, kind="ExternalOutput").ap()

    def dscr(name, shape, dt=F32):
        return nc.dram_tensor(name, list(shape), dt, kind="Internal").ap()

    I = {}
    I["xp"] = din("xp", [NSEQ_C, SEQ, D])
    I["xs"] = din("xs", [NTS, D])
    I["cl"] = din("cl", [npool * 128, KVL])
    I["ck"] = din("ck", [npool * 128, ROPE])
    I["pt"] = din("pt", [NS_C, NPAGES], I32)
    I["sre"] = din("sre", [2, NS_C, 128, 64])
    I["sim"] = din("sim", [2, NS_C, 128, 64])
    I["norm_a"] = din("norm_a", [2, D])
    I["w_in_a"] = din("w_in_a", [2, D, 2 * W])
    I["a_re"] = din("a_re", [2, 128, 64])
    I["a_im"] = din("a_im", [2, 128, 64])
    I["log_dt"] = din("log_dt", [2, 128])
    I["b_re"] = din("b_re", [2, 128, 64, 16])
    I["b_im"] = din("b_im", [2, 128, 64, 16])
    I["c_re"] = din("c_re", [2, 128, 16, 64])
    I["c_im"] = din("c_im", [2, 128, 16, 64])
    I["d_skip"] = din("d_skip", [2, W])
    I["w_glu"] = din("w_glu", [2, W, 2 * W])
    I["b_glu"] = din("b_glu", [2, 2 * W])
    I["w_out_a"] = din("w_out_a", [2, W, D])
    I["norm_kv"] = din("norm_kv", [D])
    I["w_dkv"] = din("w_dkv", [D, KVL + ROPE])
    I["norm_latent"] = din("norm_latent", [KVL])
    I["w_uk"] = din("w_uk", [KVL, D])
    I["w_uv"] = din("w_uv", [KVL, D])
    I["norm_b"] = din("norm_b", [2, D])
    I["w_in_b"] = din("w_in_b", [2, D, QL + D])
    I["norm_q"] = din("norm_q", [2, QL])
    I["w_uq"] = din("w_uq", [2, QL, NH * 192])
    I["w_out_b"] = din("w_out_b", [2, D, D])
    I["norm_f"] = din("norm_f", [D])
    I["ident"] = din("ident", [128, 128])
    I["cst"] = din("cst", [128, 64])
    I["tri"] = din("tri", [128, 128])
    I["pos"] = din("pos", [128, 17])
    I["invf"] = din("invf", [32])
    I["mnew"] = din("mnew", [64, 64])

    O = {}
    O["y_p"] = dout("y_p", [NSEQ_C, SEQ, D])
    O["y_s"] = dout("y_s", [NTS, D])
    O["lat_p"] = dout("lat_p", [NSEQ_C, SEQ, KVL])
    O["kr_p"] = dout("kr_p", [NSEQ_C, SEQ, ROPE])
    O["lat_s"] = dout("lat_s", [NTS, KVL])
    O["kr_s"] = dout("kr_s", [NTS, ROPE])
    O["hre_p"] = dout("hre_p", [2, NSEQ_C, 128, 64])
    O["him_p"] = dout("him_p", [2, NSEQ_C, 128, 64])
    O["hre_s"] = dout("hre_s", [2, NS_C, 128, 64])
    O["him_s"] = dout("him_s", [2, NS_C, 128, 64])

    XS = [[dscr("xs_%d_%d" % (u, i), [SEQ if u < 2 else NTS, D]) for i in range(2)] for u in range(3)]
    BT_S = dscr("bt_s", [2, NPAIR, 2, 128, 128], BF16)
    CT_S = dscr("ct_s", [2, NPAIR, 2, 128, 128], BF16)
    HL_S = dscr("hl_s", [2, 128, NPAIR, 2, 48])
    EB_S = dscr("eb_s", [2, NPAIR, 128, 2, 512], BF16)
    EL_S = dscr("el_s", [2, 128, NPAIR, 4])
    PG_L = dscr("pg_l", [NS_C, NPAGES, 128, KVL], BF16)
    PG_K = dscr("pg_k", [NS_C, NPAGES, 128, ROPE], BF16)

    S = Sched(nc)

    def op(eng, fn, r=(), w=()):
        S._add(eng, fn, _flat(r), _flat(w), False)

    def dma(q, out, in_, r=(), w=(), **kw):
        S._add(q, lambda e: e.dma_start(out=out, in_=in_, **kw), _flat(r), _flat(w), True)

    ARENA_BYTES = 188 * 1024
    arena = S.sb("arena", [128, ARENA_BYTES // 2], BF16)
    pbs = [S.ps("pb%d" % i, [128, 512], F32) for i in range(8)]

    def PB(i):
        return T(pbs[i][:, :], ["pb%d" % i])

    def PBb(i):
        return T(pbs[i][:, :].bitcast(BF16), ["pb%d" % i])

    idf = S.sb("idf_sb", [128, 128], F32)
    idb = S.sb("idb", [128, 128], BF16)
    onesb = S.sb("onesb", [128, 128], BF16)
    cst = S.sb("cst_sb", [128, 64], F32)
    trib = S.sb("trib", [128, 128], BF16)
    small = S.sb("small", [128, 2048], F32)
    sm_off = [0]

    def SM(n, name):
        o = sm_off[0]
        sm_off[0] += n
        assert sm_off[0] <= 2048
        return T(small[:, o:o + n], [name])

    dma("sp", idf[:], I["ident"], w=["idf"])
    dma("pool", idb[:], I["ident"], w=["idb"])
    dma("sp", cst[:], I["cst"], w=["cst"])
    dma("pool", trib[:], I["tri"], w=["trib"])
    op("dve", lambda e: e.memset(onesb[:], 1.0), w=["onesb"])
    IDF = T(idf[:], ["idf"])
    IDB = T(idb[:], ["idb"])

    gbuf = S.sb("gbuf", [128, D], F32)
    GB = T(gbuf[:], ["gbuf"])

    A = {}
    off = 0

    def AL(name, shape, dtype, at=None):
        nonlocal off
        o = off if at is None else at
        b = ABuf(arena, o, shape, dtype)
        if at is None:
            off = o + b.nbytes
            off = (off + 255) // 256 * 256
        assert o + b.nbytes <= ARENA_BYTES, (name, o, b.nbytes)
        A[name] = b
        return b

    XT = [AL("xt%d" % i, [D], F32) for i in range(3)]
    HB = [AL("hb%d" % i, [D], BF16) for i in range(2)]
    WREG = off
    off += 24 * 1024
    BIG = off
    BIGSZ = ARENA_BYTES - BIG

    stat = [SM(4, "stat%d" % i) for i in range(3)]
    rr_i = [0]

    def range_reduce(dst, src, shift, tmp_i, tmp_f, tmp_m, eng="dve"):
        if shift != 0.0:
            op(eng, lambda e: e.tensor_scalar(dst.ap, src.ap, float(shift), None, ALU.add), r=[src], w=[dst])
            s2 = dst
        else:
            s2 = src
        op(eng, lambda e: e.tensor_scalar(tmp_i.ap, s2.ap, 1.0 / TWO_PI, None, ALU.mult), r=[s2], w=[tmp_i])
        op(eng, lambda e: e.tensor_copy(tmp_f.ap, tmp_i.ap), r=[tmp_i], w=[tmp_f])
        op(eng, lambda e: e.scalar_tensor_tensor(dst.ap, tmp_f.ap, -TWO_PI, s2.ap, ALU.mult, ALU.add), r=[tmp_f, s2], w=[dst])
        op(eng, lambda e: e.tensor_single_scalar(tmp_m.ap, dst.ap, math.pi, ALU.is_gt), r=[dst], w=[tmp_m])
        op(eng, lambda e: e.scalar_tensor_tensor(dst.ap, tmp_m.ap, -TWO_PI, dst.ap, ALU.mult, ALU.add), r=[tmp_m, dst], w=[dst])
        op(eng, lambda e: e.tensor_single_scalar(tmp_m.ap, dst.ap, -math.pi, ALU.is_lt), r=[dst], w=[tmp_m])
        op(eng, lambda e: e.scalar_tensor_tensor(dst.ap, tmp_m.ap, TWO_PI, dst.ap, ALU.mult, ALU.add), r=[tmp_m, dst], w=[dst])

    def load_gain(vec_ap):
        dma("sp", gbuf[:], vec_ap.partition_broadcast(128), w=[GB])

    def rms_tile(xt, rows, ncols, gain_ap, out_t, gain_T=None):
        st = stat[rr_i[0] % 3]
        rr_i[0] += 1
        xa = xt.ap[0:rows] if rows < 128 else xt.ap
        oa = out_t.ap[0:rows] if rows < 128 else out_t.ap
        sa = st.ap[0:rows] if rows < 128 else st.ap
        ga = gain_ap[0:rows] if rows < 128 else gain_ap
        op("act", lambda e: e.activation(out=oa, in_=xa, func=AF.Square, accum_out=sa[:, 0:1]), r=[xt], w=[out_t, st])
        op("dve", lambda e: e.tensor_scalar(sa[:, 1:2], sa[:, 0:1], 1.0 / ncols, EPS, ALU.mult, ALU.add), r=[st], w=[st])
        op("act", lambda e: e.activation(out=sa[:, 1:2], in_=sa[:, 1:2], func=AF.Sqrt), r=[st], w=[st])
        op("dve", lambda e: e.reciprocal(sa[:, 2:3], sa[:, 1:2]), r=[st], w=[st])
        op("dve", lambda e: e.scalar_tensor_tensor(oa, xa, sa[:, 2:3], ga, ALU.mult, ALU.mult), r=[xt, st, GB if gain_T is None else gain_T], w=[out_t])

    pb_rot = [0]
    cvst = SM(128, "cvst")

    def load_colvec(dst, vec_ap, n):
        dma("sp", cvst.ap[0:n, :], vec_ap.rearrange("(t p) -> t p", p=128), w=[cvst])
        pv = PB(7)
        op("pe", lambda e: e.transpose(pv.ap[:, 0:n], cvst.ap[0:n, :], idf[0:n, 0:n]), r=[cvst, IDF], w=[pv])
        op("dve", lambda e: e.tensor_copy(dst.ap, pv.ap[:, 0:n]), r=[pv], w=[dst])

    def next_pb(cands):
        i = cands[pb_rot[0] % len(cands)]
        pb_rot[0] += 1
        return i

    def x_src(u, li):
        if li == 0:
            return (I["xp"][u] if u < 2 else I["xs"]), "xin%d" % u
        return XS[u][(li - 1) % 2], "xs_%d_%d" % (u, (li - 1) % 2)

    def x_dst(u, li):
        return XS[u][li % 2], "xs_%d_%d" % (u, li % 2)

    def build_hT(u, li, hT, NT, ncols_tile=128, col0=0, ntiles=None, tile0=0, keep_x=None):
        src, skey = x_src(u, li)
        rows = min(128, NT)
        nt = (NT // rows) if ntiles is None else ntiles
        for r in range(nt):
            tr = tile0 + r
            xt = XT[(tr) % 3].all() if keep_x is None else keep_x[r]
            dma("sp", xt.ap[0:rows], src[tr * rows:(tr + 1) * rows, :], r=[skey + ":%d:0" % tr, skey + ":%d:1" % tr], w=[xt])
            hb = HB[tr % 2].all()
            rms_tile(xt, rows, D, gbuf[:], hb)
            pbi = next_pb([0, 1])
            pv = PBb(pbi)
            for kt in range(8):
                op("pe", lambda e, kt=kt, pv=pv, hb=hb: e.transpose(pv.ap[:, kt * 128:kt * 128 + rows], hb.ap[0:rows, kt * 128:(kt + 1) * 128], idb[0:rows, 0:rows]),
                   r=[hb, IDB], w=[pv])
            dst = hT[:, col0 + r * rows: col0 + (r + 1) * rows]
            op("act", lambda e, pv=pv, dst=dst: e.copy(out=dst.ap, in_=pv.ap.rearrange("p (k c) -> p k c", k=8)[:, :, 0:rows]), r=[pv], w=[dst])

    fin_keys = []


    pfl = [S.sb("pfl%d" % i, [128, KVL], BF16) for i in range(2)]
    pfk = [S.sb("pfk%d" % i, [128, ROPE], BF16) for i in range(2)]
    pidx = S.sb("pidx", [128, 4 * NPAGES], I32)
    pptb = S.sb("pptb", [128, 4 * NPAGES], I32)
    PF = {"n": 0, "pending": None, "on": False}

    def prefetch_step():
        if not PF["on"]:
            return
        n = PF["n"]
        if n < NS_C * NPAGES:
            s_, pg = divmod(n, NPAGES)
            if n % (4 * NPAGES) == 0:
                blk = n // (4 * NPAGES)
                dma("sp", pptb[:], I["pt"][blk * 4:(blk + 1) * 4].rearrange("s g -> (s g)").partition_broadcast(128), w=["pptb"])
                op("dve", lambda e: e.tensor_scalar(pidx[:], pptb[:], 128.0, cst[:, 4:5], ALU.mult, ALU.add), r=["pptb", "cst"], w=["pidx"])
            col = n % (4 * NPAGES)
            b = n % 2
            S._add("pool", (lambda e, b=b, col=col: e.indirect_dma_start(
                out=pfl[b][:], out_offset=None, in_=I["cl"], in_offset=bass.IndirectOffsetOnAxis(ap=pidx[:, col:col + 1], axis=0))),
                ["pidx"], ["pfl%d" % b], True)
            S._add("pool", (lambda e, b=b, col=col: e.indirect_dma_start(
                out=pfk[b][:], out_offset=None, in_=I["ck"], in_offset=bass.IndirectOffsetOnAxis(ap=pidx[:, col:col + 1], axis=0))),
                ["pidx"], ["pfk%d" % b], True)
        if PF["pending"] is not None:
            s_, pg, b = PF["pending"]
            dma("pool", PG_L[s_, pg], pfl[b][:], r=["pfl%d" % b], w=["pg_l:%d" % s_])
            dma("pool", PG_K[s_, pg], pfk[b][:], r=["pfk%d" % b], w=["pg_k:%d" % s_])
            PF["pending"] = None
        if n < NS_C * NPAGES:
            PF["pending"] = (n // NPAGES, n % NPAGES, n % 2)
            PF["n"] = n + 1

    def prefetch_flush():
        while PF["on"] and (PF["n"] < NS_C * NPAGES or PF["pending"] is not None):
            prefetch_step()

    def s5_tables(l):
        nonlocal off
        base = BIG
        cur = [base]

        def TL(shape, dtype):
            b = ABuf(arena, cur[0], shape, dtype)
            cur[0] += (b.nbytes + 255) // 256 * 256
            assert cur[0] <= ARENA_BYTES
            return b

        nat = TL([3, 128], F32)
        ld2 = TL([2], F32)
        at = TL([3, 64], F32)
        dt_ = TL([64], F32)
        er = TL([64], F32)
        th = TL([64], F32)
        rr = TL([64], F32)
        t1 = TL([64], F32)
        t2 = TL([64], F32)
        t3 = TL([64], F32)
        ti = TL([64], I32)
        s1 = TL([64], F32)
        c1 = TL([64], F32)
        fr = TL([64], F32)
        fi = TL([64], F32)
        dma("sp", nat.ap[0:64, 0, :], I["a_re"][l].rearrange("(gp g2) p -> gp (g2 p)", g2=2), w=[nat[0]])
        dma("sp", nat.ap[0:64, 1, :], I["a_im"][l].rearrange("(gp g2) p -> gp (g2 p)", g2=2), w=[nat[1]])
        dma("sp", ld2.ap[0:64, :], I["log_dt"][l].rearrange("(gp g2) -> gp g2", g2=2), w=[ld2.all()])
        op("dve", lambda e: e.tensor_copy(nat.ap[0:64, 2, :].rearrange("q (g p) -> q g p", g=2),
                                          ld2.ap[0:64, :].unsqueeze(2).to_broadcast([64, 2, 64])), r=[ld2.all()], w=[nat[2]])
        pv = PB(7)
        for j in range(3):
            op("pe", lambda e, j=j, pv=pv: e.transpose(pv.ap[:, j * 64:(j + 1) * 64], nat.ap[0:64, j, :], idf[0:64, 0:64]), r=[nat[j], IDF], w=[pv])
        op("dve", lambda e, pv=pv: e.tensor_copy(at.ap, pv.ap[:, 0:192].rearrange("p (a b) -> p a b", a=3)), r=[pv], w=[at.all()])
        op("act", lambda e: e.activation(out=dt_.ap, in_=at.ap[:, 2, :], func=AF.Exp), r=[at[2]], w=[dt_.all()])
        op("dve", lambda e: e.tensor_tensor(er.ap, at.ap[:, 0, :], dt_.ap, ALU.mult), r=[at[0], dt_.all()], w=[er.all()])
        op("dve", lambda e: e.tensor_tensor(th.ap, at.ap[:, 1, :], dt_.ap, ALU.mult), r=[at[1], dt_.all()], w=[th.all()])
        op("act", lambda e: e.activation(out=rr.ap, in_=er.ap, func=AF.Exp), r=[er.all()], w=[rr.all()])
        range_reduce(t1.all(), th.all(), 0.0, ti.all(), t2.all(), t3.all())
        op("act", lambda e: e.activation(out=s1.ap, in_=t1.ap, func=AF.Sin), r=[t1.all()], w=[s1.all()])
        range_reduce(t1.all(), th.all(), math.pi / 2, ti.all(), t2.all(), t3.all())
        op("act", lambda e: e.activation(out=c1.ap, in_=t1.ap, func=AF.Sin), r=[t1.all()], w=[c1.all()])
        lbr, lbi = c1, s1
        op("dve", lambda e: e.tensor_tensor(lbr.ap, c1.ap, rr.ap, ALU.mult), r=[c1.all(), rr.all()], w=[lbr.all()])
        op("dve", lambda e: e.tensor_tensor(lbi.ap, s1.ap, rr.ap, ALU.mult), r=[s1.all(), rr.all()], w=[lbi.all()])
        op("dve", lambda e: e.tensor_scalar(lbr.ap, lbr.ap, -1.0, None, ALU.add), r=[lbr.all()], w=[lbr.all()])
        op("dve", lambda e: e.tensor_tensor(t1.ap, at.ap[:, 0, :], at.ap[:, 0, :], ALU.mult), r=[at[0]], w=[t1.all()])
        op("dve", lambda e: e.tensor_tensor(t2.ap, at.ap[:, 1, :], at.ap[:, 1, :], ALU.mult), r=[at[1]], w=[t2.all()])
        op("dve", lambda e: e.tensor_tensor(t1.ap, t1.ap, t2.ap, ALU.add), r=[t1.all(), t2.all()], w=[t1.all()])
        op("dve", lambda e: e.reciprocal(t3.ap, t1.ap), r=[t1.all()], w=[t3.all()])
        op("dve", lambda e: e.tensor_tensor(t1.ap, lbr.ap, at.ap[:, 0, :], ALU.mult), r=[lbr.all(), at[0]], w=[t1.all()])
        op("dve", lambda e: e.tensor_tensor(t2.ap, lbi.ap, at.ap[:, 1, :], ALU.mult), r=[lbi.all(), at[1]], w=[t2.all()])
        op("dve", lambda e: e.tensor_tensor(t1.ap, t1.ap, t2.ap, ALU.add), r=[t1.all(), t2.all()], w=[t1.all()])
        op("dve", lambda e: e.tensor_tensor(fr.ap, t1.ap, t3.ap, ALU.mult), r=[t1.all(), t3.all()], w=[fr.all()])
        op("dve", lambda e: e.tensor_tensor(t1.ap, lbi.ap, at.ap[:, 0, :], ALU.mult), r=[lbi.all(), at[0]], w=[t1.all()])
        op("dve", lambda e: e.tensor_tensor(t2.ap, lbr.ap, at.ap[:, 1, :], ALU.mult), r=[lbr.all(), at[1]], w=[t2.all()])
        op("dve", lambda e: e.tensor_tensor(t1.ap, t1.ap, t2.ap, ALU.subtract), r=[t1.all(), t2.all()], w=[t1.all()])
        op("dve", lambda e: e.tensor_tensor(fi.ap, t1.ap, t3.ap, ALU.mult), r=[t1.all(), t3.all()], w=[fi.all()])
        RRl = RR[l]
        op("dve", lambda e: e.tensor_copy(RRl.ap, rr.ap), r=[rr.all()], w=[RRl])
        NHL = NPAIR * 48
        arg = TL([NPAIR, 48], F32)
        red = TL([NPAIR, 48], F32)
        tf = TL([NPAIR, 48], F32)
        tm = TL([NPAIR, 48], F32)
        tii = TL([NPAIR, 48], I32)
        hl = TL([NPAIR, 2, 48], F32)
        CST = T(cst[:], ["cst"])
        op("dve", lambda e: e.tensor_tensor(arg.ap, th.ap.unsqueeze(2).to_broadcast([128, NPAIR, 48]),
                                            cst[:, 8:56].unsqueeze(1).to_broadcast([128, NPAIR, 48]), ALU.mult), r=[th.all(), CST], w=[arg.all()])
        range_reduce(red.all(), arg.all(), 0.0, tii.all(), tf.all(), tm.all())
        op("act", lambda e: e.activation(out=hl.ap[:, :, 0, :], in_=red.ap, func=AF.Sin), r=[red.all()], w=[hl.all()])
        range_reduce(red.all(), arg.all(), math.pi / 2, tii.all(), tf.all(), tm.all())
        op("act", lambda e: e.activation(out=hl.ap[:, :, 1, :], in_=red.ap, func=AF.Sin), r=[red.all()], w=[hl.all()])
        dma("sp", HL_S[l], hl.ap, r=[hl.all()], w=["hl_s%d" % l])
        cur[0] = arg.off
        bn = TL([2, NPAIR, 16], F32)
        bp = TL([2, NPAIR, 16], F32)
        tb = TL([NPAIR, 16], F32)
        n4 = TL([2, NPAIR, 128], BF16)
        for ri, nm in enumerate(["b_re", "b_im"]):
            v = I[nm][l].rearrange("(gp g2) p c -> g2 p gp c", g2=2)
            for g2 in range(2):
                dma("sp", bn.ap[64 * g2:64 * (g2 + 1), ri], v[g2], w=[bn[ri]])
        frb = fr.ap.unsqueeze(2).to_broadcast([128, NPAIR, 16])
        fib = fi.ap.unsqueeze(2).to_broadcast([128, NPAIR, 16])
        op("dve", lambda e: e.tensor_tensor(bp.ap[:, 0], bn.ap[:, 0], frb, ALU.mult), r=[bn[0], fr.all()], w=[bp[0]])
        op("dve", lambda e: e.tensor_tensor(tb.ap, bn.ap[:, 1], fib, ALU.mult), r=[bn[1], fi.all()], w=[tb.all()])
        op("dve", lambda e: e.tensor_tensor(bp.ap[:, 0], bp.ap[:, 0], tb.ap, ALU.subtract), r=[bp[0], tb.all()], w=[bp[0]])
        op("dve", lambda e: e.tensor_tensor(bp.ap[:, 1], bn.ap[:, 0], fib, ALU.mult), r=[bn[0], fi.all()], w=[bp[1]])
        op("dve", lambda e: e.tensor_tensor(tb.ap, bn.ap[:, 1], frb, ALU.mult), r=[bn[1], fr.all()], w=[tb.all()])
        op("dve", lambda e: e.tensor_tensor(bp.ap[:, 1], bp.ap[:, 1], tb.ap, ALU.add), r=[bp[1], tb.all()], w=[bp[1]])
        op("pool", lambda e: e.memset(n4.ap, 0.0), w=[n4.all()])
        for ri in range(2):
            n4v = n4.ap[:, ri].rearrange("p (t k) c -> p t k c", k=4)
            bpv = bp.ap[:, ri].rearrange("p (t k) c -> p t k c", k=4)
            for k in range(4):
                for g2 in range(2):
                    c0 = 32 * k + 16 * g2
                    op("dve", lambda e, n4v=n4v, bpv=bpv, k=k, g2=g2, c0=c0: e.tensor_copy(
                        n4v[64 * g2:64 * (g2 + 1), :, k, c0:c0 + 16], bpv[64 * g2:64 * (g2 + 1), :, k, :]),
                        r=[bp[ri]], w=[n4[ri]])
        stg = [TL([8, 128], BF16) for _ in range(2)]
        for t8 in range(16):
            pv = PBb(next_pb([2, 3]))
            for k in range(4):
                for ri in range(2):
                    j = k * 2 + ri
                    op("pe", lambda e, pv=pv, j=j, ri=ri, gp=t8 * 4 + k: e.transpose(pv.ap[:, j * 128:(j + 1) * 128], n4.ap[:, ri, gp, :], idb[:]),
                       r=[n4[ri], IDB], w=[pv])
            sg = stg[t8 % 2]
            op("act", lambda e, pv=pv, sg=sg: e.copy(out=sg.ap, in_=pv.ap.rearrange("p (a b) -> p a b", a=8)), r=[pv], w=[sg.all()])
            dma("sp", BT_S[l, t8 * 4:(t8 + 1) * 4].rearrange("k r p c -> p (k r) c"), sg.ap, r=[sg.all()], w=["bt_s%d" % l])
        cur[0] = bn.off
        cn = TL([2, 16, 64], F32)
        mt = TL([2, 16, 128], BF16)
        ctp = [TL([2, 768], BF16) for _ in range(2)]
        for ri, nm in enumerate(["c_re", "c_im"]):
            dma("sp", cn.ap[:, ri], I[nm][l].rearrange("(t g) c p -> (g c) t p", g=8), w=[cn[ri]])
        for ri in range(2):
            op("dve", lambda e, ri=ri: e.tensor_scalar(mt.ap[:, ri, :, 0:64], cn.ap[:, ri], cst[:, 2 * ri:2 * ri + 1], None, ALU.mult), r=[cn[ri], CST], w=[mt[ri]])
            op("dve", lambda e, ri=ri: e.tensor_scalar(mt.ap[:, ri, :, 64:128], cn.ap[:, ri], cst[:, 2 * ri + 1:2 * ri + 2], None, ALU.mult), r=[cn[ri], CST], w=[mt[ri]])
        for c in ctp:
            op("pool", lambda e, c=c: e.memset(c.ap, 0.0), w=[c.all()])
        for t8 in range(16):
            pv = PBb(next_pb([2, 3]))
            for ri in range(2):
                op("pe", lambda e, pv=pv, ri=ri, t8=t8: e.transpose(pv.ap[:, ri * 128:(ri + 1) * 128], mt.ap[:, ri, t8, :], idb[:]), r=[mt[ri], IDB], w=[pv])
            c = ctp[t8 % 2]
            for ri in range(2):
                op("act", lambda e, pv=pv, ri=ri, c=c: e.copy(out=c.ap[:, ri, :].rearrange("p (k s) -> p k s", s=192)[:, :, 0:32],
                                                             in_=pv.ap[:, ri * 128:(ri + 1) * 128].rearrange("p (k j) -> p k j", k=4)), r=[pv], w=[c.all()])
            for ri in range(2):
                dma("sp", CT_S[l, t8 * 4:(t8 + 1) * 4, ri].rearrange("k p c -> p k c"),
                    c.ap[:, ri, 0:640].rearrange("p (k s) -> p k s", s=160)[:, :, 0:128], r=[c.all()], w=["ct_s%d" % l])

        cur[0] = hl.off + (hl.nbytes + 255) // 256 * 256
        ef = [[TL([512], F32) for _ in range(2)] for _ in range(2)]
        tab2 = [TL([512], F32) for _ in range(2)]
        ebs = [TL([2, 512], BF16) for _ in range(2)]
        elt = TL([NPAIR, 4], F32)
        for gp in range(NPAIR):
            er_, ei_ = ef[gp % 2][0].all(), ef[gp % 2][1].all()
            ta_, tb_ = tab2[0].all(), tab2[1].all()
            v3 = lambda t: t.ap.rearrange("p (a b) -> p a b", a=16)
            hi = lambda sc, gp=gp: hl.ap[:, gp, sc, 0:16].unsqueeze(2).to_broadcast([128, 16, 32])
            lo = lambda sc, gp=gp: hl.ap[:, gp, sc, 16:48].unsqueeze(1).to_broadcast([128, 16, 32])
            HLT = hl.all()
            op("dve", lambda e, hi=hi, lo=lo, ta_=ta_: e.tensor_tensor(v3(ta_), hi(1), lo(1), ALU.mult), r=[HLT], w=[ta_])
            op("dve", lambda e, hi=hi, lo=lo, tb_=tb_: e.tensor_tensor(v3(tb_), hi(0), lo(0), ALU.mult), r=[HLT], w=[tb_])
            op("dve", lambda e, er_=er_, ta_=ta_, tb_=tb_: e.tensor_tensor(er_.ap, ta_.ap, tb_.ap, ALU.subtract), r=[ta_, tb_], w=[er_])
            op("dve", lambda e, hi=hi, lo=lo, ta_=ta_: e.tensor_tensor(v3(ta_), hi(0), lo(1), ALU.mult), r=[HLT, er_], w=[ta_])
            op("dve", lambda e, hi=hi, lo=lo, tb_=tb_: e.tensor_tensor(v3(tb_), hi(1), lo(0), ALU.mult), r=[HLT, er_], w=[tb_])
            op("dve", lambda e, ei_=ei_, ta_=ta_, tb_=tb_: e.tensor_tensor(ei_.ap, ta_.ap, tb_.ap, ALU.add), r=[ta_, tb_], w=[ei_])
            eb_ = ebs[gp % 2]
            op("act", lambda e, eb_=eb_, er_=er_: e.copy(out=eb_.ap[:, 0, :], in_=er_.ap), r=[er_], w=[eb_.all()])
            op("act", lambda e, eb_=eb_, ei_=ei_: e.copy(out=eb_.ap[:, 1, :], in_=ei_.ap), r=[ei_], w=[eb_.all()])
            EL_ = elt.all()
            op("act", lambda e, gp=gp, er_=er_: e.copy(out=elt.ap[:, gp, 0:1], in_=er_.ap[:, 511:512]), r=[er_], w=[EL_])
            op("act", lambda e, gp=gp, ei_=ei_: e.copy(out=elt.ap[:, gp, 1:2], in_=ei_.ap[:, 511:512]), r=[ei_], w=[EL_])
            op("act", lambda e, gp=gp, ei_=ei_: e.mul(elt.ap[:, gp, 2:3], ei_.ap[:, 511:512], -1.0), r=[ei_], w=[EL_])
            dma("sp", EB_S[l, gp], eb_.ap, r=[eb_.all()], w=["eb_s%d" % l])
        dma("sp", EL_S[l], elt.ap, r=[elt.all()], w=["el_s%d" % l])

    RR = [SM(64, "rr%d" % l) for l in range(2)]
    DSK = [SM(16, "dsk%d" % l) for l in range(2)]
    BGL = [SM(32, "bgl%d" % l) for l in range(2)]
    CR = SM(64, "cr")
    CI = SM(64, "ci")


    def s5_step2_prompt(u, l, hT, gy, base):
        NT, CH, NCH, CW = SEQ, 512, SEQ // 512, 512
        cur = [base]

        def TL(shape, dtype):
            b = ABuf(arena, cur[0], shape, dtype)
            cur[0] += (b.nbytes + 255) // 256 * 256
            assert cur[0] <= ARENA_BYTES, cur[0]
            return b
        xreg = [XT[0].off]

        def TX(shape, dtype):
            b = ABuf(arena, xreg[0], shape, dtype)
            xreg[0] += (b.nbytes + 255) // 256 * 256
            assert xreg[0] <= WREG, xreg[0]
            return b
        bub = [[TL([CW], BF16) for _ in range(2)] for _ in range(2)]
        prod = [[TL([CW], BF16) for _ in range(4)] for _ in range(3)]
        wbf = [[TL([CW], BF16) for _ in range(2)] for _ in range(2)]
        wf = [[TL([CW], F32) for _ in range(2)] for _ in range(2)]
        wb = [[TL([CW], BF16) for _ in range(2)] for _ in range(2)]
        xb = [[TL([CW], BF16) for _ in range(2)] for _ in range(2)]
        Ebs = [TL([4, 2, CW], BF16), TX([4, 2, CW], BF16)]
        Els = [TX([4, 4], F32) for _ in range(2)]
        ctm = [TL([2], F32) for _ in range(2)]
        yt = [TL([CH], F32) for _ in range(2)]
        uT = [TL([CH], BF16) for _ in range(3)]
        wreg = [ABuf(arena, WREG + i * 2048, [8, 128], BF16) for i in range(3)]
        btb = [ABuf(arena, WREG + 6144 + i * 2048, [8, 128], BF16) for i in range(2)]
        ctb = [ABuf(arena, WREG + 10240 + i * 2048, [8, 128], BF16) for i in range(2)]
        RRl = RR[l]
        DS = DSK[l]
        CRk = [T(CR.ap, ["cr%d" % kk]) for kk in range(4)]
        CIk = [T(CI.ap, ["ci%d" % kk]) for kk in range(4)]

        def load_tile_weights(t8):
            wt = wreg[t8 % 3]
            dma("pool", wt.ap, I["w_in_a"][l][:, t8 * 128:(t8 + 1) * 128].rearrange("(k p) c -> p k c", p=128), w=[wt.all()])
            bt = btb[t8 % 2]
            dma("sp", bt.ap, BT_S[l, t8 * 4:(t8 + 1) * 4].rearrange("k r p c -> p (k r) c"), r=["bt_s%d" % l], w=[bt.all()])
            ct = ctb[t8 % 2]
            dma("sp", ct.ap, CT_S[l, t8 * 4:(t8 + 1) * 4].rearrange("k r p c -> p (k r) c"), r=["ct_s%d" % l], w=[ct.all()])
            eb = Ebs[t8 % 2]
            dma("sp", eb.ap.rearrange("p k r c -> p k (r c)"), EB_S[l, t8 * 4:(t8 + 1) * 4].rearrange("k p r c -> p k (r c)"), r=["eb_s%d" % l], w=[eb.all()])
            el = Els[t8 % 2]
            dma("sp", el.ap, EL_S[l][:, t8 * 4:(t8 + 1) * 4], r=["el_s%d" % l], w=[el.all()])

        def stageA(it):
            k, uc, st, bt = it["k"], it["uc"], it["st"], it["bt"]
            pr, pi = PB(2 + st), PB(4 + st)
            op("pe", lambda e, pr=pr, k=k, uc=uc, bt=bt: e.matmul(pr.ap, bt.ap[:, 2 * k, :], uc.ap, start=True, stop=True), r=[bt.all(), uc], w=[pr])
            op("pe", lambda e, pi=pi, k=k, uc=uc, bt=bt: e.matmul(pi.ap, bt.ap[:, 2 * k + 1, :], uc.ap, start=True, stop=True), r=[bt.all(), uc], w=[pi])
            br, bi = bub[st][0].all(), bub[st][1].all()
            op("act", lambda e, br=br, pr=pr: e.copy(out=br.ap, in_=pr.ap), r=[pr], w=[br])
            op("act", lambda e, bi=bi, pi=pi: e.copy(out=bi.ap, in_=pi.ap), r=[pi], w=[bi])

        def stageB(it):
            q, k, st, s3, t8, Eb, El = it["q"], it["k"], it["st"], it["s3"], it["t8"], it["Eb"], it["El"]
            gp = t8 * 4 + k
            br, bi = bub[st][0].all(), bub[st][1].all()
            m1, m2, m3, m4 = [x.all() for x in prod[s3]]
            wri, wii = wbf[st][0].all(), wbf[st][1].all()
            wr, wi = wf[st][0].all(), wf[st][1].all()
            wbr, wbi = wb[st][0].all(), wb[st][1].all()
            Er, Ei = Eb[k, 0], Eb[k, 1]
            op("dve", lambda e, a=m1, Er=Er, br=br: e.tensor_tensor(a.ap, Er.ap, br.ap, ALU.mult), r=[Er, br], w=[m1])
            op("dve", lambda e, a=m2, Ei=Ei, bi=bi: e.tensor_tensor(a.ap, Ei.ap, bi.ap, ALU.mult), r=[Ei, bi], w=[m2])
            op("dve", lambda e, a=m3, Er=Er, bi=bi: e.tensor_tensor(a.ap, Er.ap, bi.ap, ALU.mult), r=[Er, bi], w=[m3])
            op("dve", lambda e, a=m4, Ei=Ei, br=br: e.tensor_tensor(a.ap, Ei.ap, br.ap, ALU.mult), r=[Ei, br], w=[m4])
            op("dve", lambda e, o=wri, a=m1, b=m2: e.tensor_tensor(o.ap, a.ap, b.ap, ALU.add), r=[m1, m2], w=[wri])
            op("dve", lambda e, o=wii, a=m3, b=m4: e.tensor_tensor(o.ap, a.ap, b.ap, ALU.subtract), r=[m3, m4], w=[wii])
            d0 = RRl.ap[:, gp:gp + 1].to_broadcast([128, CW])
            ini_r = 0.0 if q == 0 else CR.ap[:, gp:gp + 1]
            ini_i = 0.0 if q == 0 else CI.ap[:, gp:gp + 1]
            rdeps = [RRl] + ([] if q == 0 else [CRk[k], CIk[k]])
            op("dve", lambda e, wr=wr, wri=wri, d0=d0, ini=ini_r: e.tensor_tensor_scan(wr.ap, d0, wri.ap, ini, ALU.mult, ALU.add), r=[wri] + rdeps, w=[wr])
            op("dve", lambda e, wi=wi, wii=wii, d0=d0, ini=ini_i: e.tensor_tensor_scan(wi.ap, d0, wii.ap, ini, ALU.mult, ALU.add), r=[wii] + rdeps, w=[wi])
            op("act", lambda e, wbr=wbr, wr=wr: e.copy(out=wbr.ap, in_=wr.ap), r=[wr], w=[wbr])
            op("act", lambda e, wbi=wbi, wi=wi: e.copy(out=wbi.ap, in_=wi.ap), r=[wi], w=[wbi])
            elk = El[k]
            c_ = ctm[st].all()
            L = CW - 1
            op("act", lambda e, c_=c_, wi=wi, elk=elk: e.activation(out=c_.ap[:, 0:1], in_=wi.ap[:, L:L + 1], func=AF.Copy, scale=elk.ap[:, 2:3]), r=[wi, elk], w=[c_])
            op("act", lambda e, c_=c_, wr=wr, elk=elk: e.activation(out=c_.ap[:, 1:2], in_=wr.ap[:, L:L + 1], func=AF.Copy, scale=elk.ap[:, 1:2]), r=[wr, elk, c_], w=[c_])
            op("act", lambda e, c_=c_, wr=wr, elk=elk, gp=gp: e.activation(out=CR.ap[:, gp:gp + 1], in_=wr.ap[:, L:L + 1], func=AF.Identity, bias=c_.ap[:, 0:1], scale=elk.ap[:, 0:1]), r=[wr, elk, c_], w=[CRk[k]])
            op("act", lambda e, c_=c_, wi=wi, elk=elk, gp=gp: e.activation(out=CI.ap[:, gp:gp + 1], in_=wi.ap[:, L:L + 1], func=AF.Identity, bias=c_.ap[:, 1:2], scale=elk.ap[:, 0:1]), r=[wi, elk, c_], w=[CIk[k]])

        def stageC(it):
            q, k, uc, st, s3, pacc, t8, Eb, ct = it["q"], it["k"], it["uc"], it["st"], it["s3"], it["pacc"], it["t8"], it["Eb"], it["ct"]
            d1, d2, d3, d4 = [x.all() for x in prod[s3]]
            wbr, wbi = wb[st][0].all(), wb[st][1].all()
            xr, xi = xb[st][0].all(), xb[st][1].all()
            Er, Ei = Eb[k, 0], Eb[k, 1]
            op("dve", lambda e, a=d1, Er=Er, w_=wbr: e.tensor_tensor(a.ap, Er.ap, w_.ap, ALU.mult), r=[Er, wbr], w=[d1])
            op("dve", lambda e, a=d2, Ei=Ei, w_=wbi: e.tensor_tensor(a.ap, Ei.ap, w_.ap, ALU.mult), r=[Ei, wbi], w=[d2])
            op("dve", lambda e, a=d3, Er=Er, w_=wbi: e.tensor_tensor(a.ap, Er.ap, w_.ap, ALU.mult), r=[Er, wbi], w=[d3])
            op("dve", lambda e, a=d4, Ei=Ei, w_=wbr: e.tensor_tensor(a.ap, Ei.ap, w_.ap, ALU.mult), r=[Ei, wbr], w=[d4])
            op("dve", lambda e, xr=xr, a=d1, b=d2: e.tensor_tensor(xr.ap, a.ap, b.ap, ALU.subtract), r=[d1, d2], w=[xr])
            op("dve", lambda e, xi=xi, a=d3, b=d4: e.tensor_tensor(xi.ap, a.ap, b.ap, ALU.add), r=[d3, d4], w=[xi])
            op("pe", lambda e, pacc=pacc, k=k, xr=xr, ct=ct: e.matmul(pacc.ap, ct.ap[:, 2 * k, :], xr.ap, start=(k == 0), stop=False), r=[ct.all(), xr], w=[pacc])
            op("pe", lambda e, pacc=pacc, k=k, xi=xi, ct=ct: e.matmul(pacc.ap, ct.ap[:, 2 * k + 1, :], xi.ap, start=False, stop=(k == 3)), r=[ct.all(), xi], w=[pacc])
            if k == 3:
                y_ = yt[q % 2].all()
                op("dve", lambda e, y_=y_, uc=uc, pacc=pacc, t8=t8: e.scalar_tensor_tensor(y_.ap, uc.ap, DS.ap[:, t8:t8 + 1], pacc.ap, ALU.mult, ALU.add), r=[uc, DS, pacc], w=[y_])
                gdst = gy[t8, slice(q * CH, (q + 1) * CH)]
                op("act", lambda e, gdst=gdst, y_=y_: e.activation(out=gdst.ap, in_=y_.ap, func=AF.Gelu), r=[y_], w=[gdst])

        items = []
        rot = 0
        for t8 in range(16):
            for q in range(NCH):
                for k in range(4):
                    items.append(dict(t8=t8, q=q, k=k, st=rot % 2, s3=rot % 3, bt=btb[t8 % 2], ct=ctb[t8 % 2], Eb=Ebs[t8 % 2], El=Els[t8 % 2]))
                    rot += 1
        ucs, paccs = {}, {}

        def prep_chunk(t8, q):
            wt = wreg[t8 % 3]
            cs = slice(q * CH, (q + 1) * CH)
            pu = PB(next_pb([0, 1]))
            for kt in range(8):
                op("pe", lambda e, pu=pu, kt=kt, wt=wt, cs=cs: e.matmul(pu.ap, wt.ap[:, kt, :], hT.ap[:, kt, cs], start=(kt == 0), stop=(kt == 7)),
                   r=[wt.all(), hT[kt, cs]], w=[pu])
            uc = uT[(t8 * NCH + q) % 3].all()
            op("act", lambda e, pu=pu, uc=uc: e.copy(out=uc.ap, in_=pu.ap), r=[pu], w=[uc])
            ucs[(t8, q)] = uc
            paccs[(t8, q)] = PB(6 + (t8 * NCH + q) % 2)

        load_tile_weights(0)
        n = len(items)
        for i in range(n + 2):
            if i < n:
                it = items[i]
                if it["k"] == 0 and it["q"] == 2 and it["t8"] + 1 < 16:
                    load_tile_weights(it["t8"] + 1)
                if it["k"] == 0:
                    prep_chunk(it["t8"], it["q"])
                it["uc"] = ucs[(it["t8"], it["q"])]
                it["pacc"] = paccs[(it["t8"], it["q"])]
                stageA(it)
            if 1 <= i <= n:
                stageB(items[i - 1])
            if i >= 2:
                stageC(items[i - 2])
        for ri, (Ct, oname) in enumerate(((CR, "hre_p"), (CI, "him_p"))):
            pv = PB(next_pb([2, 3]))
            Cta = T(Ct.ap, ["cr%d" % kk for kk in range(4)] if ri == 0 else ["ci%d" % kk for kk in range(4)])
            op("pe", lambda e, pv=pv, Ct=Ct: e.transpose(pv.ap[0:64, 0:128], Ct.ap, idf[:]), r=[Cta, IDF], w=[pv])
            so = wf[0][ri].all()
            op("dve", lambda e, pv=pv, so=so: e.tensor_copy(so.ap[0:64, 0:128], pv.ap[0:64, 0:128]), r=[pv], w=[so])
            key = "%s:%d:%d" % (oname, l, u)
            dma("sp", O[oname][l, u].rearrange("(gp g2) p -> gp (g2 p)", g2=2), so.ap[0:64, 0:128], r=[so], w=[key])
            fin_keys.append(key)

    def s5_layer(u, l):
        sample = (u == 2)
        NT = NTS if sample else SEQ
        CH = 64 if sample else 512
        NCH = NT // CH
        CW = 80 if sample else 512
        cur = [BIG]

        def TL(shape, dtype):
            b = ABuf(arena, cur[0], shape, dtype)
            cur[0] += (b.nbytes + 255) // 256 * 256
            assert cur[0] <= ARENA_BYTES, cur[0]
            return b

        hT = TL([8, NT], BF16)
        gy = TL([16, NT], BF16)
        w2 = cur[0]
        load_gain(I["norm_a"][l])
        load_colvec(DSK[l], I["d_skip"][l], 16)
        load_colvec(BGL[l], I["b_glu"][l], 32)
        build_hT(u, l, hT, NT)
        if not sample:
            s5_step2_prompt(u, l, hT, gy, cur[0])
        if sample:
            E = TL([4, 2, CW], F32)
            uT = [TL([CH], BF16) for _ in range(3)]
            tmp = [[TL([CW], F32) for _ in range(4)] for _ in range(2)]
            tmp.append([ABuf(arena, XT[0].off + i * 2048, [CW], F32) for i in range(4)])
            wv = [[TL([CW], F32) for _ in range(2)] for _ in range(2)]
            xb = [[TL([CW], BF16) for _ in range(2)] for _ in range(2)]
            yt = [TL([CH], F32) for _ in range(2)]
            hlb = [TL([4, 2, 48], F32) for _ in range(2)]
            if sample:
                rs = TL([NS_C, 5], F32)
                h0 = TL([2, NPAIR, NS_C], F32)
                xsf = TL([2, NPAIR, NS_C], F32)
                snat = [TL([128], F32) for _ in range(2)]
                op("dve", lambda e: e.memset(rs.ap, 0.0), w=[rs.all()])
                op("dve", lambda e: e.memset(E.ap[:, :, 0, :], 1.0), w=[E.all()])
                op("dve", lambda e: e.memset(E.ap[:, :, 1, :], 0.0), w=[E.all()])
                for ri, nm in enumerate(["sre", "sim"]):
                    for s in range(NS_C):
                        sn = snat[(ri * NS_C + s) % 2]
                        dma("sp", sn.ap[0:64, :], I[nm][l, s].rearrange("(gp g2) p -> gp (g2 p)", g2=2), w=[sn.all()])
                        pv = PB(next_pb([2, 3]))
                        op("pe", lambda e, pv=pv, sn=sn: e.transpose(pv.ap[:, 0:64], sn.ap[0:64, :], idf[0:64, 0:64]), r=[sn.all(), IDF], w=[pv])
                        op("dve", lambda e, pv=pv, ri=ri, s=s: e.tensor_copy(h0.ap[:, ri, :, s], pv.ap[:, 0:64]), r=[pv], w=[h0[ri]])
            wreg = [ABuf(arena, WREG + i * 2048, [8, 128], BF16) for i in range(3)]
            btb = [ABuf(arena, WREG + 6144 + i * 2048, [8, 128], BF16) for i in range(2)]
            ctb = [ABuf(arena, WREG + 10240 + i * 2048, [8, 128], BF16) for i in range(2)]
            wk = "w_in_a"

            def load_tile_weights(t8):
                wt = wreg[t8 % 3]
                dma("pool", wt.ap, I["w_in_a"][l][:, t8 * 128:(t8 + 1) * 128].rearrange("(k p) c -> p k c", p=128), w=[wt.all()])
                bt = btb[t8 % 2]
                dma("sp", bt.ap, BT_S[l, t8 * 4:(t8 + 1) * 4].rearrange("k r p c -> p (k r) c"), r=["bt_s%d" % l], w=[bt.all()])
                ct = ctb[t8 % 2]
                dma("sp", ct.ap, CT_S[l, t8 * 4:(t8 + 1) * 4].rearrange("k r p c -> p (k r) c"), r=["ct_s%d" % l], w=[ct.all()])
                hb_ = hlb[t8 % 2]
                dma("sp", hb_.ap, HL_S[l][:, t8 * 4:(t8 + 1) * 4], r=["hl_s%d" % l], w=[hb_.all()])

            load_tile_weights(0)
            rot = [0]
            for t8 in range(16):
                if t8 + 1 < 16:
                    load_tile_weights(t8 + 1)
                wt = wreg[t8 % 3]
                bt = btb[t8 % 2]
                ct = ctb[t8 % 2]
                hl = hlb[t8 % 2]
                ut = uT[t8 % 2]
                for k in range(4):
                    if not sample:
                        ev = lambda ri, k=k: E.ap[:, k, ri, :].rearrange("p (a b) -> p a b", a=16)
                        hi = lambda sc, k=k, hl=hl: hl.ap[:, k, sc, 0:16].unsqueeze(2).to_broadcast([128, 16, 32])
                        lo = lambda sc, k=k, hl=hl: hl.ap[:, k, sc, 16:48].unsqueeze(1).to_broadcast([128, 16, 32])
                        ta = tmp[0][0].ap.rearrange("p (a b) -> p a b", a=16)
                        tb_ = tmp[0][1].ap.rearrange("p (a b) -> p a b", a=16)
                        Tta, Ttb = tmp[0][0].all(), tmp[0][1].all()
                        op("dve", lambda e, hi=hi, lo=lo, ta=ta: e.tensor_tensor(ta, hi(1), lo(1), ALU.mult), r=[hl.all()], w=[Tta])
                        op("dve", lambda e, hi=hi, lo=lo, tb_=tb_: e.tensor_tensor(tb_, hi(0), lo(0), ALU.mult), r=[hl.all()], w=[Ttb])
                        op("dve", lambda e, ev=ev, ta=ta, tb_=tb_: e.tensor_tensor(ev(0), ta, tb_, ALU.subtract), r=[Tta, Ttb], w=[E[k, 0]])
                        op("dve", lambda e, hi=hi, lo=lo, ta=ta: e.tensor_tensor(ta, hi(0), lo(1), ALU.mult), r=[hl.all()], w=[Tta])
                        op("dve", lambda e, hi=hi, lo=lo, tb_=tb_: e.tensor_tensor(tb_, hi(1), lo(0), ALU.mult), r=[hl.all()], w=[Ttb])
                        op("dve", lambda e, ev=ev, ta=ta, tb_=tb_: e.tensor_tensor(ev(1), ta, tb_, ALU.add), r=[Tta, Ttb], w=[E[k, 1]])
                    else:
                        for ri, sc in ((0, 1), (1, 0)):
                            op("dve", lambda e, k=k, ri=ri, sc=sc, hl=hl: e.tensor_copy(
                                E.ap[:, k, ri, :].rearrange("p (s t) -> p s t", t=5)[:, :, 1:5],
                                hl.ap[:, k, sc, 16:20].unsqueeze(1).to_broadcast([128, NS_C, 4])), r=[hl.all()], w=[E[k, ri]])
                RRl = RR[l]
                DS = DSK[l]
                if sample:
                    v5 = lambda t: t.ap.rearrange("p (s t) -> p s t", t=5)[:, :, 1:5]
                    p4 = lambda p: p.ap[:, 0:CH].rearrange("p (s t) -> p s t", t=4)
                    s0 = lambda t: t.ap.rearrange("p (s t) -> p s t", t=5)[:, :, 0]
                    l5 = lambda t: t.ap.rearrange("p (s t) -> p s t", t=5)[:, :, 4]
                else:
                    v5 = lambda t: t.ap
                    p4 = lambda p: p.ap

                def stage1(it):
                    q, k, uc, st = it["q"], it["k"], it["uc"], it["st"]
                    gp = t8 * 4 + k
                    pr = PB(2 + st)
                    pi = PB(4 + st)
                    op("pe", lambda e, pr=pr, k=k, uc=uc, bt=bt: e.matmul(pr.ap[:, 0:CH], bt.ap[:, 2 * k, :], uc.ap, start=True, stop=True), r=[bt.all(), uc], w=[pr])
                    op("pe", lambda e, pi=pi, k=k, uc=uc, bt=bt: e.matmul(pi.ap[:, 0:CH], bt.ap[:, 2 * k + 1, :], uc.ap, start=True, stop=True), r=[bt.all(), uc], w=[pi])
                    t1_, t2_, t3_, t4_ = [x.all() for x in tmp[it["s3"]]]
                    wri, wii = [x.all() for x in wv[st]]
                    Er, Ei = E[k, 0], E[k, 1]
                    op("dve", lambda e, a=t1_, Er=Er, pr=pr: e.tensor_tensor(v5(a), v5(Er), p4(pr), ALU.mult), r=[Er, pr], w=[t1_])
                    op("dve", lambda e, a=t2_, Ei=Ei, pi=pi: e.tensor_tensor(v5(a), v5(Ei), p4(pi), ALU.mult), r=[Ei, pi], w=[t2_])
                    op("dve", lambda e, a=t3_, Er=Er, pi=pi: e.tensor_tensor(v5(a), v5(Er), p4(pi), ALU.mult), r=[Er, pi], w=[t3_])
                    op("dve", lambda e, a=t4_, Ei=Ei, pr=pr: e.tensor_tensor(v5(a), v5(Ei), p4(pr), ALU.mult), r=[Ei, pr], w=[t4_])
                    op("dve", lambda e, o=wri, a=t1_, b=t2_: e.tensor_tensor(v5(o), v5(a), v5(b), ALU.add), r=[t1_, t2_], w=[wri])
                    op("dve", lambda e, o=wii, a=t3_, b=t4_: e.tensor_tensor(v5(o), v5(a), v5(b), ALU.subtract), r=[t3_, t4_], w=[wii])
                    if sample:
                        op("dve", lambda e, o=wri, gp=gp: e.tensor_copy(s0(o), h0.ap[:, 0, gp, :]), r=[h0[0]], w=[wri])
                        op("dve", lambda e, o=wii, gp=gp: e.tensor_copy(s0(o), h0.ap[:, 1, gp, :]), r=[h0[1]], w=[wii])

                def stage2(it):
                    q, k, uc, st, pacc = it["q"], it["k"], it["uc"], it["st"], it["pacc"]
                    gp = t8 * 4 + k
                    t1_, t2_, t3_, t4_ = [x.all() for x in tmp[it["s3"]]]
                    wri, wii = [x.all() for x in wv[st]]
                    wr, wi = wri, wii
                    xr, xi = [x.all() for x in xb[st]]
                    Er, Ei = E[k, 0], E[k, 1]
                    if sample:
                        op("dve", lambda e, gp=gp: e.tensor_copy(rs.ap[:, :, 1:5], RRl.ap[:, gp:gp + 1].unsqueeze(2).to_broadcast([128, NS_C, 4])), r=[RRl], w=[rs.all()])
                        d0 = rs.ap.rearrange("p s t -> p (s t)")
                        ini_r = ini_i = 0.0
                        rdeps = [rs.all()]
                    else:
                        d0 = RRl.ap[:, gp:gp + 1].to_broadcast([128, CW])
                        ini_r = 0.0 if q == 0 else CR.ap[:, gp:gp + 1]
                        ini_i = 0.0 if q == 0 else CI.ap[:, gp:gp + 1]
                        rdeps = [RRl] + ([] if q == 0 else [CRk[k], CIk[k]])
                    op("dve", lambda e, wr=wr, wri=wri, d0=d0, ini=ini_r: e.tensor_tensor_scan(wr.ap, d0, wri.ap, ini, ALU.mult, ALU.add), r=[wri] + rdeps, w=[wr])
                    op("dve", lambda e, wi=wi, wii=wii, d0=d0, ini=ini_i: e.tensor_tensor_scan(wi.ap, d0, wii.ap, ini, ALU.mult, ALU.add), r=[wii] + rdeps, w=[wi])
                    op("dve", lambda e, a=t1_, Er=Er, wr=wr: e.tensor_tensor(a.ap, Er.ap, wr.ap, ALU.mult), r=[Er, wr], w=[t1_])
                    op("dve", lambda e, a=t2_, Ei=Ei, wi=wi: e.tensor_tensor(a.ap, Ei.ap, wi.ap, ALU.mult), r=[Ei, wi], w=[t2_])
                    op("dve", lambda e, a=t3_, Er=Er, wi=wi: e.tensor_tensor(a.ap, Er.ap, wi.ap, ALU.mult), r=[Er, wi], w=[t3_])
                    op("dve", lambda e, a=t4_, Ei=Ei, wr=wr: e.tensor_tensor(a.ap, Ei.ap, wr.ap, ALU.mult), r=[Ei, wr], w=[t4_])
                    op("dve", lambda e, xr=xr, a=t1_, b=t2_: e.tensor_tensor(xr.ap, a.ap, b.ap, ALU.subtract), r=[t1_, t2_], w=[xr])
                    op("dve", lambda e, xi=xi, a=t3_, b=t4_: e.tensor_tensor(xi.ap, a.ap, b.ap, ALU.add), r=[t3_, t4_], w=[xi])
                    if sample:
                        op("dve", lambda e, gp=gp, a=t1_, b=t2_: e.tensor_tensor(xsf.ap[:, 0, gp, :], l5(a), l5(b), ALU.subtract), r=[t1_, t2_], w=[xsf[0]])
                        op("dve", lambda e, gp=gp, a=t3_, b=t4_: e.tensor_tensor(xsf.ap[:, 1, gp, :], l5(a), l5(b), ALU.add), r=[t3_, t4_], w=[xsf[1]])
                    else:
                        op("dve", lambda e, gp=gp, a=t1_, b=t2_: e.tensor_tensor(CR.ap[:, gp:gp + 1], a.ap[:, CW - 1:CW], b.ap[:, CW - 1:CW], ALU.subtract), r=[t1_, t2_], w=[CRk[k]])
                        op("dve", lambda e, gp=gp, a=t3_, b=t4_: e.tensor_tensor(CI.ap[:, gp:gp + 1], a.ap[:, CW - 1:CW], b.ap[:, CW - 1:CW], ALU.add), r=[t3_, t4_], w=[CIk[k]])
                    op("pe", lambda e, pacc=pacc, k=k, xr=xr, ct=ct: e.matmul(pacc.ap[:, 0:CW], ct.ap[:, 2 * k, :], xr.ap, start=(k == 0), stop=False), r=[ct.all(), xr], w=[pacc])
                    op("pe", lambda e, pacc=pacc, k=k, xi=xi, ct=ct: e.matmul(pacc.ap[:, 0:CW], ct.ap[:, 2 * k + 1, :], xi.ap, start=False, stop=(k == 3)), r=[ct.all(), xi], w=[pacc])
                    if k == 3:
                        y_ = yt[q % 2].all()
                        if sample:
                            pav = pacc.ap[:, 0:CW].rearrange("p (s t) -> p s t", t=5)[:, :, 1:5]
                            ucv = uc.ap.rearrange("p (s t) -> p s t", t=4)
                            yv = y_.ap.rearrange("p (s t) -> p s t", t=4)
                        else:
                            pav, ucv, yv = pacc.ap, uc.ap, y_.ap
                        op("dve", lambda e, yv=yv, ucv=ucv, pav=pav, t8=t8: e.scalar_tensor_tensor(yv, ucv, DS.ap[:, t8:t8 + 1], pav, ALU.mult, ALU.add), r=[uc, DS, pacc], w=[y_])
                        gdst = gy[t8, slice(q * CH, (q + 1) * CH)]
                        op("act", lambda e, gdst=gdst, y_=y_: e.activation(out=gdst.ap, in_=y_.ap, func=AF.Gelu), r=[y_], w=[gdst])

                CRk = [T(CR.ap, ["cr%d" % kk]) for kk in range(4)]
                CIk = [T(CI.ap, ["ci%d" % kk]) for kk in range(4)]
                prev = None
                for q in range(NCH):
                    cs = slice(q * CH, (q + 1) * CH)
                    pu = PB(next_pb([0, 1]))
                    for kt in range(8):
                        op("pe", lambda e, pu=pu, kt=kt, wt=wt, cs=cs: e.matmul(pu.ap[:, 0:CH], wt.ap[:, kt, :], hT.ap[:, kt, cs], start=(kt == 0), stop=(kt == 7)),
                           r=[wt.all(), hT[kt, cs]], w=[pu])
                    uc = uT[(t8 * NCH + q) % 3].all()
                    op("act", lambda e, pu=pu, uc=uc: e.copy(out=uc.ap, in_=pu.ap[:, 0:CH]), r=[pu], w=[uc])
                    pacc = PB(next_pb([6, 7]))
                    for k in range(4):
                        it = dict(q=q, k=k, uc=uc, st=rot[0] % 2, s3=rot[0] % 3, pacc=pacc)
                        rot[0] += 1
                        stage1(it)
                        if prev is not None:
                            stage2(prev)
                        prev = it
                stage2(prev)
        if False:
            for ri, (Ct, oname) in enumerate(((CR, "hre_p"), (CI, "him_p"))):
                pv = PB(next_pb([2, 3]))
                op("pe", lambda e, pv=pv, Ct=Ct: e.transpose(pv.ap[0:64, 0:128], Ct.ap, idf[:]), r=[Ct, IDF], w=[pv])
                so = tmp[0][ri].all()
                op("dve", lambda e, pv=pv, so=so: e.tensor_copy(so.ap[0:64, 0:128], pv.ap[0:64, 0:128]), r=[pv], w=[so])
                key = "%s:%d:%d" % (oname, l, u)
                dma("sp", O[oname][l, u].rearrange("(gp g2) p -> gp (g2 p)", g2=2), so.ap[0:64, 0:128], r=[so], w=[key])
                fin_keys.append(key)
        if sample:
            for ri, oname in enumerate(("hre_s", "him_s")):
                for s in range(NS_C):
                    pv = PB(next_pb([2, 3]))
                    op("pe", lambda e, pv=pv, ri=ri, s=s: e.transpose(pv.ap[0:64, 0:128], xsf.ap[:, ri, :, s], idf[:]), r=[xsf[ri], IDF], w=[pv])
                    so = XT[s % 3][0:128]
                    op("dve", lambda e, pv=pv, so=so: e.tensor_copy(so.ap[0:64, 0:128], pv.ap[0:64, 0:128]), r=[pv], w=[so])
                    key = "%s:%d:%d" % (oname, l, s)
                    dma("sp", O[oname][l, s].rearrange("(gp g2) p -> gp (g2 p)", g2=2), so.ap[0:64, 0:128], r=[so], w=[key])
                    fin_keys.append(key)
        cur[0] = w2
        HF = min(1024, NT)
        NHF = NT // HF
        HC = min(512, HF)
        NHC = HF // HC
        vT = TL([16, HF], BF16)
        wo = ABuf(arena, WREG, [16, 512], BF16)
        sz = [TL([HC], F32) for _ in range(2)]
        sg = [TL([HC], F32) for _ in range(2)]
        tg = [TL([HC], F32) for _ in range(2)]
        wz = [ABuf(arena, WREG + i * 2048, [8, 128], BF16) for i in range(2)]
        wga = [ABuf(arena, WREG + 4096 + i * 4096, [16, 128], BF16) for i in range(2)]
        wgb = [ABuf(arena, WREG + 12288 + i * 4096, [16, 128], BF16) for i in range(2)]

        def load_glu_w(j):
            dma("pool", wz[j % 2].ap, I["w_in_a"][l][:, W + j * 128: W + (j + 1) * 128].rearrange("(k p) c -> p k c", p=128), w=[wz[j % 2].all()])
            dma("pool", wga[j % 2].ap, I["w_glu"][l][:, j * 128:(j + 1) * 128].rearrange("(k p) c -> p k c", p=128), w=[wga[j % 2].all()])
            dma("pool", wgb[j % 2].ap, I["w_glu"][l][:, W + j * 128: W + (j + 1) * 128].rearrange("(k p) c -> p k c", p=128), w=[wgb[j % 2].all()])

        src, skey = x_src(u, l)
        dst, dkey = x_dst(u, l)
        rows = min(128, NT)
        for hf in range(NHF):
            load_glu_w(0)
            for j in range(16):
                if j + 1 < 16:
                    load_glu_w(j + 1)
                for c in range(NHC):
                    cs = slice(hf * HF + c * HC, hf * HF + (c + 1) * HC)
                    ls = slice(c * HC, (c + 1) * HC)
                    pz = PB(next_pb([0, 1]))
                    pa = PB(next_pb([2, 3]))
                    pg = PB(next_pb([4, 5]))
                    wzj, waj, wbj = wz[j % 2], wga[j % 2], wgb[j % 2]
                    for kt in range(8):
                        op("pe", lambda e, pz=pz, kt=kt, wzj=wzj, cs=cs: e.matmul(pz.ap[:, 0:HC], wzj.ap[:, kt, :], hT.ap[:, kt, cs], start=(kt == 0), stop=(kt == 7)),
                           r=[wzj.all(), hT[kt, cs]], w=[pz])
                    for kt in range(16):
                        op("pe", lambda e, pa=pa, kt=kt, waj=waj, cs=cs: e.matmul(pa.ap[:, 0:HC], waj.ap[:, kt, :], gy.ap[:, kt, cs], start=(kt == 0), stop=(kt == 15)),
                           r=[waj.all(), gy[kt, cs]], w=[pa])
                    for kt in range(16):
                        op("pe", lambda e, pg=pg, kt=kt, wbj=wbj, cs=cs: e.matmul(pg.ap[:, 0:HC], wbj.ap[:, kt, :], gy.ap[:, kt, cs], start=(kt == 0), stop=(kt == 15)),
                           r=[wbj.all(), gy[kt, cs]], w=[pg])
                    i2 = (j * NHC + c) % 2
                    sz_, sg_, tg_ = sz[i2].all(), sg[i2].all(), tg[i2].all()
                    BG = BGL[l]
                    op("act", lambda e, sz_=sz_, pz=pz: e.activation(out=sz_.ap, in_=pz.ap[:, 0:HC], func=AF.Silu), r=[pz], w=[sz_])
                    op("act", lambda e, sg_=sg_, pg=pg, j=j: e.activation(out=sg_.ap, in_=pg.ap[:, 0:HC], func=AF.Sigmoid, bias=BG.ap[:, 16 + j:17 + j]), r=[pg, BG], w=[sg_])
                    op("dve", lambda e, tg_=tg_, pa=pa, sg_=sg_, j=j: e.scalar_tensor_tensor(tg_.ap, pa.ap[:, 0:HC], BG.ap[:, j:j + 1], sg_.ap, ALU.add, ALU.mult), r=[pa, BG, sg_], w=[tg_])
                    vd = vT[j, ls]
                    op("dve", lambda e, vd=vd, tg_=tg_, sz_=sz_: e.tensor_tensor(vd.ap, tg_.ap, sz_.ap, ALU.mult), r=[tg_, sz_], w=[vd])
            ntile = HF // rows
            for hc in range(2):
                dma("pool", wo.ap, I["w_out_a"][l][:, hc * 512:(hc + 1) * 512].rearrange("(k p) c -> p k c", p=128), w=[wo.all()])
                for r_ in range(ntile):
                    tr = hf * ntile + r_
                    xt = XT[(tr * 2 + hc) % 3][0:512]
                    dma("sp", xt.ap[0:rows], src[tr * rows:(tr + 1) * rows, hc * 512:(hc + 1) * 512], r=[skey + ":%d:%d" % (tr, hc)], w=[xt])
                    po = PB(next_pb([6, 7]))
                    for kt in range(16):
                        op("pe", lambda e, po=po, kt=kt, r_=r_: e.matmul(po.ap[0:rows, :], vT.ap[:, kt, r_ * rows:(r_ + 1) * rows], wo.ap[:, kt, :],
                                                                        start=(kt == 0), stop=(kt == 15)),
                           r=[vT[kt, r_ * rows:(r_ + 1) * rows], wo.all()], w=[po])
                    op("dve", lambda e, po=po, xt=xt: e.tensor_tensor(xt.ap[0:rows], xt.ap[0:rows], po.ap[0:rows, :], ALU.add), r=[xt, po], w=[xt])
                    dma("sp", dst[tr * rows:(tr + 1) * rows, hc * 512:(hc + 1) * 512], xt.ap[0:rows], r=[xt], w=[dkey + ":%d:%d" % (tr, hc)])


    ROPC = SM(17 * 32, "ropc")
    ROPS = SM(17 * 32, "rops")

    def rope_tables():
        cur = [BIG]

        def TL(shape, dtype):
            b = ABuf(arena, cur[0], shape, dtype)
            cur[0] += (b.nbytes + 255) // 256 * 256
            return b
        posb = TL([17], F32)
        inv = TL([32], F32)
        ang = TL([17, 32], F32)
        red = TL([17, 32], F32)
        tf = TL([17, 32], F32)
        tm = TL([17, 32], F32)
        ti = TL([17, 32], I32)
        dma("sp", posb.ap, I["pos"], w=[posb.all()])
        dma("sp", inv.ap, I["invf"].partition_broadcast(128), w=[inv.all()])
        op("dve", lambda e: e.tensor_tensor(ang.ap, posb.ap.unsqueeze(2).to_broadcast([128, 17, 32]), inv.ap.unsqueeze(1).to_broadcast([128, 17, 32]), ALU.mult),
           r=[posb.all(), inv.all()], w=[ang.all()])
        range_reduce(red.all(), ang.all(), 0.0, ti.all(), tf.all(), tm.all())
        op("act", lambda e: e.activation(out=ROPS.ap.rearrange("p (a b) -> p a b", a=17), in_=red.ap, func=AF.Sin), r=[red.all()], w=[ROPS])
        range_reduce(red.all(), ang.all(), math.pi / 2, ti.all(), tf.all(), tm.all())
        op("act", lambda e: e.activation(out=ROPC.ap.rearrange("p (a b) -> p a b", a=17), in_=red.ap, func=AF.Sin), r=[red.all()], w=[ROPC])

    def rope_apply(src4, dst4, rows, t_idx, nh, tA, tB):
        cosb = ROPC.ap[0:rows, t_idx * 32:(t_idx + 1) * 32].unsqueeze(1).to_broadcast([rows, nh, 32])
        sinb = ROPS.ap[0:rows, t_idx * 32:(t_idx + 1) * 32].unsqueeze(1).to_broadcast([rows, nh, 32])
        a = tA.ap[0:rows, 0:nh * 32].rearrange("p (h j) -> p h j", h=nh)
        b = tB.ap[0:rows, 0:nh * 32].rearrange("p (h j) -> p h j", h=nh)
        return cosb, sinb, a, b

    def mla_mem(u):
        sample = (u == 2)
        NT = NTS if sample else SEQ
        cur = [BIG]
        M = {}

        def TL(name, shape, dtype):
            b = ABuf(arena, cur[0], shape, dtype)
            cur[0] += (b.nbytes + 255) // 256 * 256
            assert cur[0] <= ARENA_BYTES, (name, cur[0])
            M[name] = b
            return b
        TL("latT", [2, NT], BF16)
        TL("krT2", [NT], BF16)
        TL("glat", [KVL], F32)
        TL("gq", [QL], F32)
        if not sample:
            TL("KnT", [NH, NT], BF16)
            TL("V", [16, D], BF16)
        else:
            TL("latb", [KVL], BF16)
            TL("wukT", [NH, KVL], BF16)
            TL("wuv", [2, D], BF16)
            TL("qlT", [2, NH, NTS], BF16)
            TL("qrTs", [NH, NTS], BF16)
            TL("Oall", [2, NH, NTS], F32)
            TL("Dall", [NH, NTS], F32)
            TL("olT", [2, NH, NTS], BF16)
            TL("mnew", [NTS], F32)
            for i in range(2):
                TL("pgl%d" % i, [4, KVL], BF16)
                TL("pgk%d" % i, [4, ROPE], BF16)
                TL("cT%d" % i, [2, 512], BF16)
                TL("krT%d" % i, [512], BF16)
                TL("pTs%d" % i, [128], BF16)
            TL("pTn", [512], BF16)
        CH = min(512, NT)
        TL("hTc", [8, CH], BF16)
        TL("hTt0", [8, 128], BF16)
        TL("hTt1", [8, 128], BF16)
        TL("latf", [KVL], F32)
        TL("krf", [ROPE], F32)
        TL("krb", [128], BF16)
        TL("cqb0", [QL], BF16)
        TL("cqb1", [QL], BF16)
        TL("cqT", [3, CH], BF16)
        TL("sgT", [NH, CH], BF16)
        TL("qnT", [NH, CH], BF16)
        TL("qrb0", [512], BF16)
        TL("qrb1", [512], BF16)
        TL("qrT", [4, CH], BF16)
        for i in range(4):
            TL("pT%d" % i, [CH], BF16)
        for i in range(2):
            TL("rden%d" % i, [CH], F32)
            TL("tgs%d" % i, [CH], F32)
            TL("ra%d" % i, [256], F32)
            TL("rb%d" % i, [256], F32)
        TL("ogT", [NH, CH], BF16)
        return M

    def latent(u, M):
        sample = (u == 2)
        NT = NTS if sample else SEQ
        rows = min(128, NT)
        ntile = NT // rows
        src, skey = x_src(u, 2)
        latT, krT2 = M["latT"], M["krT2"]
        wd = ABuf(arena, WREG, [8, KVL + ROPE], BF16)
        dma("pool", wd.ap, I["w_dkv"].rearrange("(k p) c -> p k c", p=128), w=[wd.all()])
        load_gain(I["norm_kv"])
        glat = M["glat"].all()
        dma("sp", glat.ap, I["norm_latent"].partition_broadcast(128), w=[glat])
        for tr in range(ntile):
            xt = XT[tr % 3].all()
            dma("sp", xt.ap[0:rows], src[tr * rows:(tr + 1) * rows, :], r=[skey + ":%d:0" % tr, skey + ":%d:1" % tr], w=[xt])
            hb = HB[tr % 2].all()
            rms_tile(xt, rows, D, gbuf[:], hb)
            pv = PBb(next_pb([0, 1]))
            for kt in range(8):
                op("pe", lambda e, kt=kt, pv=pv, hb=hb: e.transpose(pv.ap[:, kt * 128:kt * 128 + rows], hb.ap[0:rows, kt * 128:(kt + 1) * 128], idb[0:rows, 0:rows]), r=[hb, IDB], w=[pv])
            ht = M["hTt%d" % (tr % 2)]
            op("act", lambda e, pv=pv, ht=ht: e.copy(out=ht.ap[:, :, 0:rows], in_=pv.ap.rearrange("p (k c) -> p k c", k=8)[:, :, 0:rows]), r=[pv], w=[ht.all()])
            pc = PB(next_pb([2, 3]))
            for kt in range(8):
                op("pe", lambda e, kt=kt, pc=pc, ht=ht: e.matmul(pc.ap[0:rows, 0:KVL + ROPE], ht.ap[:, kt, 0:rows], wd.ap[:, kt, :], start=(kt == 0), stop=(kt == 7)),
                   r=[ht.all(), wd.all()], w=[pc])
            latf = M["latf"].all()
            pcl = T(pc.ap[:, 0:KVL], pc.k)
            rms_tile(pcl, rows, KVL, glat.ap, latf, gain_T=glat)
            okey = "lat:%d:%d" % (u, tr)
            odst = O["lat_p"][u] if not sample else O["lat_s"]
            dma("sp", odst[tr * rows:(tr + 1) * rows, :], latf.ap[0:rows], r=[latf], w=[okey])
            fin_keys.append(okey)
            lb_ = HB[tr % 2][0:KVL] if not sample else M["latb"].all()
            op("dve", lambda e, lb_=lb_, latf=latf: e.tensor_copy(lb_.ap[0:rows], latf.ap[0:rows]), r=[latf], w=[lb_])
            pv2 = PBb(next_pb([4, 5]))
            for c2 in range(2):
                op("pe", lambda e, c2=c2, pv2=pv2, lb_=lb_: e.transpose(pv2.ap[:, c2 * 128:c2 * 128 + rows], lb_.ap[0:rows, c2 * 128:(c2 + 1) * 128], idb[0:rows, 0:rows]), r=[lb_, IDB], w=[pv2])
            ld = latT[:, tr * rows:(tr + 1) * rows]
            op("act", lambda e, pv2=pv2, ld=ld: e.copy(out=ld.ap, in_=pv2.ap[:, 0:256].rearrange("p (k c) -> p k c", k=2)[:, :, 0:rows]), r=[pv2], w=[ld])
            t_idx = 16 if sample else tr
            krf = M["krf"].all()
            ra, rb = M["ra%d" % (tr % 2)].all(), M["rb%d" % (tr % 2)].all()
            cs_ = ROPC.ap[0:rows, t_idx * 32:(t_idx + 1) * 32]
            sn_ = ROPS.ap[0:rows, t_idx * 32:(t_idx + 1) * 32]
            x1 = pc.ap[0:rows, KVL:KVL + 32]
            x2 = pc.ap[0:rows, KVL + 32:KVL + 64]
            op("dve", lambda e, ra=ra, x1=x1, cs_=cs_: e.tensor_tensor(ra.ap[0:rows, 0:32], x1, cs_, ALU.mult), r=[pc, ROPC], w=[ra])
            op("dve", lambda e, rb=rb, x2=x2, sn_=sn_: e.tensor_tensor(rb.ap[0:rows, 0:32], x2, sn_, ALU.mult), r=[pc, ROPS], w=[rb])
            op("dve", lambda e, ra=ra, rb=rb, krf=krf: e.tensor_tensor(krf.ap[0:rows, 0:32], ra.ap[0:rows, 0:32], rb.ap[0:rows, 0:32], ALU.subtract), r=[ra, rb], w=[krf])
            op("dve", lambda e, ra=ra, x1=x1, sn_=sn_: e.tensor_tensor(ra.ap[0:rows, 32:64], x1, sn_, ALU.mult), r=[pc, ROPS], w=[ra])
            op("dve", lambda e, rb=rb, x2=x2, cs_=cs_: e.tensor_tensor(rb.ap[0:rows, 32:64], x2, cs_, ALU.mult), r=[pc, ROPC], w=[rb])
            op("dve", lambda e, ra=ra, rb=rb, krf=krf: e.tensor_tensor(krf.ap[0:rows, 32:64], ra.ap[0:rows, 32:64], rb.ap[0:rows, 32:64], ALU.add), r=[ra, rb], w=[krf])
            okey = "kr:%d:%d" % (u, tr)
            odst = O["kr_p"][u] if not sample else O["kr_s"]
            dma("sp", odst[tr * rows:(tr + 1) * rows, :], krf.ap[0:rows], r=[krf], w=[okey])
            fin_keys.append(okey)
            krb = M["krb"].all()
            op("dve", lambda e, krb=krb, krf=krf: e.tensor_copy(krb.ap[0:rows].rearrange("p (a b) -> p a b", a=2), krf.ap[0:rows].unsqueeze(1).to_broadcast([rows, 2, ROPE])), r=[krf], w=[krb])
            pv3 = PBb(next_pb([6, 7]))
            op("pe", lambda e, pv3=pv3, krb=krb: e.transpose(pv3.ap[:, 0:rows], krb.ap[0:rows, :], idb[0:rows, 0:rows]), r=[krb, IDB], w=[pv3])
            kd = krT2[tr * rows:(tr + 1) * rows]
            op("dve", lambda e, pv3=pv3, kd=kd: e.tensor_copy(kd.ap, pv3.ap[:, 0:rows]), r=[pv3], w=[kd])
        if not sample:
            wuk = ABuf(arena, WREG + 8192, [2, D], BF16)
            wuv = ABuf(arena, WREG + 12288, [2, D], BF16)
            dma("pool", wuk.ap, I["w_uk"].rearrange("(k p) c -> p k c", p=128), w=[wuk.all()])
            dma("pool", wuv.ap, I["w_uv"].rearrange("(k p) c -> p k c", p=128), w=[wuv.all()])
            KnT, V = M["KnT"], M["V"]
            for h in range(NH):
                for q in range(NT // 512):
                    cs = slice(q * 512, (q + 1) * 512)
                    pk = PB(next_pb([0, 1, 2, 3]))
                    for c2 in range(2):
                        op("pe", lambda e, pk=pk, c2=c2, h=h, cs=cs: e.matmul(pk.ap, wuk.ap[:, c2, h * 128:(h + 1) * 128], latT.ap[:, c2, cs], start=(c2 == 0), stop=(c2 == 1)),
                           r=[wuk.all(), latT[c2, cs]], w=[pk])
                    kd = KnT[h, cs]
                    if (h + q) % 2 == 0:
                        op("act", lambda e, pk=pk, kd=kd: e.copy(out=kd.ap, in_=pk.ap), r=[pk], w=[kd])
                    else:
                        op("dve", lambda e, pk=pk, kd=kd: e.tensor_copy(kd.ap, pk.ap), r=[pk], w=[kd])
            for r_ in range(16):
                for hc in range(2):
                    pk = PB(next_pb([4, 5, 6, 7]))
                    for c2 in range(2):
                        op("pe", lambda e, pk=pk, c2=c2, r_=r_, hc=hc: e.matmul(pk.ap, latT.ap[:, c2, r_ * 128:(r_ + 1) * 128], wuv.ap[:, c2, hc * 512:(hc + 1) * 512], start=(c2 == 0), stop=(c2 == 1)),
                           r=[wuv.all(), latT[c2, r_ * 128:(r_ + 1) * 128]], w=[pk])
                    vd = V[r_, hc * 512:(hc + 1) * 512]
                    if (r_ + hc) % 2 == 0:
                        op("act", lambda e, pk=pk, vd=vd: e.copy(out=vd.ap, in_=pk.ap), r=[pk], w=[vd])
                    else:
                        op("dve", lambda e, pk=pk, vd=vd: e.tensor_copy(vd.ap, pk.ap), r=[pk], w=[vd])

    def mla_layer(u, j, M):
        sample = (u == 2)
        NT = NTS if sample else SEQ
        rows = min(128, NT)
        CH = min(512, NT)
        NCH = NT // CH
        TPC = CH // rows
        li = 2 + j
        src, skey = x_src(u, li)
        dst, dkey = x_dst(u, li)
        hTc, cqT, sgT, qnT, qrT, ogT = M["hTc"], M["cqT"], M["sgT"], M["qnT"], M["qrT"], M["ogT"]
        gq = M["gq"].all()
        load_gain(I["norm_b"][j])
        dma("sp", gq.ap, I["norm_q"][j].partition_broadcast(128), w=[gq])
        wcq = ABuf(arena, WREG, [8, QL], BF16)
        wgt = [ABuf(arena, WREG + 6144 + i * 2048, [8, 128], BF16) for i in range(2)]
        wuq = ABuf(arena, WREG + 10240, [3, NH * 192], BF16)
        wob = ABuf(arena, WREG, [8, 512], BF16)
        if sample:
            sample_prep(M)
        for q in range(NCH):
            dma("pool", wcq.ap, I["w_in_b"][j][:, 0:QL].rearrange("(k p) c -> p k c", p=128), w=[wcq.all()])
            dma("pool", wuq.ap, I["w_uq"][j].rearrange("(k p) c -> p k c", p=128), w=[wuq.all()])
            build_hT(u, li, hTc, NT, ntiles=TPC, tile0=q * TPC)
            for t4 in range(TPC):
                ts_ = slice(t4 * rows, (t4 + 1) * rows)
                pc = PB(next_pb([2, 3]))
                for kt in range(8):
                    op("pe", lambda e, pc=pc, kt=kt, ts_=ts_: e.matmul(pc.ap[0:rows, 0:QL], hTc.ap[:, kt, ts_], wcq.ap[:, kt, :], start=(kt == 0), stop=(kt == 7)),
                       r=[hTc[kt, ts_], wcq.all()], w=[pc])
                cqb = M["cqb%d" % (t4 % 2)].all()
                rms_tile(T(pc.ap[:, 0:QL], pc.k), rows, QL, gq.ap, cqb, gain_T=gq)
                pv = PBb(next_pb([4, 5]))
                for c3 in range(3):
                    op("pe", lambda e, pv=pv, c3=c3, cqb=cqb: e.transpose(pv.ap[:, c3 * 128:c3 * 128 + rows], cqb.ap[0:rows, c3 * 128:(c3 + 1) * 128], idb[0:rows, 0:rows]), r=[cqb, IDB], w=[pv])
                cd = cqT[:, ts_]
                op("dve", lambda e, pv=pv, cd=cd: e.tensor_copy(cd.ap, pv.ap[:, 0:384].rearrange("p (k c) -> p k c", k=3)[:, :, 0:rows]), r=[pv], w=[cd])
            def load_gate(i):
                dma("pool", wgt[i % 2].ap, I["w_in_b"][j][:, QL + i * 128: QL + (i + 1) * 128].rearrange("(k p) c -> p k c", p=128), w=[wgt[i % 2].all()])
            load_gate(0)
            for i in range(NH):
                if i + 1 < NH:
                    load_gate(i + 1)
                pg = PB(next_pb([6, 7]))
                wg = wgt[i % 2]
                for kt in range(8):
                    op("pe", lambda e, pg=pg, kt=kt, wg=wg: e.matmul(pg.ap[:, 0:CH], wg.ap[:, kt, :], hTc.ap[:, kt, :], start=(kt == 0), stop=(kt == 7)), r=[wg.all(), hTc[kt]], w=[pg])
                sd = sgT[i]
                op("act", lambda e, pg=pg, sd=sd: e.activation(out=sd.ap, in_=pg.ap[:, 0:CH], func=AF.Silu), r=[pg], w=[sd])
            for h in range(NH):
                pq = PB(next_pb([0, 1]))
                for c3 in range(3):
                    op("pe", lambda e, pq=pq, c3=c3, h=h: e.matmul(pq.ap[:, 0:CH], wuq.ap[:, c3, h * 192:h * 192 + 128], cqT.ap[:, c3, :], start=(c3 == 0), stop=(c3 == 2)),
                       r=[wuq.all(), cqT[c3]], w=[pq])
                qd = qnT[h]
                op("dve", lambda e, pq=pq, qd=qd: e.tensor_copy(qd.ap, pq.ap[:, 0:CH]), r=[pq], w=[qd])
            wuq_r = wuq.ap.rearrange("p k (h d) -> p k h d", h=NH)
            for t4 in range(TPC):
                ts_ = slice(t4 * rows, (t4 + 1) * rows)
                pr_ = PB(next_pb([2, 3]))
                for c3 in range(3):
                    op("pe", lambda e, pr_=pr_, c3=c3, ts_=ts_: e.matmul(pr_.ap[0:rows, :].rearrange("p (h d) -> p h d", h=NH), cqT.ap[:, c3, ts_], wuq_r[:, c3, :, 128:192], start=(c3 == 0), stop=(c3 == 2)),
                       r=[wuq.all(), cqT[c3, ts_]], w=[pr_])
                t_idx = 16 if sample else (q * TPC + t4)
                ra, rb = M["ra%d" % (t4 % 2)].all(), M["rb%d" % (t4 % 2)].all()
                qrb = M["qrb%d" % (t4 % 2)].all()
                cosb = ROPC.ap[0:rows, t_idx * 32:(t_idx + 1) * 32].unsqueeze(1).to_broadcast([rows, NH, 32])
                sinb = ROPS.ap[0:rows, t_idx * 32:(t_idx + 1) * 32].unsqueeze(1).to_broadcast([rows, NH, 32])
                pr4 = pr_.ap[0:rows, :].rearrange("p (h d) -> p h d", h=NH)
                q1, q2 = pr4[:, :, 0:32], pr4[:, :, 32:64]
                rav = ra.ap[0:rows].rearrange("p (h d) -> p h d", h=NH)
                rbv = rb.ap[0:rows].rearrange("p (h d) -> p h d", h=NH)
                qv = qrb.ap[0:rows].rearrange("p (h d) -> p h d", h=NH)
                op("dve", lambda e, rav=rav, q1=q1, cosb=cosb: e.tensor_tensor(rav, q1, cosb, ALU.mult), r=[pr_, ROPC], w=[ra])
                op("dve", lambda e, rbv=rbv, q2=q2, sinb=sinb: e.tensor_tensor(rbv, q2, sinb, ALU.mult), r=[pr_, ROPS], w=[rb])
                op("pool", lambda e, qv=qv, rav=rav, rbv=rbv: e.tensor_tensor(qv[:, :, 0:32], rav, rbv, ALU.subtract), r=[ra, rb], w=[qrb])
                op("dve", lambda e, rav=rav, q1=q1, sinb=sinb: e.tensor_tensor(rav, q1, sinb, ALU.mult), r=[pr_, ROPS, qrb], w=[ra])
                op("dve", lambda e, rbv=rbv, q2=q2, cosb=cosb: e.tensor_tensor(rbv, q2, cosb, ALU.mult), r=[pr_, ROPC, qrb], w=[rb])
                op("pool", lambda e, qv=qv, rav=rav, rbv=rbv: e.tensor_tensor(qv[:, :, 32:64], rav, rbv, ALU.add), r=[ra, rb], w=[qrb])
                if not sample:
                    pv = PBb(next_pb([4, 5]))
                    for hp in range(4):
                        op("pe", lambda e, pv=pv, hp=hp, qrb=qrb: e.transpose(pv.ap[:, hp * 128:(hp + 1) * 128], qrb.ap[:, hp * 128:(hp + 1) * 128], idb[:]), r=[qrb, IDB], w=[pv])
                    qd = qrT[:, ts_]
                    op("act", lambda e, pv=pv, qd=qd: e.copy(out=qd.ap, in_=pv.ap[:, 0:512].rearrange("p (k c) -> p k c", k=4)), r=[pv], w=[qd])
                else:
                    pv = PBb(next_pb([4, 5]))
                    for h in range(NH):
                        op("pe", lambda e, pv=pv, h=h, qrb=qrb: e.transpose(pv.ap[0:64, h * 64:(h + 1) * 64], qrb.ap[0:64, h * 64:(h + 1) * 64], idb[0:64, 0:64]), r=[qrb, IDB], w=[pv])
                    qd = M["qrTs"].all()
                    op("act", lambda e, pv=pv, qd=qd: e.copy(out=qd.ap[0:64], in_=pv.ap[0:64, 0:512].rearrange("p (k c) -> p k c", k=NH)), r=[pv], w=[qd])
            if not sample:
                KnT, V, krT2 = M["KnT"], M["V"], M["krT2"]
                pti = 0
                for h in range(NH):
                    po = PB(next_pb([0, 1]))
                    pd = PB(next_pb([2, 3]))
                    nkt = 4 * q + 4
                    base = 64 * (h % 2)
                    def emit_pv(pt_, kt, c0, po=po, pd=pd, h=h, nkt=nkt):
                        op("pe", lambda e, po=po, pt_=pt_, h=h, kt=kt, c0=c0, nkt=nkt: e.matmul(po.ap[:, c0:512], V.ap[:, kt, h * 128:(h + 1) * 128], pt_.ap[:, c0:512], start=(kt == 0), stop=(kt == nkt - 1), skip_group_check=True),
                           r=[V[kt, h * 128:(h + 1) * 128], pt_], w=[po])
                        op("pe", lambda e, pd=pd, pt_=pt_, kt=kt, c0=c0, nkt=nkt: e.matmul(pd.ap[:, c0:512], onesb[:], pt_.ap[:, c0:512], start=(kt == 0), stop=(kt == nkt - 1), skip_group_check=True),
                           r=["onesb", pt_], w=[pd])
                    pend = None
                    for kt in range(nkt):
                        c0 = max(0, kt - 4 * q) * 128
                        ks = slice(kt * 128, (kt + 1) * 128)
                        ps_ = PB(next_pb([4, 5, 6, 7]))
                        op("pe", lambda e, ps_=ps_, h=h, ks=ks, c0=c0: e.matmul(ps_.ap[:, c0:512], KnT.ap[:, h, ks], qnT.ap[:, h, c0:512], start=True, stop=False),
                           r=[KnT[h, ks], qnT[h]], w=[ps_])
                        op("pe", lambda e, ps_=ps_, h=h, ks=ks, c0=c0, base=base: e.matmul(ps_.ap[:, c0:512], krT2.ap[base:base + 64, ks], qrT.ap[base:base + 64, h // 2, c0:512], start=False, stop=True),
                           r=[krT2[ks], qrT[h // 2]], w=[ps_])
                        pt_ = M["pT%d" % (pti % 4)].all()
                        pti += 1
                        op("act", lambda e, ps_=ps_, pt_=pt_, c0=c0: e.activation(out=pt_.ap[:, c0:512], in_=ps_.ap[:, c0:512], func=AF.Exp, scale=SCALE), r=[ps_], w=[pt_])
                        if kt >= 4 * q:
                            op("dve", lambda e, pt_=pt_, c0=c0: e.tensor_tensor(pt_.ap[:, c0:c0 + 128], pt_.ap[:, c0:c0 + 128], trib[:], ALU.mult), r=[pt_, "trib"], w=[pt_])
                        if pend is not None:
                            emit_pv(*pend)
                        pend = (pt_, kt, c0)
                        prefetch_step()
                    emit_pv(*pend)
                    rd = M["rden%d" % (h % 2)].all()
                    tg_ = M["tgs%d" % (h % 2)].all()
                    op("dve", lambda e, rd=rd, pd=pd: e.reciprocal(rd.ap, pd.ap), r=[pd], w=[rd])
                    op("dve", lambda e, tg_=tg_, rd=rd, h=h: e.tensor_tensor(tg_.ap, rd.ap, sgT.ap[:, h, :], ALU.mult), r=[rd, sgT[h]], w=[tg_])
                    od = ogT[h]
                    op("dve", lambda e, od=od, po=po, tg_=tg_: e.tensor_tensor(od.ap, po.ap, tg_.ap, ALU.mult), r=[po, tg_], w=[od])
            else:
                sample_attention(M, sgT, qnT, ogT)
            for hc in range(2):
                dma("pool", wob.ap, I["w_out_b"][j][:, hc * 512:(hc + 1) * 512].rearrange("(k p) c -> p k c", p=128), w=[wob.all()])
                for t4 in range(TPC):
                    tr = q * TPC + t4
                    ts_ = slice(t4 * rows, (t4 + 1) * rows)
                    xt = XT[(tr * 2 + hc) % 3][0:512]
                    dma("sp", xt.ap[0:rows], src[tr * rows:(tr + 1) * rows, hc * 512:(hc + 1) * 512], r=[skey + ":%d:%d" % (tr, hc)], w=[xt])
                    pp = PB(next_pb([6, 7]))
                    for h in range(NH):
                        op("pe", lambda e, pp=pp, h=h, ts_=ts_: e.matmul(pp.ap[0:rows, :], ogT.ap[:, h, ts_], wob.ap[:, h, :], start=(h == 0), stop=(h == NH - 1)),
                           r=[ogT[h, ts_], wob.all()], w=[pp])
                    op("dve", lambda e, pp=pp, xt=xt: e.tensor_tensor(xt.ap[0:rows], xt.ap[0:rows], pp.ap[0:rows, :], ALU.add), r=[xt, pp], w=[xt])
                    dma("sp", dst[tr * rows:(tr + 1) * rows, hc * 512:(hc + 1) * 512], xt.ap[0:rows], r=[xt], w=[dkey + ":%d:%d" % (tr, hc)])

    def sample_prep(M):
        if M.get("_prep"):
            return
        M["_prep"] = True
        wuk = ABuf(arena, WREG + 16384, [2, D], BF16)
        dma("pool", wuk.ap, I["w_uk"].rearrange("(k p) c -> p k c", p=128), w=[wuk.all()])
        dma("pool", M["wuv"].ap, I["w_uv"].rearrange("(k p) c -> p k c", p=128), w=[M["wuv"].all()])
        wukT = M["wukT"]
        for h in range(NH):
            pv = PBb(next_pb([4, 5]))
            for c2 in range(2):
                op("pe", lambda e, pv=pv, c2=c2, h=h: e.transpose(pv.ap[:, c2 * 128:(c2 + 1) * 128], wuk.ap[:, c2, h * 128:(h + 1) * 128], idb[:]), r=[wuk.all(), IDB], w=[pv])
            wd_ = wukT[h]
            op("dve", lambda e, pv=pv, wd_=wd_: e.tensor_copy(wd_.ap, pv.ap[:, 0:256]), r=[pv], w=[wd_])
        mn = M["mnew"].all()
        dma("sp", mn.ap[0:64], I["mnew"], w=[mn])

    def sample_attention(M, sgT, qnT, ogT):
        wukT, wuv, qlT, qrTs, latT, krT2, latb = M["wukT"], M["wuv"], M["qlT"], M["qrTs"], M["latT"], M["krT2"], M["latb"]
        Oall, Dall, olT = M["Oall"], M["Dall"], M["olT"]
        for h in range(NH):
            for c2 in range(2):
                pq = PB(next_pb([0, 1]))
                op("pe", lambda e, pq=pq, h=h, c2=c2: e.matmul(pq.ap[:, 0:NTS], wukT.ap[:, h, c2 * 128:(c2 + 1) * 128], qnT.ap[:, h, :], start=True, stop=True), r=[wukT[h], qnT[h]], w=[pq])
                qd = qlT[c2, h]
                op("dve", lambda e, pq=pq, qd=qd: e.tensor_copy(qd.ap, pq.ap[:, 0:NTS]), r=[pq], w=[qd])
        step = 0
        for s in range(NS_C):
            qs = slice(4 * s, 4 * s + 4)
            pacc = PB(2)
            pden = PB(3)
            for g in range(NPAGES // 4):
                b = step % 2
                step += 1
                pgl, pgk, cT, krT, pTs = M["pgl%d" % b], M["pgk%d" % b], M["cT%d" % b], M["krT%d" % b], M["pTs%d" % b]
                dma("sp", pgl.ap, PG_L[s, g * 4:(g + 1) * 4].rearrange("g t c -> t g c"), r=["pg_l:%d" % s], w=[pgl.all()])
                dma("sp", pgk.ap, PG_K[s, g * 4:(g + 1) * 4].rearrange("g t c -> t g c"), r=["pg_k:%d" % s], w=[pgk.all()])
                pva = PBb(4 + b)
                for i in range(4):
                    for c2 in range(2):
                        op("pe", lambda e, pva=pva, i=i, c2=c2, pgl=pgl: e.transpose(pva.ap[:, (c2 * 4 + i) * 128:(c2 * 4 + i + 1) * 128], pgl.ap[:, i, c2 * 128:(c2 + 1) * 128], idb[:]), r=[pgl[i], IDB], w=[pva])
                op("act", lambda e, pva=pva, cT=cT: e.copy(out=cT.ap, in_=pva.ap.rearrange("p (a b) -> p a b", a=2)), r=[pva], w=[cT.all()])
                pvb = PBb(6 + b)
                for i in range(4):
                    op("pe", lambda e, pvb=pvb, i=i, pgk=pgk: e.transpose(pvb.ap[0:64, i * 128:(i + 1) * 128], pgk.ap[:, i, :], idb[:]), r=[pgk[i], IDB], w=[pvb])
                op("dve", lambda e, pvb=pvb, krT=krT: e.tensor_copy(krT.ap[0:64], pvb.ap[0:64, 0:512]), r=[pvb], w=[krT.all()])
                psc = PB(next_pb([0, 1]))
                for i in range(4):
                    o_ = psc.ap[:, i * 32:(i + 1) * 32].rearrange("p (h t) -> p h t", h=NH)
                    for c2 in range(2):
                        op("pe", lambda e, o_=o_, i=i, c2=c2, cT=cT, qs=qs: e.matmul(o_, cT.ap[:, c2, i * 128:(i + 1) * 128], qlT.ap[:, c2, :, qs], start=(c2 == 0), stop=False),
                           r=[cT.all(), qlT[c2]], w=[psc])
                    op("pe", lambda e, o_=o_, i=i, krT=krT, qs=qs: e.matmul(o_, krT.ap[0:64, i * 128:(i + 1) * 128], qrTs.ap[0:64, :, qs], start=False, stop=True),
                       r=[krT.all(), qrTs.all()], w=[psc])
                op("act", lambda e, psc=psc, pTs=pTs: e.activation(out=pTs.ap, in_=psc.ap[:, 0:128], func=AF.Exp, scale=SCALE), r=[psc], w=[pTs.all()])
                for i in range(4):
                    for c2 in range(2):
                        first = (g == 0 and i == 0)
                        last = (g == NPAGES // 4 - 1 and i == 3)
                        op("pe", lambda e, pacc=pacc, i=i, c2=c2, pgl=pgl, pTs=pTs, first=first, last=last: e.matmul(pacc.ap[:, c2 * 32:(c2 + 1) * 32], pgl.ap[:, i, c2 * 128:(c2 + 1) * 128], pTs.ap[:, i * 32:(i + 1) * 32], start=first, stop=last, skip_group_check=True),
                           r=[pgl[i], pTs.all()], w=[pacc])
                    op("pe", lambda e, pden=pden, i=i, pTs=pTs, first=first, last=last: e.matmul(pden.ap[:, 0:32], onesb[:], pTs.ap[:, i * 32:(i + 1) * 32], start=first, stop=last, skip_group_check=True),
                       r=["onesb", pTs.all()], w=[pden])
            op("dve", lambda e, pacc=pacc, qs=qs: e.tensor_copy(Oall.ap[:, :, :, qs], pacc.ap[:, 0:64].rearrange("p (c h t) -> p c h t", c=2, h=NH)), r=[pacc], w=[Oall.all()])
            op("dve", lambda e, pden=pden, qs=qs: e.tensor_copy(Dall.ap[:, :, qs], pden.ap[:, 0:32].rearrange("p (h t) -> p h t", h=NH)), r=[pden], w=[Dall.all()])
        psn = PB(next_pb([0, 1]))
        for c2 in range(2):
            op("pe", lambda e, psn=psn, c2=c2: e.matmul(psn.ap[0:64, :], latT.ap[:, c2, 0:64], qlT.ap[:, c2].rearrange("p h t -> p (h t)"), start=(c2 == 0), stop=False), r=[latT[c2], qlT[c2]], w=[psn])
        op("pe", lambda e, psn=psn: e.matmul(psn.ap[0:64, :], krT2.ap[0:64, 0:64], qrTs.ap[0:64].rearrange("p h t -> p (h t)"), start=False, stop=True), r=[krT2.all(), qrTs.all()], w=[psn])
        pTn = M["pTn"].all()
        tgs = XT[0][0:512]
        op("act", lambda e, psn=psn: e.activation(out=tgs.ap[0:64], in_=psn.ap[0:64, :], func=AF.Exp, scale=SCALE), r=[psn], w=[tgs])
        mn = M["mnew"].all()
        op("dve", lambda e: e.tensor_tensor(pTn.ap[0:64].rearrange("p (h t) -> p h t", h=NH), tgs.ap[0:64].rearrange("p (h t) -> p h t", h=NH),
                                            mn.ap[0:64].unsqueeze(1).to_broadcast([64, NH, NTS]), ALU.mult), r=[tgs, mn], w=[pTn])
        for c2 in range(2):
            pn = PB(next_pb([4, 5]))
            op("pe", lambda e, pn=pn, c2=c2: e.matmul(pn.ap, latb.ap[0:64, c2 * 128:(c2 + 1) * 128], pTn.ap[0:64], start=True, stop=True), r=[latb.all(), pTn], w=[pn])
            od = Oall[c2]
            op("dve", lambda e, pn=pn, od=od: e.tensor_tensor(od.ap.rearrange("p h t -> p (h t)"), od.ap.rearrange("p h t -> p (h t)"), pn.ap, ALU.add), r=[pn, od], w=[od])
        pn = PB(next_pb([6, 7]))
        op("pe", lambda e, pn=pn: e.matmul(pn.ap, onesb[0:64, :], pTn.ap[0:64], start=True, stop=True), r=["onesb", pTn], w=[pn])
        Da = Dall.all()
        op("dve", lambda e, pn=pn: e.tensor_tensor(Dall.ap.rearrange("p h t -> p (h t)"), Dall.ap.rearrange("p h t -> p (h t)"), pn.ap, ALU.add), r=[pn, Da], w=[Da])
        op("dve", lambda e: e.reciprocal(Dall.ap, Dall.ap), r=[Da], w=[Da])
        for c2 in range(2):
            od = olT[c2]
            op("dve", lambda e, c2=c2, od=od: e.tensor_tensor(od.ap, Oall.ap[:, c2], Dall.ap, ALU.mult), r=[Oall[c2], Da], w=[od])
        for h in range(NH):
            pq = PB(next_pb([0, 1]))
            for c2 in range(2):
                op("pe", lambda e, pq=pq, h=h, c2=c2: e.matmul(pq.ap[:, 0:NTS], wuv.ap[:, c2, h * 128:(h + 1) * 128], olT.ap[:, c2, h, :], start=(c2 == 0), stop=(c2 == 1)), r=[wuv.all(), olT[c2]], w=[pq])
            od = ogT[h]
            op("dve", lambda e, pq=pq, od=od, h=h: e.tensor_tensor(od.ap, pq.ap[:, 0:NTS], sgT.ap[:, h, :], ALU.mult), r=[pq, sgT[h]], w=[od])

    def final_norm(u):
        sample = (u == 2)
        NT = NTS if sample else SEQ
        rows = min(128, NT)
        src, skey = x_src(u, 4)
        load_gain(I["norm_f"])
        for tr in range(NT // rows):
            xt = XT[tr % 3].all()
            dma("sp", xt.ap[0:rows], src[tr * rows:(tr + 1) * rows, :], r=[skey + ":%d:0" % tr, skey + ":%d:1" % tr], w=[xt])
            ot = XT[(tr + 1) % 3].all()
            rms_tile(xt, rows, D, gbuf[:], ot)
            key = "y:%d:%d" % (u, tr)
            odst = O["y_p"][u] if not sample else O["y_s"]
            dma("sp", odst[tr * rows:(tr + 1) * rows, :], ot.ap[0:rows], r=[ot], w=[key])
            fin_keys.append(key)

    PF["on"] = ("mla" in stages) and (2 in units)
    if "s5" in stages:
        for l in range(2):
            s5_tables(l)
            for u in units:
                s5_layer(u, l)

    if "lat" in stages:
        rope_tables()
        for u in units:
            if u == 2:
                prefetch_flush()
            M = mla_mem(u)
            latent(u, M)
            if "mla" in stages:
                for j in range(2):
                    mla_layer(u, j, M)
            if "fin" in stages:
                final_norm(u)

    if "dbg_e" in stages:
        s5_tables(0)
        for gp in range(4):
            eb = ABuf(arena, BIG, [2, 512], BF16)
            dma("sp", eb.ap, EB_S[0, gp], r=["eb_s0"], w=[eb.all()])
            xt = XT[gp % 3].all()
            op("dve", lambda e, xt=xt, eb=eb: e.tensor_copy(xt.ap, eb.ap.rearrange("p r c -> p (r c)")), r=[eb.all()], w=[xt])
            key = "dbg:%d" % gp
            dma("sp", O["y_p"][0][gp * 128:(gp + 1) * 128, :], xt.ap, r=[xt], w=[key])
            fin_keys.append(key)
        el = ABuf(arena, BIG + 4096, [NPAIR, 4], F32)
        dma("sp", el.ap, EL_S[0], r=["el_s0"], w=[el.all()])
        dma("sp", O["y_p"][1][0:128, 0:256], el.ap.rearrange("p a b -> p (a b)"), r=[el.all()], w=["dbg:el"])
        fin_keys.append("dbg:el")

    if "dbg_x" in stages:
        for u in units:
            NT = NTS if u == 2 else SEQ
            rows = min(128, NT)
            src, skey = x_src(u, 2)
            for tr in range(NT // rows):
                xt = XT[tr % 3].all()
                dma("sp", xt.ap[0:rows], src[tr * rows:(tr + 1) * rows, :], r=[skey + ":%d:0" % tr, skey + ":%d:1" % tr], w=[xt])
                key = "y:%d:%d" % (u, tr)
                dst = O["y_p"][u] if u < 2 else O["y_s"]
                dma("sp", dst[tr * rows:(tr + 1) * rows, :], xt.ap[0:rows], r=[xt], w=[key])
                fin_keys.append(key)

    S.emit(final_keys=fin_keys)
    global LAST_S
    LAST_S = S
    return nc


def _consts():
    ident = np.eye(128, dtype=np.float32)
    cst = np.zeros((128, 64), np.float32)
    g8 = (np.arange(128) // 16)
    cst[:, 0] = (g8 % 2 == 0)
    cst[:, 1] = (g8 % 2 == 1)
    cst[:, 2] = -cst[:, 0]
    cst[:, 3] = -cst[:, 1]
    mvec = np.concatenate([32.0 * np.arange(16), 1.0 + np.arange(32)]).astype(np.float32)
    cst[:, 8:56] = mvec[None, :]
    cst[:, 4] = np.arange(128)
    tri = (np.arange(128)[:, None] <= np.arange(128)[None, :]).astype(np.float32)
    pos = np.zeros((128, 17), np.float32)
    for t in range(16):
        pos[:, t] = t * 128 + np.arange(128)
    pos[:, 16] = 8192 + (np.arange(128) % 4)
    invf = (10000.0 ** (-np.arange(32, dtype=np.float32) / 32)).astype(np.float32)
    kk = np.arange(64)
    mnew = ((kk[:, None] // 4 == kk[None, :] // 4) & (kk[:, None] % 4 <= kk[None, :] % 4)).astype(np.float32)
    return dict(ident=ident, cst=cst, tri=tri, pos=pos, invf=invf, mnew=mnew)


_WEIGHT_KEYS = ["norm_a", "w_in_a", "a_re", "a_im", "log_dt", "b_re", "b_im", "c_re", "c_im", "d_skip", "w_glu", "b_glu",
                "w_out_a", "norm_kv", "w_dkv", "norm_latent", "w_uk", "w_uv", "norm_b", "w_in_b", "norm_q", "w_uq",
                "w_out_b", "norm_f"]


def make_in_maps(inputs, cores, npool=NPOOL):
    cs = _consts()
    f = lambda a: np.ascontiguousarray(np.asarray(a))
    cl = f(inputs["cache_latent"]).reshape(npool * 128, KVL)
    ck = f(inputs["cache_krope"]).reshape(npool * 128, ROPE)
    shared = {k: f(inputs[k]) for k in _WEIGHT_KEYS}
    maps = []
    for c in cores:
        m = dict(shared)
        m.update(cs)
        m["xp"] = f(inputs["x_prompt"][NSEQ_C * c:NSEQ_C * (c + 1)])
        m["xs"] = f(inputs["x_sample"][NS_C * c:NS_C * (c + 1)]).reshape(NTS, D)
        m["cl"] = cl
        m["ck"] = ck
        m["pt"] = f(inputs["page_table"][NS_C * c:NS_C * (c + 1)]).astype(np.int32)
        m["sre"] = f(inputs["state_ssm_re"][:, NS_C * c:NS_C * (c + 1)])
        m["sim"] = f(inputs["state_ssm_im"][:, NS_C * c:NS_C * (c + 1)])
        maps.append(m)
    return maps


def assemble(results):
    y_p = np.concatenate([r["y_p"] for r in results], axis=0)
    y_s = np.concatenate([r["y_s"].reshape(NS_C, DSEQ, D) for r in results], axis=0)
    lat_p = np.concatenate([r["lat_p"] for r in results], axis=0)
    kr_p = np.concatenate([r["kr_p"] for r in results], axis=0)
    lat_s = np.concatenate([r["lat_s"].reshape(NS_C, DSEQ, KVL) for r in results], axis=0)
    kr_s = np.concatenate([r["kr_s"].reshape(NS_C, DSEQ, ROPE) for r in results], axis=0)
    hre_p = np.concatenate([r["hre_p"] for r in results], axis=1)
    him_p = np.concatenate([r["him_p"] for r in results], axis=1)
    hre_s = np.concatenate([r["hre_s"] for r in results], axis=1)
    him_s = np.concatenate([r["him_s"] for r in results], axis=1)
    return tuple(np.ascontiguousarray(a, dtype=np.float32) for a in
                 (y_p, y_s, lat_p, kr_p, lat_s, kr_s, hre_p, him_p, hre_s, him_s))


def kernel(**inputs):
    nc = build_program()
    maps = make_in_maps(inputs, list(range(NCORES)))
    res = run_bass_kernel_spmd(nc, maps, core_ids=list(range(NCORES)))
    return assemble(res.results)
```

```python
import contextlib
import math
import numpy as np
import concourse.bass as bass
import concourse.mybir as mybir
from concourse.bass_utils import run_bass_kernel_spmd

F32 = mybir.dt.float32
BF16 = mybir.dt.bfloat16
I32 = mybir.dt.int32
AF = mybir.ActivationFunctionType
ALU = mybir.AluOpType

NCORES = 8
D = 1024
SEQ = 2048
NSEQ_C = 2
NS_C = 16
DSEQ = 4
NTS = NS_C * DSEQ
W = 2048
NPAIR = 64
KVL = 256
ROPE = 64
QL = 384
NH = 8
NPOOL = 10240
NPAGES = 64
EPS = 1e-6
SCALE = 1.0 / math.sqrt(192.0)
TWO_PI = 2.0 * math.pi
GRAN = 256


class Sched:
    NDS = 24

    def __init__(self, nc):
        self.nc = nc
        self.ops = []
        self.stack = contextlib.ExitStack()
        self.last_w = {}
        self.readers = {}
        self.n_dma = 0

    def sb(self, name, shape, dtype):
        return self.stack.enter_context(self.nc.sbuf_tensor(name, list(shape), dtype))

    def ps(self, name, shape, dtype):
        return self.stack.enter_context(self.nc.psum_tensor(name, list(shape), dtype))

    def _add(self, eng, fn, reads, writes, dma):
        idx = len(self.ops)
        deps = set()
        for k in reads:
            d = self.last_w.get(k)
            if d is not None:
                deps.add(d)
        for k in writes:
            d = self.last_w.get(k)
            if d is not None:
                deps.add(d)
            deps.update(self.readers.get(k, ()))
        deps.discard(idx)
        for k in reads:
            self.readers.setdefault(k, []).append(idx)
        for k in writes:
            self.last_w[k] = idx
            self.readers[k] = []
        o = dict(eng=eng, fn=fn, deps=deps, dma=dma)
        if dma:
            o["dma_i"] = self.n_dma
            self.n_dma += 1
        self.ops.append(o)
        return idx

    def emit(self, final_keys=()):
        nc = self.nc
        ops = self.ops
        engs = ["pe", "act", "dve", "pool", "sp"]
        need_sig = [False] * len(ops)
        for i, o in enumerate(ops):
            for d in o["deps"]:
                od = ops[d]
                if od["dma"]:
                    continue
                if od["eng"] == "pe" and o["eng"] == "pe" and not o["dma"]:
                    continue
                need_sig[d] = True
        final_deps = set()
        for k in final_keys:
            if k in self.last_w:
                final_deps.add(self.last_w[k])
        for d in final_deps:
            if not ops[d]["dma"]:
                need_sig[d] = True
        cnt = {e: 0 for e in engs}
        sig_val = [None] * len(ops)
        nds = self.NDS
        dma_uses = [0] * nds
        for i, o in enumerate(ops):
            if o["dma"]:
                s = o["dma_i"] % nds
                dma_uses[s] += 1
                sig_val[i] = ("d", s, 16 * dma_uses[s])
            elif need_sig[i]:
                cnt[o["eng"]] += 1
                sig_val[i] = ("e", o["eng"], cnt[o["eng"]])
        with contextlib.ExitStack() as st:
            esem = {e: st.enter_context(nc.semaphore("es_" + e)) for e in engs}
            dsem = [st.enter_context(nc.semaphore("ds_%d" % j)) for j in range(nds)]
            block = st.enter_context(nc.Block())
            per_eng = {e: [] for e in engs}
            for i, o in enumerate(ops):
                per_eng[o["eng"]].append(i)

            def semof(sv):
                return dsem[sv[1]] if sv[0] == "d" else esem[sv[1]]

            def body(ename):
                def _f(eng):
                    waited = {}
                    for i in per_eng[ename]:
                        o = ops[i]
                        waits = {}
                        for d in o["deps"]:
                            od = ops[d]
                            if (not od["dma"]) and od["eng"] == "pe" and ename == "pe" and not o["dma"]:
                                continue
                            sv = sig_val[d]
                            key = (sv[0], sv[1])
                            if waits.get(key, 0) < sv[2]:
                                waits[key] = sv[2]
                        if o["dma"]:
                            sv = sig_val[i]
                            if sv[2] > 16:
                                key = (sv[0], sv[1])
                                if waits.get(key, 0) < sv[2] - 16:
                                    waits[key] = sv[2] - 16
                        for key, v in waits.items():
                            if waited.get(key, 0) >= v:
                                continue
                            waited[key] = v
                            eng.wait_ge(dsem[key[1]] if key[0] == "d" else esem[key[1]], v)
                        ins = o["fn"](eng)
                        sv = sig_val[i]
                        if sv is not None:
                            ins.then_inc(semof(sv), 16 if sv[0] == "d" else 1)
                    if ename == "sp":
                        for d in sorted(final_deps):
                            sv = sig_val[d]
                            key = (sv[0], sv[1])
                            if waited.get(key, 0) >= sv[2]:
                                continue
                            waited[key] = sv[2]
                            eng.wait_ge(semof(sv), sv[2])
                return _f

            block.tensor(body("pe"))
            block.scalar(body("act"))
            block.vector(body("dve"))
            block.gpsimd(body("pool"))
            block.sync(body("sp"))
        self.stack.close()


class T:
    __slots__ = ("ap", "k")

    def __init__(self, ap, k):
        self.ap = ap
        self.k = k


class ABuf:
    def __init__(self, arena, off, shape, dtype, tag="A"):
        es = 2 if dtype == BF16 else 4
        n = int(np.prod(shape))
        assert off % 4 == 0
        ap = arena[:, off // 2: off // 2 + n * es // 2]
        if dtype != BF16:
            ap = ap.bitcast(dtype)
        if len(shape) == 2:
            ap = ap.rearrange("p (a b) -> p a b", a=shape[0])
        elif len(shape) == 3:
            ap = ap.rearrange("p (a b c) -> p a b c", a=shape[0], b=shape[1])
        elif len(shape) == 4:
            ap = ap.rearrange("p (a b c d) -> p a b c d", a=shape[0], b=shape[1], c=shape[2])
        self.ap = ap
        self.off = off
        self.shape = tuple(shape)
        self.es = es
        self.tag = tag
        self.nbytes = n * es
        self._offs = None

    def keys(self, index=()):
        if self._offs is None:
            self._offs = (np.arange(int(np.prod(self.shape)), dtype=np.int64).reshape(self.shape) * self.es + self.off)
        sel = self._offs[index] if index else self._offs
        sel = np.asarray(sel).ravel()
        g = np.unique(np.concatenate([sel // GRAN, (sel + self.es - 1) // GRAN]))
        return ["%s%d" % (self.tag, int(x)) for x in g]

    def __getitem__(self, index):
        if not isinstance(index, tuple):
            index = (index,)
        return T(self.ap[(slice(None),) + index], self.keys(index))

    def all(self):
        return T(self.ap, self.keys())

    def rows(self, lo, hi, index=()):
        if not isinstance(index, tuple):
            index = (index,)
        return T(self.ap[(slice(lo, hi),) + index], self.keys(index))


def _flat(xs):
    out = []
    for x in xs:
        if isinstance(x, T):
            out.extend(x.k)
        elif isinstance(x, str):
            out.append(x)
        else:
            out.extend(_flat(x))
    return out


def build_program(stages=("s5", "lat", "mla", "fin"), units=(0, 1, 2), npool=NPOOL):
    nc = bass.Bass("TRN2", target_bir_lowering=False)

    def din(name, shape, dt=F32):
        return nc.dram_tensor(name, list(shape), dt, kind="ExternalInput").ap()

    def dout(name, shape, dt=F32):
        return nc.dram_tensor(name, list(shape), dt, kind="ExternalOutput").ap()

    def dscr(name, shape, dt=F32):
        return nc.dram_tensor(name, list(shape), dt, kind="Internal").ap()

    I = {}
    I["xp"] = din("xp", [NSEQ_C, SEQ, D])
    I["xs"] = din("xs", [NTS, D])
    I["cl"] = din("cl", [npool * 128, KVL])
    I["ck"] = din("ck", [npool * 128, ROPE])
    I["pt"] = din("pt", [NS_C, NPAGES], I32)
    I["sre"] = din("sre", [2, NS_C, 128, 64])
    I["sim"] = din("sim", [2, NS_C, 128, 64])
    I["norm_a"] = din("norm_a", [2, D])
    I["w_in_a"] = din("w_in_a", [2, D, 2 * W])
    I["a_re"] = din("a_re", [2, 128, 64])
    I["a_im"] = din("a_im", [2, 128, 64])
    I["log_dt"] = din("log_dt", [2, 128])
    I["b_re"] = din("b_re", [2, 128, 64, 16])
    I["b_im"] = din("b_im", [2, 128, 64, 16])
    I["c_re"] = din("c_re", [2, 128, 16, 64])
    I["c_im"] = din("c_im", [2, 128, 16, 64])
    I["d_skip"] = din("d_skip", [2, W])
    I["w_glu"] = din("w_glu", [2, W, 2 * W])
    I["b_glu"] = din("b_glu", [2, 2 * W])
    I["w_out_a"] = din("w_out_a", [2, W, D])
    I["norm_kv"] = din("norm_kv", [D])
    I["w_dkv"] = din("w_dkv", [D, KVL + ROPE])
    I["norm_latent"] = din("norm_latent", [KVL])
    I["w_uk"] = din("w_uk", [KVL, D])
    I["w_uv"] = din("w_uv", [KVL, D])
    I["norm_b"] = din("norm_b", [2, D])
    I["w_in_b"] = din("w_in_b", [2, D, QL + D])
    I["norm_q"] = din("norm_q", [2, QL])
    I["w_uq"] = din("w_uq", [2, QL, NH * 192])
    I["w_out_b"] = din("w_out_b", [2, D, D])
    I["norm_f"] = din("norm_f", [D])
    I["ident"] = din("ident", [128, 128])
    I["cst"] = din("cst", [128, 64])
    I["tri"] = din("tri", [128, 128])
    I["pos"] = din("pos", [128, 17])
    I["invf"] = din("invf", [32])
    I["mnew"] = din("mnew", [64, 64])

    O = {}
    O["y_p"] = dout("y_p", [NSEQ_C, SEQ, D])
    O["y_s"] = dout("y_s", [NTS, D])
    O["lat_p"] = dout("lat_p", [NSEQ_C, SEQ, KVL])
    O["kr_p"] = dout("kr_p", [NSEQ_C, SEQ, ROPE])
    O["lat_s"] = dout("lat_s", [NTS, KVL])
    O["kr_s"] = dout("kr_s", [NTS, ROPE])
    O["hre_p"] = dout("hre_p", [2, NSEQ_C, 128, 64])
    O["him_p"] = dout("him_p", [2, NSEQ_C, 128, 64])
    O["hre_s"] = dout("hre_s", [2, NS_C, 128, 64])
    O["him_s"] = dout("him_s", [2, NS_C, 128, 64])

    XS = [[dscr("xs_%d_%d" % (u, i), [SEQ if u < 2 else NTS, D]) for i in range(2)] for u in range(3)]
    BT_S = dscr("bt_s", [2, NPAIR, 2, 128, 128], BF16)
    CT_S = dscr("ct_s", [2, NPAIR, 2, 128, 128], BF16)
    HL_S = dscr("hl_s", [2, 128, NPAIR, 2, 48])
    EB_S = dscr("eb_s", [2, NPAIR, 128, 2, 512], BF16)
    EL_S = dscr("el_s", [2, 128, NPAIR, 4])
    PG_L = dscr("pg_l", [NS_C, NPAGES, 128, KVL], BF16)
    PG_K = dscr("pg_k", [NS_C, NPAGES, 128, ROPE], BF16)

    S = Sched(nc)

    def op(eng, fn, r=(), w=()):
        S._add(eng, fn, _flat(r), _flat(w), False)

    def dma(q, out, in_, r=(), w=(), **kw):
        S._add(q, lambda e: e.dma_start(out=out, in_=in_, **kw), _flat(r), _flat(w), True)

    ARENA_BYTES = 188 * 1024
    arena = S.sb("arena", [128, ARENA_BYTES // 2], BF16)
    pbs = [S.ps("pb%d" % i, [128, 512], F32) for i in range(8)]

    def PB(i):
        return T(pbs[i][:, :], ["pb%d" % i])

    def PBb(i):
        return T(pbs[i][:, :].bitcast(BF16), ["pb%d" % i])

    idf = S.sb("idf_sb", [128, 128], F32)
    idb = S.sb("idb", [128, 128], BF16)
    onesb = S.sb("onesb", [128, 128], BF16)
    cst = S.sb("cst_sb", [128, 64], F32)
    trib = S.sb("trib", [128, 128], BF16)
    small = S.sb("small", [128, 2048], F32)
    sm_off = [0]

    def SM(n, name):
        o = sm_off[0]
        sm_off[0] += n
        assert sm_off[0] <= 2048
        return T(small[:, o:o + n], [name])

    dma("sp", idf[:], I["ident"], w=["idf"])
    dma("pool", idb[:], I["ident"], w=["idb"])
    dma("sp", cst[:], I["cst"], w=["cst"])
    dma("pool", trib[:], I["tri"], w=["trib"])
    op("dve", lambda e: e.memset(onesb[:], 1.0), w=["onesb"])
    IDF = T(idf[:], ["idf"])
    IDB = T(idb[:], ["idb"])

    gbuf = S.sb("gbuf", [128, D], F32)
    GB = T(gbuf[:], ["gbuf"])

    A = {}
    off = 0

    def AL(name, shape, dtype, at=None):
        nonlocal off
        o = off if at is None else at
        b = ABuf(arena, o, shape, dtype)
        if at is None:
            off = o + b.nbytes
            off = (off + 255) // 256 * 256
        assert o + b.nbytes <= ARENA_BYTES, (name, o, b.nbytes)
        A[name] = b
        return b

    XT = [AL("xt%d" % i, [D], F32) for i in range(3)]
    HB = [AL("hb%d" % i, [D], BF16) for i in range(2)]
    WREG = off
    off += 24 * 1024
    BIG = off
    BIGSZ = ARENA_BYTES - BIG

    stat = [SM(4, "stat%d" % i) for i in range(3)]
    rr_i = [0]

    def range_reduce(dst, src, shift, tmp_i, tmp_f, tmp_m, eng="dve"):
        if shift != 0.0:
            op(eng, lambda e: e.tensor_scalar(dst.ap, src.ap, float(shift), None, ALU.add), r=[src], w=[dst])
            s2 = dst
        else:
            s2 = src
        op(eng, lambda e: e.tensor_scalar(tmp_i.ap, s2.ap, 1.0 / TWO_PI, None, ALU.mult), r=[s2], w=[tmp_i])
        op(eng, lambda e: e.tensor_copy(tmp_f.ap, tmp_i.ap), r=[tmp_i], w=[tmp_f])
        op(eng, lambda e: e.scalar_tensor_tensor(dst.ap, tmp_f.ap, -TWO_PI, s2.ap, ALU.mult, ALU.add), r=[tmp_f, s2], w=[dst])
        op(eng, lambda e: e.tensor_single_scalar(tmp_m.ap, dst.ap, math.pi, ALU.is_gt), r=[dst], w=[tmp_m])
        op(eng, lambda e: e.scalar_tensor_tensor(dst.ap, tmp_m.ap, -TWO_PI, dst.ap, ALU.mult, ALU.add), r=[tmp_m, dst], w=[dst])
        op(eng, lambda e: e.tensor_single_scalar(tmp_m.ap, dst.ap, -math.pi, ALU.is_lt), r=[dst], w=[tmp_m])
        op(eng, lambda e: e.scalar_tensor_tensor(dst.ap, tmp_m.ap, TWO_PI, dst.ap, ALU.mult, ALU.add), r=[tmp_m, dst], w=[dst])

    def load_gain(vec_ap):
        dma("sp", gbuf[:], vec_ap.partition_broadcast(128), w=[GB])

    def rms_tile(xt, rows, ncols, gain_ap, out_t, gain_T=None):
        st = stat[rr_i[0] % 3]
        rr_i[0] += 1
        xa = xt.ap[0:rows] if rows < 128 else xt.ap
        oa = out_t.ap[0:rows] if rows < 128 else out_t.ap
        sa = st.ap[0:rows] if rows < 128 else st.ap
        ga = gain_ap[0:rows] if rows < 128 else gain_ap
        op("act", lambda e: e.activation(out=oa, in_=xa, func=AF.Square, accum_out=sa[:, 0:1]), r=[xt], w=[out_t, st])
        op("dve", lambda e: e.tensor_scalar(sa[:, 1:2], sa[:, 0:1], 1.0 / ncols, EPS, ALU.mult, ALU.add), r=[st], w=[st])
        op("act", lambda e: e.activation(out=sa[:, 1:2], in_=sa[:, 1:2], func=AF.Sqrt), r=[st], w=[st])
        op("dve", lambda e: e.reciprocal(sa[:, 2:3], sa[:, 1:2]), r=[st], w=[st])
        op("dve", lambda e: e.scalar_tensor_tensor(oa, xa, sa[:, 2:3], ga, ALU.mult, ALU.mult), r=[xt, st, GB if gain_T is None else gain_T], w=[out_t])

    pb_rot = [0]
    cvst = SM(128, "cvst")

    def load_colvec(dst, vec_ap, n):
        dma("sp", cvst.ap[0:n, :], vec_ap.rearrange("(t p) -> t p", p=128), w=[cvst])
        pv = PB(7)
        op("pe", lambda e: e.transpose(pv.ap[:, 0:n], cvst.ap[0:n, :], idf[0:n, 0:n]), r=[cvst, IDF], w=[pv])
        op("dve", lambda e: e.tensor_copy(dst.ap, pv.ap[:, 0:n]), r=[pv], w=[dst])

    def next_pb(cands):
        i = cands[pb_rot[0] % len(cands)]
        pb_rot[0] += 1
        return i

    def x_src(u, li):
        if li == 0:
            return (I["xp"][u] if u < 2 else I["xs"]), "xin%d" % u
        return XS[u][(li - 1) % 2], "xs_%d_%d" % (u, (li - 1) % 2)

    def x_dst(u, li):
        return XS[u][li % 2], "xs_%d_%d" % (u, li % 2)

    def build_hT(u, li, hT, NT, ncols_tile=128, col0=0, ntiles=None, tile0=0, keep_x=None):
        src, skey = x_src(u, li)
        rows = min(128, NT)
        nt = (NT // rows) if ntiles is None else ntiles
        for r in range(nt):
            tr = tile0 + r
            xt = XT[(tr) % 3].all() if keep_x is None else keep_x[r]
            dma("sp", xt.ap[0:rows], src[tr * rows:(tr + 1) * rows, :], r=[skey + ":%d:0" % tr, skey + ":%d:1" % tr], w=[xt])
            hb = HB[tr % 2].all()
            rms_tile(xt, rows, D, gbuf[:], hb)
            pbi = next_pb([0, 1])
            pv = PBb(pbi)
            for kt in range(8):
                op("pe", lambda e, kt=kt, pv=pv, hb=hb: e.transpose(pv.ap[:, kt * 128:kt * 128 + rows], hb.ap[0:rows, kt * 128:(kt + 1) * 128], idb[0:rows, 0:rows]),
                   r=[hb, IDB], w=[pv])
            dst = hT[:, col0 + r * rows: col0 + (r + 1) * rows]
            op("act", lambda e, pv=pv, dst=dst: e.copy(out=dst.ap, in_=pv.ap.rearrange("p (k c) -> p k c", k=8)[:, :, 0:rows]), r=[pv], w=[dst])

    fin_keys = []


    pfl = [S.sb("pfl%d" % i, [128, KVL], BF16) for i in range(2)]
    pfk = [S.sb("pfk%d" % i, [128, ROPE], BF16) for i in range(2)]
    pidx = S.sb("pidx", [128, 4 * NPAGES], I32)
    pptb = S.sb("pptb", [128, 4 * NPAGES], I32)
    PF = {"n": 0, "pending": None, "on": False}

    def prefetch_step():
        if not PF["on"]:
            return
        n = PF["n"]
        if n < NS_C * NPAGES:
            s_, pg = divmod(n, NPAGES)
            if n % (4 * NPAGES) == 0:
                blk = n // (4 * NPAGES)
                dma("sp", pptb[:], I["pt"][blk * 4:(blk + 1) * 4].rearrange("s g -> (s g)").partition_broadcast(128), w=["pptb"])
                op("dve", lambda e: e.tensor_scalar(pidx[:], pptb[:], 128.0, cst[:, 4:5], ALU.mult, ALU.add), r=["pptb", "cst"], w=["pidx"])
            col = n % (4 * NPAGES)
            b = n % 2
            S._add("pool", (lambda e, b=b, col=col: e.indirect_dma_start(
                out=pfl[b][:], out_offset=None, in_=I["cl"], in_offset=bass.IndirectOffsetOnAxis(ap=pidx[:, col:col + 1], axis=0))),
                ["pidx"], ["pfl%d" % b], True)
            S._add("pool", (lambda e, b=b, col=col: e.indirect_dma_start(
                out=pfk[b][:], out_offset=None, in_=I["ck"], in_offset=bass.IndirectOffsetOnAxis(ap=pidx[:, col:col + 1], axis=0))),
                ["pidx"], ["pfk%d" % b], True)
        if PF["pending"] is not None:
            s_, pg, b = PF["pending"]
            dma("sp", PG_L[s_, pg], pfl[b][:], r=["pfl%d" % b], w=["pg_l:%d" % s_])
            dma("sp", PG_K[s_, pg], pfk[b][:], r=["pfk%d" % b], w=["pg_k:%d" % s_])
            PF["pending"] = None
        if n < NS_C * NPAGES:
            PF["pending"] = (n // NPAGES, n % NPAGES, n % 2)
            PF["n"] = n + 1

    def prefetch_flush():
        while PF["on"] and (PF["n"] < NS_C * NPAGES or PF["pending"] is not None):
            prefetch_step()

    def s5_tables(l):
        nonlocal off
        base = BIG
        cur = [base]

        def TL(shape, dtype):
            b = ABuf(arena, cur[0], shape, dtype)
            cur[0] += (b.nbytes + 255) // 256 * 256
            assert cur[0] <= ARENA_BYTES
            return b

        nat = TL([3, 128], F32)
        ld2 = TL([2], F32)
        at = TL([3, 64], F32)
        dt_ = TL([64], F32)
        er = TL([64], F32)
        th = TL([64], F32)
        rr = TL([64], F32)
        t1 = TL([64], F32)
        t2 = TL([64], F32)
        t3 = TL([64], F32)
        ti = TL([64], I32)
        s1 = TL([64], F32)
        c1 = TL([64], F32)
        fr = TL([64], F32)
        fi = TL([64], F32)
        dma("sp", nat.ap[0:64, 0, :], I["a_re"][l].rearrange("(gp g2) p -> gp (g2 p)", g2=2), w=[nat[0]])
        dma("sp", nat.ap[0:64, 1, :], I["a_im"][l].rearrange("(gp g2) p -> gp (g2 p)", g2=2), w=[nat[1]])
        dma("sp", ld2.ap[0:64, :], I["log_dt"][l].rearrange("(gp g2) -> gp g2", g2=2), w=[ld2.all()])
        op("dve", lambda e: e.tensor_copy(nat.ap[0:64, 2, :].rearrange("q (g p) -> q g p", g=2),
                                          ld2.ap[0:64, :].unsqueeze(2).to_broadcast([64, 2, 64])), r=[ld2.all()], w=[nat[2]])
        pv = PB(7)
        for j in range(3):
            op("pe", lambda e, j=j, pv=pv: e.transpose(pv.ap[:, j * 64:(j + 1) * 64], nat.ap[0:64, j, :], idf[0:64, 0:64]), r=[nat[j], IDF], w=[pv])
        op("dve", lambda e, pv=pv: e.tensor_copy(at.ap, pv.ap[:, 0:192].rearrange("p (a b) -> p a b", a=3)), r=[pv], w=[at.all()])
        op("act", lambda e: e.activation(out=dt_.ap, in_=at.ap[:, 2, :], func=AF.Exp), r=[at[2]], w=[dt_.all()])
        op("dve", lambda e: e.tensor_tensor(er.ap, at.ap[:, 0, :], dt_.ap, ALU.mult), r=[at[0], dt_.all()], w=[er.all()])
        op("dve", lambda e: e.tensor_tensor(th.ap, at.ap[:, 1, :], dt_.ap, ALU.mult), r=[at[1], dt_.all()], w=[th.all()])
        op("act", lambda e: e.activation(out=rr.ap, in_=er.ap, func=AF.Exp), r=[er.all()], w=[rr.all()])
        range_reduce(t1.all(), th.all(), 0.0, ti.all(), t2.all(), t3.all())
        op("act", lambda e: e.activation(out=s1.ap, in_=t1.ap, func=AF.Sin), r=[t1.all()], w=[s1.all()])
        range_reduce(t1.all(), th.all(), math.pi / 2, ti.all(), t2.all(), t3.all())
        op("act", lambda e: e.activation(out=c1.ap, in_=t1.ap, func=AF.Sin), r=[t1.all()], w=[c1.all()])
        lbr, lbi = c1, s1
        op("dve", lambda e: e.tensor_tensor(lbr.ap, c1.ap, rr.ap, ALU.mult), r=[c1.all(), rr.all()], w=[lbr.all()])
        op("dve", lambda e: e.tensor_tensor(lbi.ap, s1.ap, rr.ap, ALU.mult), r=[s1.all(), rr.all()], w=[lbi.all()])
        op("dve", lambda e: e.tensor_scalar(lbr.ap, lbr.ap, -1.0, None, ALU.add), r=[lbr.all()], w=[lbr.all()])
        op("dve", lambda e: e.tensor_tensor(t1.ap, at.ap[:, 0, :], at.ap[:, 0, :], ALU.mult), r=[at[0]], w=[t1.all()])
        op("dve", lambda e: e.tensor_tensor(t2.ap, at.ap[:, 1, :], at.ap[:, 1, :], ALU.mult), r=[at[1]], w=[t2.all()])
        op("dve", lambda e: e.tensor_tensor(t1.ap, t1.ap, t2.ap, ALU.add), r=[t1.all(), t2.all()], w=[t1.all()])
        op("dve", lambda e: e.reciprocal(t3.ap, t1.ap), r=[t1.all()], w=[t3.all()])
        op("dve", lambda e: e.tensor_tensor(t1.ap, lbr.ap, at.ap[:, 0, :], ALU.mult), r=[lbr.all(), at[0]], w=[t1.all()])
        op("dve", lambda e: e.tensor_tensor(t2.ap, lbi.ap, at.ap[:, 1, :], ALU.mult), r=[lbi.all(), at[1]], w=[t2.all()])
        op("dve", lambda e: e.tensor_tensor(t1.ap, t1.ap, t2.ap, ALU.add), r=[t1.all(), t2.all()], w=[t1.all()])
        op("dve", lambda e: e.tensor_tensor(fr.ap, t1.ap, t3.ap, ALU.mult), r=[t1.all(), t3.all()], w=[fr.all()])
        op("dve", lambda e: e.tensor_tensor(t1.ap, lbi.ap, at.ap[:, 0, :], ALU.mult), r=[lbi.all(), at[0]], w=[t1.all()])
        op("dve", lambda e: e.tensor_tensor(t2.ap, lbr.ap, at.ap[:, 1, :], ALU.mult), r=[lbr.all(), at[1]], w=[t2.all()])
        op("dve", lambda e: e.tensor_tensor(t1.ap, t1.ap, t2.ap, ALU.subtract), r=[t1.all(), t2.all()], w=[t1.all()])
        op("dve", lambda e: e.tensor_tensor(fi.ap, t1.ap, t3.ap, ALU.mult), r=[t1.all(), t3.all()], w=[fi.all()])
        RRl = RR[l]
        op("dve", lambda e: e.tensor_copy(RRl.ap, rr.ap), r=[rr.all()], w=[RRl])
        NHL = NPAIR * 48
        arg = TL([NPAIR, 48], F32)
        red = TL([NPAIR, 48], F32)
        tf = TL([NPAIR, 48], F32)
        tm = TL([NPAIR, 48], F32)
        tii = TL([NPAIR, 48], I32)
        hl = TL([NPAIR, 2, 48], F32)
        CST = T(cst[:], ["cst"])
        op("dve", lambda e: e.tensor_tensor(arg.ap, th.ap.unsqueeze(2).to_broadcast([128, NPAIR, 48]),
                                            cst[:, 8:56].unsqueeze(1).to_broadcast([128, NPAIR, 48]), ALU.mult), r=[th.all(), CST], w=[arg.all()])
        range_reduce(red.all(), arg.all(), 0.0, tii.all(), tf.all(), tm.all())
        op("act", lambda e: e.activation(out=hl.ap[:, :, 0, :], in_=red.ap, func=AF.Sin), r=[red.all()], w=[hl.all()])
        range_reduce(red.all(), arg.all(), math.pi / 2, tii.all(), tf.all(), tm.all())
        op("act", lambda e: e.activation(out=hl.ap[:, :, 1, :], in_=red.ap, func=AF.Sin), r=[red.all()], w=[hl.all()])
        dma("sp", HL_S[l], hl.ap, r=[hl.all()], w=["hl_s%d" % l])
        cur[0] = arg.off
        bn = TL([2, NPAIR, 16], F32)
        bp = TL([2, NPAIR, 16], F32)
        tb = TL([NPAIR, 16], F32)
        n4 = TL([2, NPAIR, 128], BF16)
        for ri, nm in enumerate(["b_re", "b_im"]):
            v = I[nm][l].rearrange("(gp g2) p c -> g2 p gp c", g2=2)
            for g2 in range(2):
                dma("sp", bn.ap[64 * g2:64 * (g2 + 1), ri], v[g2], w=[bn[ri]])
        frb = fr.ap.unsqueeze(2).to_broadcast([128, NPAIR, 16])
        fib = fi.ap.unsqueeze(2).to_broadcast([128, NPAIR, 16])
        op("dve", lambda e: e.tensor_tensor(bp.ap[:, 0], bn.ap[:, 0], frb, ALU.mult), r=[bn[0], fr.all()], w=[bp[0]])
        op("dve", lambda e: e.tensor_tensor(tb.ap, bn.ap[:, 1], fib, ALU.mult), r=[bn[1], fi.all()], w=[tb.all()])
        op("dve", lambda e: e.tensor_tensor(bp.ap[:, 0], bp.ap[:, 0], tb.ap, ALU.subtract), r=[bp[0], tb.all()], w=[bp[0]])
        op("dve", lambda e: e.tensor_tensor(bp.ap[:, 1], bn.ap[:, 0], fib, ALU.mult), r=[bn[0], fi.all()], w=[bp[1]])
        op("dve", lambda e: e.tensor_tensor(tb.ap, bn.ap[:, 1], frb, ALU.mult), r=[bn[1], fr.all()], w=[tb.all()])
        op("dve", lambda e: e.tensor_tensor(bp.ap[:, 1], bp.ap[:, 1], tb.ap, ALU.add), r=[bp[1], tb.all()], w=[bp[1]])
        op("pool", lambda e: e.memset(n4.ap, 0.0), w=[n4.all()])
        for ri in range(2):
            n4v = n4.ap[:, ri].rearrange("p (t k) c -> p t k c", k=4)
            bpv = bp.ap[:, ri].rearrange("p (t k) c -> p t k c", k=4)
            for k in range(4):
                for g2 in range(2):
                    c0 = 32 * k + 16 * g2
                    op("dve", lambda e, n4v=n4v, bpv=bpv, k=k, g2=g2, c0=c0: e.tensor_copy(
                        n4v[64 * g2:64 * (g2 + 1), :, k, c0:c0 + 16], bpv[64 * g2:64 * (g2 + 1), :, k, :]),
                        r=[bp[ri]], w=[n4[ri]])
        stg = [TL([8, 128], BF16) for _ in range(2)]
        for t8 in range(16):
            pv = PBb(next_pb([2, 3]))
            for k in range(4):
                for ri in range(2):
                    j = k * 2 + ri
                    op("pe", lambda e, pv=pv, j=j, ri=ri, gp=t8 * 4 + k: e.transpose(pv.ap[:, j * 128:(j + 1) * 128], n4.ap[:, ri, gp, :], idb[:]),
                       r=[n4[ri], IDB], w=[pv])
            sg = stg[t8 % 2]
            op("act", lambda e, pv=pv, sg=sg: e.copy(out=sg.ap, in_=pv.ap.rearrange("p (a b) -> p a b", a=8)), r=[pv], w=[sg.all()])
            dma("sp", BT_S[l, t8 * 4:(t8 + 1) * 4].rearrange("k r p c -> p (k r) c"), sg.ap, r=[sg.all()], w=["bt_s%d" % l])
        cur[0] = bn.off
        cn = TL([2, 16, 64], F32)
        mt = TL([2, 16, 128], BF16)
        ctp = [TL([2, 768], BF16) for _ in range(2)]
        for ri, nm in enumerate(["c_re", "c_im"]):
            dma("sp", cn.ap[:, ri], I[nm][l].rearrange("(t g) c p -> (g c) t p", g=8), w=[cn[ri]])
        for ri in range(2):
            op("dve", lambda e, ri=ri: e.tensor_scalar(mt.ap[:, ri, :, 0:64], cn.ap[:, ri], cst[:, 2 * ri:2 * ri + 1], None, ALU.mult), r=[cn[ri], CST], w=[mt[ri]])
            op("dve", lambda e, ri=ri: e.tensor_scalar(mt.ap[:, ri, :, 64:128], cn.ap[:, ri], cst[:, 2 * ri + 1:2 * ri + 2], None, ALU.mult), r=[cn[ri], CST], w=[mt[ri]])
        for c in ctp:
            op("pool", lambda e, c=c: e.memset(c.ap, 0.0), w=[c.all()])
        for t8 in range(16):
            pv = PBb(next_pb([2, 3]))
            for ri in range(2):
                op("pe", lambda e, pv=pv, ri=ri, t8=t8: e.transpose(pv.ap[:, ri * 128:(ri + 1) * 128], mt.ap[:, ri, t8, :], idb[:]), r=[mt[ri], IDB], w=[pv])
            c = ctp[t8 % 2]
            for ri in range(2):
                op("act", lambda e, pv=pv, ri=ri, c=c: e.copy(out=c.ap[:, ri, :].rearrange("p (k s) -> p k s", s=192)[:, :, 0:32],
                                                             in_=pv.ap[:, ri * 128:(ri + 1) * 128].rearrange("p (k j) -> p k j", k=4)), r=[pv], w=[c.all()])
            for ri in range(2):
                dma("sp", CT_S[l, t8 * 4:(t8 + 1) * 4, ri].rearrange("k p c -> p k c"),
                    c.ap[:, ri, 0:640].rearrange("p (k s) -> p k s", s=160)[:, :, 0:128], r=[c.all()], w=["ct_s%d" % l])

        cur[0] = hl.off + (hl.nbytes + 255) // 256 * 256
        ef = [[TL([512], F32) for _ in range(2)] for _ in range(2)]
        tab2 = [TL([512], F32) for _ in range(2)]
        ebs = [TL([2, 512], BF16) for _ in range(2)]
        elt = TL([NPAIR, 4], F32)
        for gp in range(NPAIR):
            er_, ei_ = ef[gp % 2][0].all(), ef[gp % 2][1].all()
            ta_, tb_ = tab2[0].all(), tab2[1].all()
            v3 = lambda t: t.ap.rearrange("p (a b) -> p a b", a=16)
            hi = lambda sc, gp=gp: hl.ap[:, gp, sc, 0:16].unsqueeze(2).to_broadcast([128, 16, 32])
            lo = lambda sc, gp=gp: hl.ap[:, gp, sc, 16:48].unsqueeze(1).to_broadcast([128, 16, 32])
            HLT = hl.all()
            op("dve", lambda e, hi=hi, lo=lo, ta_=ta_: e.tensor_tensor(v3(ta_), hi(1), lo(1), ALU.mult), r=[HLT], w=[ta_])
            op("dve", lambda e, hi=hi, lo=lo, tb_=tb_: e.tensor_tensor(v3(tb_), hi(0), lo(0), ALU.mult), r=[HLT], w=[tb_])
            op("dve", lambda e, er_=er_, ta_=ta_, tb_=tb_: e.tensor_tensor(er_.ap, ta_.ap, tb_.ap, ALU.subtract), r=[ta_, tb_], w=[er_])
            op("dve", lambda e, hi=hi, lo=lo, ta_=ta_: e.tensor_tensor(v3(ta_), hi(0), lo(1), ALU.mult), r=[HLT, er_], w=[ta_])
            op("dve", lambda e, hi=hi, lo=lo, tb_=tb_: e.tensor_tensor(v3(tb_), hi(1), lo(0), ALU.mult), r=[HLT, er_], w=[tb_])
            op("dve", lambda e, ei_=ei_, ta_=ta_, tb_=tb_: e.tensor_tensor(ei_.ap, ta_.ap, tb_.ap, ALU.add), r=[ta_, tb_], w=[ei_])
            eb_ = ebs[gp % 2]
            op("act", lambda e, eb_=eb_, er_=er_: e.copy(out=eb_.ap[:, 0, :], in_=er_.ap), r=[er_], w=[eb_.all()])
            op("act", lambda e, eb_=eb_, ei_=ei_: e.copy(out=eb_.ap[:, 1, :], in_=ei_.ap), r=[ei_], w=[eb_.all()])
            EL_ = elt.all()
            op("act", lambda e, gp=gp, er_=er_: e.copy(out=elt.ap[:, gp, 0:1], in_=er_.ap[:, 511:512]), r=[er_], w=[EL_])
            op("act", lambda e, gp=gp, ei_=ei_: e.copy(out=elt.ap[:, gp, 1:2], in_=ei_.ap[:, 511:512]), r=[ei_], w=[EL_])
            op("act", lambda e, gp=gp, ei_=ei_: e.mul(elt.ap[:, gp, 2:3], ei_.ap[:, 511:512], -1.0), r=[ei_], w=[EL_])
            dma("sp", EB_S[l, gp], eb_.ap, r=[eb_.all()], w=["eb_s%d" % l])
        dma("sp", EL_S[l], elt.ap, r=[elt.all()], w=["el_s%d" % l])

    RR = [SM(64, "rr%d" % l) for l in range(2)]
    DSK = [SM(16, "dsk%d" % l) for l in range(2)]
    BGL = [SM(32, "bgl%d" % l) for l in range(2)]
    CR = SM(64, "cr")
    CI = SM(64, "ci")


    def s5_step2_prompt(u, l, hT, gy, base):
        NT, CH, NCH, CW = SEQ, 512, SEQ // 512, 512
        cur = [base]

        def TL(shape, dtype):
            b = ABuf(arena, cur[0], shape, dtype)
            cur[0] += (b.nbytes + 255) // 256 * 256
            assert cur[0] <= ARENA_BYTES, cur[0]
            return b
        xreg = [XT[0].off]

        def TX(shape, dtype):
            b = ABuf(arena, xreg[0], shape, dtype)
            xreg[0] += (b.nbytes + 255) // 256 * 256
            assert xreg[0] <= WREG, xreg[0]
            return b
        bub = [[TL([CW], BF16) for _ in range(2)] for _ in range(2)]
        prod = [[TL([CW], BF16) for _ in range(4)] for _ in range(3)]
        wbf = [[TL([CW], BF16) for _ in range(2)] for _ in range(2)]
        wf = [[TL([CW], F32) for _ in range(2)] for _ in range(2)]
        wb = [[TL([CW], BF16) for _ in range(2)] for _ in range(2)]
        xb = [[TL([CW], BF16) for _ in range(2)] for _ in range(2)]
        Ebs = [TL([4, 2, CW], BF16), TX([4, 2, CW], BF16)]
        Els = [TX([4, 4], F32) for _ in range(2)]
        ctm = [TL([2], F32) for _ in range(2)]
        yt = [TL([CH], F32) for _ in range(2)]
        uT = [TL([CH], BF16) for _ in range(3)]
        wreg = [ABuf(arena, WREG + i * 2048, [8, 128], BF16) for i in range(3)]
        btb = [ABuf(arena, WREG + 6144 + i * 2048, [8, 128], BF16) for i in range(2)]
        ctb = [ABuf(arena, WREG + 10240 + i * 2048, [8, 128], BF16) for i in range(2)]
        RRl = RR[l]
        DS = DSK[l]
        CRk = [T(CR.ap, ["cr%d" % kk]) for kk in range(4)]
        CIk = [T(CI.ap, ["ci%d" % kk]) for kk in range(4)]

        def load_tile_weights(t8):
            wt = wreg[t8 % 3]
            dma("pool", wt.ap, I["w_in_a"][l][:, t8 * 128:(t8 + 1) * 128].rearrange("(k p) c -> p k c", p=128), w=[wt.all()])
            bt = btb[t8 % 2]
            dma("sp", bt.ap, BT_S[l, t8 * 4:(t8 + 1) * 4].rearrange("k r p c -> p (k r) c"), r=["bt_s%d" % l], w=[bt.all()])
            ct = ctb[t8 % 2]
            dma("sp", ct.ap, CT_S[l, t8 * 4:(t8 + 1) * 4].rearrange("k r p c -> p (k r) c"), r=["ct_s%d" % l], w=[ct.all()])
            eb = Ebs[t8 % 2]
            dma("sp", eb.ap.rearrange("p k r c -> p k (r c)"), EB_S[l, t8 * 4:(t8 + 1) * 4].rearrange("k p r c -> p k (r c)"), r=["eb_s%d" % l], w=[eb.all()])
            el = Els[t8 % 2]
            dma("sp", el.ap, EL_S[l][:, t8 * 4:(t8 + 1) * 4], r=["el_s%d" % l], w=[el.all()])

        def stageA(it):
            k, uc, st, bt = it["k"], it["uc"], it["st"], it["bt"]
            pr, pi = PB(2 + st), PB(4 + st)
            op("pe", lambda e, pr=pr, k=k, uc=uc, bt=bt: e.matmul(pr.ap, bt.ap[:, 2 * k, :], uc.ap, start=True, stop=True), r=[bt.all(), uc], w=[pr])
            op("pe", lambda e, pi=pi, k=k, uc=uc, bt=bt: e.matmul(pi.ap, bt.ap[:, 2 * k + 1, :], uc.ap, start=True, stop=True), r=[bt.all(), uc], w=[pi])
            br, bi = bub[st][0].all(), bub[st][1].all()
            op("act", lambda e, br=br, pr=pr: e.copy(out=br.ap, in_=pr.ap), r=[pr], w=[br])
            op("act", lambda e, bi=bi, pi=pi: e.copy(out=bi.ap, in_=pi.ap), r=[pi], w=[bi])

        def stageB(it):
            q, k, st, s3, t8, Eb, El = it["q"], it["k"], it["st"], it["s3"], it["t8"], it["Eb"], it["El"]
            gp = t8 * 4 + k
            br, bi = bub[st][0].all(), bub[st][1].all()
            m1, m2, m3, m4 = [x.all() for x in prod[s3]]
            wri, wii = wbf[st][0].all(), wbf[st][1].all()
            wr, wi = wf[st][0].all(), wf[st][1].all()
            wbr, wbi = wb[st][0].all(), wb[st][1].all()
            Er, Ei = Eb[k, 0], Eb[k, 1]
            op("dve", lambda e, a=m1, Er=Er, br=br: e.tensor_tensor(a.ap, Er.ap, br.ap, ALU.mult), r=[Er, br], w=[m1])
            op("dve", lambda e, a=m2, Ei=Ei, bi=bi: e.tensor_tensor(a.ap, Ei.ap, bi.ap, ALU.mult), r=[Ei, bi], w=[m2])
            op("dve", lambda e, a=m3, Er=Er, bi=bi: e.tensor_tensor(a.ap, Er.ap, bi.ap, ALU.mult), r=[Er, bi], w=[m3])
            op("dve", lambda e, a=m4, Ei=Ei, br=br: e.tensor_tensor(a.ap, Ei.ap, br.ap, ALU.mult), r=[Ei, br], w=[m4])
            op("dve", lambda e, o=wri, a=m1, b=m2: e.tensor_tensor(o.ap, a.ap, b.ap, ALU.add), r=[m1, m2], w=[wri])
            op("dve", lambda e, o=wii, a=m3, b=m4: e.tensor_tensor(o.ap, a.ap, b.ap, ALU.subtract), r=[m3, m4], w=[wii])
            d0 = RRl.ap[:, gp:gp + 1].to_broadcast([128, CW])
            ini_r = 0.0 if q == 0 else CR.ap[:, gp:gp + 1]
            ini_i = 0.0 if q == 0 else CI.ap[:, gp:gp + 1]
            rdeps = [RRl] + ([] if q == 0 else [CRk[k], CIk[k]])
            op("dve", lambda e, wr=wr, wri=wri, d0=d0, ini=ini_r: e.tensor_tensor_scan(wr.ap, d0, wri.ap, ini, ALU.mult, ALU.add), r=[wri] + rdeps, w=[wr])
            op("dve", lambda e, wi=wi, wii=wii, d0=d0, ini=ini_i: e.tensor_tensor_scan(wi.ap, d0, wii.ap, ini, ALU.mult, ALU.add), r=[wii] + rdeps, w=[wi])
            op("act", lambda e, wbr=wbr, wr=wr: e.copy(out=wbr.ap, in_=wr.ap), r=[wr], w=[wbr])
            op("act", lambda e, wbi=wbi, wi=wi: e.copy(out=wbi.ap, in_=wi.ap), r=[wi], w=[wbi])
            elk = El[k]
            c_ = ctm[st].all()
            L = CW - 1
            op("act", lambda e, c_=c_, wi=wi, elk=elk: e.activation(out=c_.ap[:, 0:1], in_=wi.ap[:, L:L + 1], func=AF.Copy, scale=elk.ap[:, 2:3]), r=[wi, elk], w=[c_])
            op("act", lambda e, c_=c_, wr=wr, elk=elk: e.activation(out=c_.ap[:, 1:2], in_=wr.ap[:, L:L + 1], func=AF.Copy, scale=elk.ap[:, 1:2]), r=[wr, elk, c_], w=[c_])
            op("act", lambda e, c_=c_, wr=wr, elk=elk, gp=gp: e.activation(out=CR.ap[:, gp:gp + 1], in_=wr.ap[:, L:L + 1], func=AF.Identity, bias=c_.ap[:, 0:1], scale=elk.ap[:, 0:1]), r=[wr, elk, c_], w=[CRk[k]])
            op("act", lambda e, c_=c_, wi=wi, elk=elk, gp=gp: e.activation(out=CI.ap[:, gp:gp + 1], in_=wi.ap[:, L:L + 1], func=AF.Identity, bias=c_.ap[:, 1:2], scale=elk.ap[:, 0:1]), r=[wi, elk, c_], w=[CIk[k]])

        def stageC(it):
            q, k, uc, st, s3, pacc, t8, Eb, ct = it["q"], it["k"], it["uc"], it["st"], it["s3"], it["pacc"], it["t8"], it["Eb"], it["ct"]
            d1, d2, d3, d4 = [x.all() for x in prod[s3]]
            wbr, wbi = wb[st][0].all(), wb[st][1].all()
            xr, xi = xb[st][0].all(), xb[st][1].all()
            Er, Ei = Eb[k, 0], Eb[k, 1]
            op("dve", lambda e, a=d1, Er=Er, w_=wbr: e.tensor_tensor(a.ap, Er.ap, w_.ap, ALU.mult), r=[Er, wbr], w=[d1])
            op("dve", lambda e, a=d2, Ei=Ei, w_=wbi: e.tensor_tensor(a.ap, Ei.ap, w_.ap, ALU.mult), r=[Ei, wbi], w=[d2])
            op("dve", lambda e, a=d3, Er=Er, w_=wbi: e.tensor_tensor(a.ap, Er.ap, w_.ap, ALU.mult), r=[Er, wbi], w=[d3])
            op("dve", lambda e, a=d4, Ei=Ei, w_=wbr: e.tensor_tensor(a.ap, Ei.ap, w_.ap, ALU.mult), r=[Ei, wbr], w=[d4])
            op("dve", lambda e, xr=xr, a=d1, b=d2: e.tensor_tensor(xr.ap, a.ap, b.ap, ALU.subtract), r=[d1, d2], w=[xr])
            op("dve", lambda e, xi=xi, a=d3, b=d4: e.tensor_tensor(xi.ap, a.ap, b.ap, ALU.add), r=[d3, d4], w=[xi])
            op("pe", lambda e, pacc=pacc, k=k, xr=xr, ct=ct: e.matmul(pacc.ap, ct.ap[:, 2 * k, :], xr.ap, start=(k == 0), stop=False), r=[ct.all(), xr], w=[pacc])
            op("pe", lambda e, pacc=pacc, k=k, xi=xi, ct=ct: e.matmul(pacc.ap, ct.ap[:, 2 * k + 1, :], xi.ap, start=False, stop=(k == 3)), r=[ct.all(), xi], w=[pacc])
            if k == 3:
                y_ = yt[q % 2].all()
                op("dve", lambda e, y_=y_, uc=uc, pacc=pacc, t8=t8: e.scalar_tensor_tensor(y_.ap, uc.ap, DS.ap[:, t8:t8 + 1], pacc.ap, ALU.mult, ALU.add), r=[uc, DS, pacc], w=[y_])
                gdst = gy[t8, slice(q * CH, (q + 1) * CH)]
                op("act", lambda e, gdst=gdst, y_=y_: e.activation(out=gdst.ap, in_=y_.ap, func=AF.Gelu), r=[y_], w=[gdst])

        items = []
        rot = 0
        for t8 in range(16):
            for q in range(NCH):
                for k in range(4):
                    items.append(dict(t8=t8, q=q, k=k, st=rot % 2, s3=rot % 3, bt=btb[t8 % 2], ct=ctb[t8 % 2], Eb=Ebs[t8 % 2], El=Els[t8 % 2]))
                    rot += 1
        ucs, paccs = {}, {}

        def prep_chunk(t8, q):
            wt = wreg[t8 % 3]
            cs = slice(q * CH, (q + 1) * CH)
            pu = PB(next_pb([0, 1]))
            for kt in range(8):
                op("pe", lambda e, pu=pu, kt=kt, wt=wt, cs=cs: e.matmul(pu.ap, wt.ap[:, kt, :], hT.ap[:, kt, cs], start=(kt == 0), stop=(kt == 7)),
                   r=[wt.all(), hT[kt, cs]], w=[pu])
            uc = uT[(t8 * NCH + q) % 3].all()
            op("act", lambda e, pu=pu, uc=uc: e.copy(out=uc.ap, in_=pu.ap), r=[pu], w=[uc])
            ucs[(t8, q)] = uc
            paccs[(t8, q)] = PB(6 + (t8 * NCH + q) % 2)

        load_tile_weights(0)
        n = len(items)
        for i in range(n + 2):
            if i < n:
                it = items[i]
                if it["k"] == 0 and it["q"] == 2 and it["t8"] + 1 < 16:
                    load_tile_weights(it["t8"] + 1)
                if it["k"] == 0:
                    prep_chunk(it["t8"], it["q"])
                it["uc"] = ucs[(it["t8"], it["q"])]
                it["pacc"] = paccs[(it["t8"], it["q"])]
                stageA(it)
            if 1 <= i <= n:
                stageB(items[i - 1])
            if i >= 2:
                stageC(items[i - 2])
            prefetch_step()
        for ri, (Ct, oname) in enumerate(((CR, "hre_p"), (CI, "him_p"))):
            pv = PB(next_pb([2, 3]))
            Cta = T(Ct.ap, ["cr%d" % kk for kk in range(4)] if ri == 0 else ["ci%d" % kk for kk in range(4)])
            op("pe", lambda e, pv=pv, Ct=Ct: e.transpose(pv.ap[0:64, 0:128], Ct.ap, idf[:]), r=[Cta, IDF], w=[pv])
            so = wf[0][ri].all()
            op("dve", lambda e, pv=pv, so=so: e.tensor_copy(so.ap[0:64, 0:128], pv.ap[0:64, 0:128]), r=[pv], w=[so])
            key = "%s:%d:%d" % (oname, l, u)
            dma("sp", O[oname][l, u].rearrange("(gp g2) p -> gp (g2 p)", g2=2), so.ap[0:64, 0:128], r=[so], w=[key])
            fin_keys.append(key)

    def s5_layer(u, l):
        sample = (u == 2)
        NT = NTS if sample else SEQ
        CH = 64 if sample else 512
        NCH = NT // CH
        CW = 80 if sample else 512
        cur = [BIG]

        def TL(shape, dtype):
            b = ABuf(arena, cur[0], shape, dtype)
            cur[0] += (b.nbytes + 255) // 256 * 256
            assert cur[0] <= ARENA_BYTES, cur[0]
            return b

        hT = TL([8, NT], BF16)
        gy = TL([16, NT], BF16)
        w2 = cur[0]
        load_gain(I["norm_a"][l])
        load_colvec(DSK[l], I["d_skip"][l], 16)
        load_colvec(BGL[l], I["b_glu"][l], 32)
        build_hT(u, l, hT, NT)
        if not sample:
            s5_step2_prompt(u, l, hT, gy, cur[0])
        if sample:
            E = TL([4, 2, CW], F32)
            uT = [TL([CH], BF16) for _ in range(3)]
            tmp = [[TL([CW], F32) for _ in range(4)] for _ in range(2)]
            tmp.append([ABuf(arena, XT[0].off + i * 2048, [CW], F32) for i in range(4)])
            wv = [[TL([CW], F32) for _ in range(2)] for _ in range(2)]
            xb = [[TL([CW], BF16) for _ in range(2)] for _ in range(2)]
            yt = [TL([CH], F32) for _ in range(2)]
            hlb = [TL([4, 2, 48], F32) for _ in range(2)]
            if sample:
                rs = TL([NS_C, 5], F32)
                h0 = TL([2, NPAIR, NS_C], F32)
                xsf = TL([2, NPAIR, NS_C], F32)
                snat = [TL([128], F32) for _ in range(2)]
                op("dve", lambda e: e.memset(rs.ap, 0.0), w=[rs.all()])
                op("dve", lambda e: e.memset(E.ap[:, :, 0, :], 1.0), w=[E.all()])
                op("dve", lambda e: e.memset(E.ap[:, :, 1, :], 0.0), w=[E.all()])
                for ri, nm in enumerate(["sre", "sim"]):
                    for s in range(NS_C):
                        sn = snat[(ri * NS_C + s) % 2]
                        dma("sp", sn.ap[0:64, :], I[nm][l, s].rearrange("(gp g2) p -> gp (g2 p)", g2=2), w=[sn.all()])
                        pv = PB(next_pb([2, 3]))
                        op("pe", lambda e, pv=pv, sn=sn: e.transpose(pv.ap[:, 0:64], sn.ap[0:64, :], idf[0:64, 0:64]), r=[sn.all(), IDF], w=[pv])
                        op("dve", lambda e, pv=pv, ri=ri, s=s: e.tensor_copy(h0.ap[:, ri, :, s], pv.ap[:, 0:64]), r=[pv], w=[h0[ri]])
            wreg = [ABuf(arena, WREG + i * 2048, [8, 128], BF16) for i in range(3)]
            btb = [ABuf(arena, WREG + 6144 + i * 2048, [8, 128], BF16) for i in range(2)]
            ctb = [ABuf(arena, WREG + 10240 + i * 2048, [8, 128], BF16) for i in range(2)]
            wk = "w_in_a"

            def load_tile_weights(t8):
                wt = wreg[t8 % 3]
                dma("pool", wt.ap, I["w_in_a"][l][:, t8 * 128:(t8 + 1) * 128].rearrange("(k p) c -> p k c", p=128), w=[wt.all()])
                bt = btb[t8 % 2]
                dma("sp", bt.ap, BT_S[l, t8 * 4:(t8 + 1) * 4].rearrange("k r p c -> p (k r) c"), r=["bt_s%d" % l], w=[bt.all()])
                ct = ctb[t8 % 2]
                dma("sp", ct.ap, CT_S[l, t8 * 4:(t8 + 1) * 4].rearrange("k r p c -> p (k r) c"), r=["ct_s%d" % l], w=[ct.all()])
                hb_ = hlb[t8 % 2]
                dma("sp", hb_.ap, HL_S[l][:, t8 * 4:(t8 + 1) * 4], r=["hl_s%d" % l], w=[hb_.all()])

            load_tile_weights(0)
            rot = [0]
            for t8 in range(16):
                if t8 + 1 < 16:
                    load_tile_weights(t8 + 1)
                wt = wreg[t8 % 3]
                bt = btb[t8 % 2]
                ct = ctb[t8 % 2]
                hl = hlb[t8 % 2]
                ut = uT[t8 % 2]
                for k in range(4):
                    if not sample:
                        ev = lambda ri, k=k: E.ap[:, k, ri, :].rearrange("p (a b) -> p a b", a=16)
                        hi = lambda sc, k=k, hl=hl: hl.ap[:, k, sc, 0:16].unsqueeze(2).to_broadcast([128, 16, 32])
                        lo = lambda sc, k=k, hl=hl: hl.ap[:, k, sc, 16:48].unsqueeze(1).to_broadcast([128, 16, 32])
                        ta = tmp[0][0].ap.rearrange("p (a b) -> p a b", a=16)
                        tb_ = tmp[0][1].ap.rearrange("p (a b) -> p a b", a=16)
                        Tta, Ttb = tmp[0][0].all(), tmp[0][1].all()
                        op("dve", lambda e, hi=hi, lo=lo, ta=ta: e.tensor_tensor(ta, hi(1), lo(1), ALU.mult), r=[hl.all()], w=[Tta])
                        op("dve", lambda e, hi=hi, lo=lo, tb_=tb_: e.tensor_tensor(tb_, hi(0), lo(0), ALU.mult), r=[hl.all()], w=[Ttb])
                        op("dve", lambda e, ev=ev, ta=ta, tb_=tb_: e.tensor_tensor(ev(0), ta, tb_, ALU.subtract), r=[Tta, Ttb], w=[E[k, 0]])
                        op("dve", lambda e, hi=hi, lo=lo, ta=ta: e.tensor_tensor(ta, hi(0), lo(1), ALU.mult), r=[hl.all()], w=[Tta])
                        op("dve", lambda e, hi=hi, lo=lo, tb_=tb_: e.tensor_tensor(tb_, hi(1), lo(0), ALU.mult), r=[hl.all()], w=[Ttb])
                        op("dve", lambda e, ev=ev, ta=ta, tb_=tb_: e.tensor_tensor(ev(1), ta, tb_, ALU.add), r=[Tta, Ttb], w=[E[k, 1]])
                    else:
                        for ri, sc in ((0, 1), (1, 0)):
                            op("dve", lambda e, k=k, ri=ri, sc=sc, hl=hl: e.tensor_copy(
                                E.ap[:, k, ri, :].rearrange("p (s t) -> p s t", t=5)[:, :, 1:5],
                                hl.ap[:, k, sc, 16:20].unsqueeze(1).to_broadcast([128, NS_C, 4])), r=[hl.all()], w=[E[k, ri]])
                RRl = RR[l]
                DS = DSK[l]
                if sample:
                    v5 = lambda t: t.ap.rearrange("p (s t) -> p s t", t=5)[:, :, 1:5]
                    p4 = lambda p: p.ap[:, 0:CH].rearrange("p (s t) -> p s t", t=4)
                    s0 = lambda t: t.ap.rearrange("p (s t) -> p s t", t=5)[:, :, 0]
                    l5 = lambda t: t.ap.rearrange("p (s t) -> p s t", t=5)[:, :, 4]
                else:
                    v5 = lambda t: t.ap
                    p4 = lambda p: p.ap

                def stage1(it):
                    q, k, uc, st = it["q"], it["k"], it["uc"], it["st"]
                    gp = t8 * 4 + k
                    pr = PB(2 + st)
                    pi = PB(4 + st)
                    op("pe", lambda e, pr=pr, k=k, uc=uc, bt=bt: e.matmul(pr.ap[:, 0:CH], bt.ap[:, 2 * k, :], uc.ap, start=True, stop=True), r=[bt.all(), uc], w=[pr])
                    op("pe", lambda e, pi=pi, k=k, uc=uc, bt=bt: e.matmul(pi.ap[:, 0:CH], bt.ap[:, 2 * k + 1, :], uc.ap, start=True, stop=True), r=[bt.all(), uc], w=[pi])
                    t1_, t2_, t3_, t4_ = [x.all() for x in tmp[it["s3"]]]
                    wri, wii = [x.all() for x in wv[st]]
                    Er, Ei = E[k, 0], E[k, 1]
                    op("dve", lambda e, a=t1_, Er=Er, pr=pr: e.tensor_tensor(v5(a), v5(Er), p4(pr), ALU.mult), r=[Er, pr], w=[t1_])
                    op("dve", lambda e, a=t2_, Ei=Ei, pi=pi: e.tensor_tensor(v5(a), v5(Ei), p4(pi), ALU.mult), r=[Ei, pi], w=[t2_])
                    op("dve", lambda e, a=t3_, Er=Er, pi=pi: e.tensor_tensor(v5(a), v5(Er), p4(pi), ALU.mult), r=[Er, pi], w=[t3_])
                    op("dve", lambda e, a=t4_, Ei=Ei, pr=pr: e.tensor_tensor(v5(a), v5(Ei), p4(pr), ALU.mult), r=[Ei, pr], w=[t4_])
                    op("dve", lambda e, o=wri, a=t1_, b=t2_: e.tensor_tensor(v5(o), v5(a), v5(b), ALU.add), r=[t1_, t2_], w=[wri])
                    op("dve", lambda e, o=wii, a=t3_, b=t4_: e.tensor_tensor(v5(o), v5(a), v5(b), ALU.subtract), r=[t3_, t4_], w=[wii])
                    if sample:
                        op("dve", lambda e, o=wri, gp=gp: e.tensor_copy(s0(o), h0.ap[:, 0, gp, :]), r=[h0[0]], w=[wri])
                        op("dve", lambda e, o=wii, gp=gp: e.tensor_copy(s0(o), h0.ap[:, 1, gp, :]), r=[h0[1]], w=[wii])

                def stage2(it):
                    q, k, uc, st, pacc = it["q"], it["k"], it["uc"], it["st"], it["pacc"]
                    gp = t8 * 4 + k
                    t1_, t2_, t3_, t4_ = [x.all() for x in tmp[it["s3"]]]
                    wri, wii = [x.all() for x in wv[st]]
                    wr, wi = wri, wii
                    xr, xi = [x.all() for x in xb[st]]
                    Er, Ei = E[k, 0], E[k, 1]
                    if sample:
                        op("dve", lambda e, gp=gp: e.tensor_copy(rs.ap[:, :, 1:5], RRl.ap[:, gp:gp + 1].unsqueeze(2).to_broadcast([128, NS_C, 4])), r=[RRl], w=[rs.all()])
                        d0 = rs.ap.rearrange("p s t -> p (s t)")
                        ini_r = ini_i = 0.0
                        rdeps = [rs.all()]
                    else:
                        d0 = RRl.ap[:, gp:gp + 1].to_broadcast([128, CW])
                        ini_r = 0.0 if q == 0 else CR.ap[:, gp:gp + 1]
                        ini_i = 0.0 if q == 0 else CI.ap[:, gp:gp + 1]
                        rdeps = [RRl] + ([] if q == 0 else [CRk[k], CIk[k]])
                    op("dve", lambda e, wr=wr, wri=wri, d0=d0, ini=ini_r: e.tensor_tensor_scan(wr.ap, d0, wri.ap, ini, ALU.mult, ALU.add), r=[wri] + rdeps, w=[wr])
                    op("dve", lambda e, wi=wi, wii=wii, d0=d0, ini=ini_i: e.tensor_tensor_scan(wi.ap, d0, wii.ap, ini, ALU.mult, ALU.add), r=[wii] + rdeps, w=[wi])
                    op("dve", lambda e, a=t1_, Er=Er, wr=wr: e.tensor_tensor(a.ap, Er.ap, wr.ap, ALU.mult), r=[Er, wr], w=[t1_])
                    op("dve", lambda e, a=t2_, Ei=Ei, wi=wi: e.tensor_tensor(a.ap, Ei.ap, wi.ap, ALU.mult), r=[Ei, wi], w=[t2_])
                    op("dve", lambda e, a=t3_, Er=Er, wi=wi: e.tensor_tensor(a.ap, Er.ap, wi.ap, ALU.mult), r=[Er, wi], w=[t3_])
                    op("dve", lambda e, a=t4_, Ei=Ei, wr=wr: e.tensor_tensor(a.ap, Ei.ap, wr.ap, ALU.mult), r=[Ei, wr], w=[t4_])
                    op("dve", lambda e, xr=xr, a=t1_, b=t2_: e.tensor_tensor(xr.ap, a.ap, b.ap, ALU.subtract), r=[t1_, t2_], w=[xr])
                    op("dve", lambda e, xi=xi, a=t3_, b=t4_: e.tensor_tensor(xi.ap, a.ap, b.ap, ALU.add), r=[t3_, t4_], w=[xi])
                    if sample:
                        op("dve", lambda e, gp=gp, a=t1_, b=t2_: e.tensor_tensor(xsf.ap[:, 0, gp, :], l5(a), l5(b), ALU.subtract), r=[t1_, t2_], w=[xsf[0]])
                        op("dve", lambda e, gp=gp, a=t3_, b=t4_: e.tensor_tensor(xsf.ap[:, 1, gp, :], l5(a), l5(b), ALU.add), r=[t3_, t4_], w=[xsf[1]])
                    else:
                        op("dve", lambda e, gp=gp, a=t1_, b=t2_: e.tensor_tensor(CR.ap[:, gp:gp + 1], a.ap[:, CW - 1:CW], b.ap[:, CW - 1:CW], ALU.subtract), r=[t1_, t2_], w=[CRk[k]])
                        op("dve", lambda e, gp=gp, a=t3_, b=t4_: e.tensor_tensor(CI.ap[:, gp:gp + 1], a.ap[:, CW - 1:CW], b.ap[:, CW - 1:CW], ALU.add), r=[t3_, t4_], w=[CIk[k]])
                    op("pe", lambda e, pacc=pacc, k=k, xr=xr, ct=ct: e.matmul(pacc.ap[:, 0:CW], ct.ap[:, 2 * k, :], xr.ap, start=(k == 0), stop=False), r=[ct.all(), xr], w=[pacc])
                    op("pe", lambda e, pacc=pacc, k=k, xi=xi, ct=ct: e.matmul(pacc.ap[:, 0:CW], ct.ap[:, 2 * k + 1, :], xi.ap, start=False, stop=(k == 3)), r=[ct.all(), xi], w=[pacc])
                    if k == 3:
                        y_ = yt[q % 2].all()
                        if sample:
                            pav = pacc.ap[:, 0:CW].rearrange("p (s t) -> p s t", t=5)[:, :, 1:5]
                            ucv = uc.ap.rearrange("p (s t) -> p s t", t=4)
                            yv = y_.ap.rearrange("p (s t) -> p s t", t=4)
                        else:
                            pav, ucv, yv = pacc.ap, uc.ap, y_.ap
                        op("dve", lambda e, yv=yv, ucv=ucv, pav=pav, t8=t8: e.scalar_tensor_tensor(yv, ucv, DS.ap[:, t8:t8 + 1], pav, ALU.mult, ALU.add), r=[uc, DS, pacc], w=[y_])
                        gdst = gy[t8, slice(q * CH, (q + 1) * CH)]
                        op("act", lambda e, gdst=gdst, y_=y_: e.activation(out=gdst.ap, in_=y_.ap, func=AF.Gelu), r=[y_], w=[gdst])

                CRk = [T(CR.ap, ["cr%d" % kk]) for kk in range(4)]
                CIk = [T(CI.ap, ["ci%d" % kk]) for kk in range(4)]
                prev = None
                for q in range(NCH):
                    cs = slice(q * CH, (q + 1) * CH)
                    pu = PB(next_pb([0, 1]))
                    for kt in range(8):
                        op("pe", lambda e, pu=pu, kt=kt, wt=wt, cs=cs: e.matmul(pu.ap[:, 0:CH], wt.ap[:, kt, :], hT.ap[:, kt, cs], start=(kt == 0), stop=(kt == 7)),
                           r=[wt.all(), hT[kt, cs]], w=[pu])
                    uc = uT[(t8 * NCH + q) % 3].all()
                    op("act", lambda e, pu=pu, uc=uc: e.copy(out=uc.ap, in_=pu.ap[:, 0:CH]), r=[pu], w=[uc])
                    pacc = PB(next_pb([6, 7]))
                    for k in range(4):
                        it = dict(q=q, k=k, uc=uc, st=rot[0] % 2, s3=rot[0] % 3, pacc=pacc)
                        rot[0] += 1
                        stage1(it)
                        if prev is not None:
                            stage2(prev)
                        prev = it
                stage2(prev)
        if False:
            for ri, (Ct, oname) in enumerate(((CR, "hre_p"), (CI, "him_p"))):
                pv = PB(next_pb([2, 3]))
                op("pe", lambda e, pv=pv, Ct=Ct: e.transpose(pv.ap[0:64, 0:128], Ct.ap, idf[:]), r=[Ct, IDF], w=[pv])
                so = tmp[0][ri].all()
                op("dve", lambda e, pv=pv, so=so: e.tensor_copy(so.ap[0:64, 0:128], pv.ap[0:64, 0:128]), r=[pv], w=[so])
                key = "%s:%d:%d" % (oname, l, u)
                dma("sp", O[oname][l, u].rearrange("(gp g2) p -> gp (g2 p)", g2=2), so.ap[0:64, 0:128], r=[so], w=[key])
                fin_keys.append(key)
        if sample:
            for ri, oname in enumerate(("hre_s", "him_s")):
                for s in range(NS_C):
                    pv = PB(next_pb([2, 3]))
                    op("pe", lambda e, pv=pv, ri=ri, s=s: e.transpose(pv.ap[0:64, 0:128], xsf.ap[:, ri, :, s], idf[:]), r=[xsf[ri], IDF], w=[pv])
                    so = XT[s % 3][0:128]
                    op("dve", lambda e, pv=pv, so=so: e.tensor_copy(so.ap[0:64, 0:128], pv.ap[0:64, 0:128]), r=[pv], w=[so])
                    key = "%s:%d:%d" % (oname, l, s)
                    dma("sp", O[oname][l, s].rearrange("(gp g2) p -> gp (g2 p)", g2=2), so.ap[0:64, 0:128], r=[so], w=[key])
                    fin_keys.append(key)
        cur[0] = w2
        HF = min(1024, NT)
        NHF = NT // HF
        HC = min(512, HF)
        NHC = HF // HC
        vT = TL([16, HF], BF16)
        wo = ABuf(arena, WREG, [16, 512], BF16)
        sz = [TL([HC], F32) for _ in range(2)]
        sg = [TL([HC], F32) for _ in range(2)]
        tg = [TL([HC], F32) for _ in range(2)]
        wz = [ABuf(arena, WREG + i * 2048, [8, 128], BF16) for i in range(2)]
        wga = [ABuf(arena, WREG + 4096 + i * 4096, [16, 128], BF16) for i in range(2)]
        wgb = [ABuf(arena, WREG + 12288 + i * 4096, [16, 128], BF16) for i in range(2)]

        def load_glu_w(j):
            dma("pool", wz[j % 2].ap, I["w_in_a"][l][:, W + j * 128: W + (j + 1) * 128].rearrange("(k p) c -> p k c", p=128), w=[wz[j % 2].all()])
            dma("pool", wga[j % 2].ap, I["w_glu"][l][:, j * 128:(j + 1) * 128].rearrange("(k p) c -> p k c", p=128), w=[wga[j % 2].all()])
            dma("pool", wgb[j % 2].ap, I["w_glu"][l][:, W + j * 128: W + (j + 1) * 128].rearrange("(k p) c -> p k c", p=128), w=[wgb[j % 2].all()])

        src, skey = x_src(u, l)
        dst, dkey = x_dst(u, l)
        rows = min(128, NT)
        for hf in range(NHF):
            load_glu_w(0)
            for j in range(16):
                if j + 1 < 16:
                    load_glu_w(j + 1)
                for c in range(NHC):
                    cs = slice(hf * HF + c * HC, hf * HF + (c + 1) * HC)
                    ls = slice(c * HC, (c + 1) * HC)
                    pz = PB(next_pb([0, 1]))
                    pa = PB(next_pb([2, 3]))
                    pg = PB(next_pb([4, 5]))
                    wzj, waj, wbj = wz[j % 2], wga[j % 2], wgb[j % 2]
                    for kt in range(8):
                        op("pe", lambda e, pz=pz, kt=kt, wzj=wzj, cs=cs: e.matmul(pz.ap[:, 0:HC], wzj.ap[:, kt, :], hT.ap[:, kt, cs], start=(kt == 0), stop=(kt == 7)),
                           r=[wzj.all(), hT[kt, cs]], w=[pz])
                    for kt in range(16):
                        op("pe", lambda e, pa=pa, kt=kt, waj=waj, cs=cs: e.matmul(pa.ap[:, 0:HC], waj.ap[:, kt, :], gy.ap[:, kt, cs], start=(kt == 0), stop=(kt == 15)),
                           r=[waj.all(), gy[kt, cs]], w=[pa])
                    for kt in range(16):
                        op("pe", lambda e, pg=pg, kt=kt, wbj=wbj, cs=cs: e.matmul(pg.ap[:, 0:HC], wbj.ap[:, kt, :], gy.ap[:, kt, cs], start=(kt == 0), stop=(kt == 15)),
                           r=[wbj.all(), gy[kt, cs]], w=[pg])
                    i2 = (j * NHC + c) % 2
                    sz_, sg_, tg_ = sz[i2].all(), sg[i2].all(), tg[i2].all()
                    BG = BGL[l]
                    op("act", lambda e, sz_=sz_, pz=pz: e.activation(out=sz_.ap, in_=pz.ap[:, 0:HC], func=AF.Silu), r=[pz], w=[sz_])
                    op("act", lambda e, sg_=sg_, pg=pg, j=j: e.activation(out=sg_.ap, in_=pg.ap[:, 0:HC], func=AF.Sigmoid, bias=BG.ap[:, 16 + j:17 + j]), r=[pg, BG], w=[sg_])
                    op("dve", lambda e, tg_=tg_, pa=pa, sg_=sg_, j=j: e.scalar_tensor_tensor(tg_.ap, pa.ap[:, 0:HC], BG.ap[:, j:j + 1], sg_.ap, ALU.add, ALU.mult), r=[pa, BG, sg_], w=[tg_])
                    vd = vT[j, ls]
                    op("pool", lambda e, vd=vd, tg_=tg_, sz_=sz_: e.tensor_tensor(vd.ap, tg_.ap, sz_.ap, ALU.mult), r=[tg_, sz_], w=[vd])
            ntile = HF // rows
            for hc in range(2):
                dma("pool", wo.ap, I["w_out_a"][l][:, hc * 512:(hc + 1) * 512].rearrange("(k p) c -> p k c", p=128), w=[wo.all()])
                for r_ in range(ntile):
                    tr = hf * ntile + r_
                    xt = XT[(tr * 2 + hc) % 3][0:512]
                    dma("sp", xt.ap[0:rows], src[tr * rows:(tr + 1) * rows, hc * 512:(hc + 1) * 512], r=[skey + ":%d:%d" % (tr, hc)], w=[xt])
                    po = PB(next_pb([6, 7]))
                    for kt in range(16):
                        op("pe", lambda e, po=po, kt=kt, r_=r_: e.matmul(po.ap[0:rows, :], vT.ap[:, kt, r_ * rows:(r_ + 1) * rows], wo.ap[:, kt, :],
                                                                        start=(kt == 0), stop=(kt == 15)),
                           r=[vT[kt, r_ * rows:(r_ + 1) * rows], wo.all()], w=[po])
                    op("dve", lambda e, po=po, xt=xt: e.tensor_tensor(xt.ap[0:rows], xt.ap[0:rows], po.ap[0:rows, :], ALU.add), r=[xt, po], w=[xt])
                    dma("sp", dst[tr * rows:(tr + 1) * rows, hc * 512:(hc + 1) * 512], xt.ap[0:rows], r=[xt], w=[dkey + ":%d:%d" % (tr, hc)])


    ROPC = SM(17 * 32, "ropc")
    ROPS = SM(17 * 32, "rops")

    def rope_tables():
        cur = [BIG]

        def TL(shape, dtype):
            b = ABuf(arena, cur[0], shape, dtype)
            cur[0] += (b.nbytes + 255) // 256 * 256
            return b
        posb = TL([17], F32)
        inv = TL([32], F32)
        ang = TL([17, 32], F32)
        red = TL([17, 32], F32)
        tf = TL([17, 32], F32)
        tm = TL([17, 32], F32)
        ti = TL([17, 32], I32)
        dma("sp", posb.ap, I["pos"], w=[posb.all()])
        dma("sp", inv.ap, I["invf"].partition_broadcast(128), w=[inv.all()])
        op("dve", lambda e: e.tensor_tensor(ang.ap, posb.ap.unsqueeze(2).to_broadcast([128, 17, 32]), inv.ap.unsqueeze(1).to_broadcast([128, 17, 32]), ALU.mult),
           r=[posb.all(), inv.all()], w=[ang.all()])
        range_reduce(red.all(), ang.all(), 0.0, ti.all(), tf.all(), tm.all())
        op("act", lambda e: e.activation(out=ROPS.ap.rearrange("p (a b) -> p a b", a=17), in_=red.ap, func=AF.Sin), r=[red.all()], w=[ROPS])
        range_reduce(red.all(), ang.all(), math.pi / 2, ti.all(), tf.all(), tm.all())
        op("act", lambda e: e.activation(out=ROPC.ap.rearrange("p (a b) -> p a b", a=17), in_=red.ap, func=AF.Sin), r=[red.all()], w=[ROPC])

    def rope_apply(src4, dst4, rows, t_idx, nh, tA, tB):
        cosb = ROPC.ap[0:rows, t_idx * 32:(t_idx + 1) * 32].unsqueeze(1).to_broadcast([rows, nh, 32])
        sinb = ROPS.ap[0:rows, t_idx * 32:(t_idx + 1) * 32].unsqueeze(1).to_broadcast([rows, nh, 32])
        a = tA.ap[0:rows, 0:nh * 32].rearrange("p (h j) -> p h j", h=nh)
        b = tB.ap[0:rows, 0:nh * 32].rearrange("p (h j) -> p h j", h=nh)
        return cosb, sinb, a, b

    def mla_mem(u):
        sample = (u == 2)
        NT = NTS if sample else SEQ
        cur = [BIG]
        M = {}

        def TL(name, shape, dtype):
            b = ABuf(arena, cur[0], shape, dtype)
            cur[0] += (b.nbytes + 255) // 256 * 256
            assert cur[0] <= ARENA_BYTES, (name, cur[0])
            M[name] = b
            return b
        TL("latT", [2, NT], BF16)
        TL("krT2", [NT], BF16)
        TL("glat", [KVL], F32)
        TL("gq", [QL], F32)
        if not sample:
            TL("KnT", [NH, NT], BF16)
            TL("V", [16, D], BF16)
        else:
            TL("latb", [KVL], BF16)
            TL("wukT", [NH, KVL], BF16)
            TL("wuv", [2, D], BF16)
            TL("qlT", [2, NH, NTS], BF16)
            TL("qrTs", [NH, NTS], BF16)
            TL("Oall", [2, NH, NTS], F32)
            TL("Dall", [NH, NTS], F32)
            TL("olT", [2, NH, NTS], BF16)
            TL("mnew", [NTS], F32)
            for i in range(2):
                TL("pgl%d" % i, [4, KVL], BF16)
                TL("pgk%d" % i, [4, ROPE], BF16)
                TL("cT%d" % i, [2, 512], BF16)
                TL("krT%d" % i, [512], BF16)
                TL("pTs%d" % i, [128], BF16)
            TL("pTn", [512], BF16)
        CH = min(512, NT)
        TL("hTc", [8, CH], BF16)
        TL("hTt0", [8, 128], BF16)
        TL("hTt1", [8, 128], BF16)
        TL("latf", [KVL], F32)
        TL("krf", [ROPE], F32)
        TL("krb", [128], BF16)
        TL("cqb0", [QL], BF16)
        TL("cqb1", [QL], BF16)
        TL("cqT", [3, CH], BF16)
        TL("sgT", [NH, CH], BF16)
        TL("qnT", [NH, CH], BF16)
        TL("qrb0", [512], BF16)
        TL("qrb1", [512], BF16)
        TL("qrT", [4, CH], BF16)
        for i in range(4):
            TL("pT%d" % i, [CH], BF16)
        for i in range(2):
            TL("rden%d" % i, [CH], F32)
            TL("tgs%d" % i, [CH], F32)
            TL("ra%d" % i, [256], F32)
            TL("rb%d" % i, [256], F32)
        TL("ogT", [NH, CH], BF16)
        return M

    def latent(u, M):
        sample = (u == 2)
        NT = NTS if sample else SEQ
        rows = min(128, NT)
        ntile = NT // rows
        src, skey = x_src(u, 2)
        latT, krT2 = M["latT"], M["krT2"]
        wd = ABuf(arena, WREG, [8, KVL + ROPE], BF16)
        dma("pool", wd.ap, I["w_dkv"].rearrange("(k p) c -> p k c", p=128), w=[wd.all()])
        load_gain(I["norm_kv"])
        glat = M["glat"].all()
        dma("sp", glat.ap, I["norm_latent"].partition_broadcast(128), w=[glat])
        for tr in range(ntile):
            xt = XT[tr % 3].all()
            dma("sp", xt.ap[0:rows], src[tr * rows:(tr + 1) * rows, :], r=[skey + ":%d:0" % tr, skey + ":%d:1" % tr], w=[xt])
            hb = HB[tr % 2].all()
            rms_tile(xt, rows, D, gbuf[:], hb)
            pv = PBb(next_pb([0, 1]))
            for kt in range(8):
                op("pe", lambda e, kt=kt, pv=pv, hb=hb: e.transpose(pv.ap[:, kt * 128:kt * 128 + rows], hb.ap[0:rows, kt * 128:(kt + 1) * 128], idb[0:rows, 0:rows]), r=[hb, IDB], w=[pv])
            ht = M["hTt%d" % (tr % 2)]
            op("act", lambda e, pv=pv, ht=ht: e.copy(out=ht.ap[:, :, 0:rows], in_=pv.ap.rearrange("p (k c) -> p k c", k=8)[:, :, 0:rows]), r=[pv], w=[ht.all()])
            pc = PB(next_pb([2, 3]))
            for kt in range(8):
                op("pe", lambda e, kt=kt, pc=pc, ht=ht: e.matmul(pc.ap[0:rows, 0:KVL + ROPE], ht.ap[:, kt, 0:rows], wd.ap[:, kt, :], start=(kt == 0), stop=(kt == 7)),
                   r=[ht.all(), wd.all()], w=[pc])
            latf = M["latf"].all()
            pcl = T(pc.ap[:, 0:KVL], pc.k)
            rms_tile(pcl, rows, KVL, glat.ap, latf, gain_T=glat)
            okey = "lat:%d:%d" % (u, tr)
            odst = O["lat_p"][u] if not sample else O["lat_s"]
            dma("sp", odst[tr * rows:(tr + 1) * rows, :], latf.ap[0:rows], r=[latf], w=[okey])
            fin_keys.append(okey)
            lb_ = HB[tr % 2][0:KVL] if not sample else M["latb"].all()
            op("dve", lambda e, lb_=lb_, latf=latf: e.tensor_copy(lb_.ap[0:rows], latf.ap[0:rows]), r=[latf], w=[lb_])
            pv2 = PBb(next_pb([4, 5]))
            for c2 in range(2):
                op("pe", lambda e, c2=c2, pv2=pv2, lb_=lb_: e.transpose(pv2.ap[:, c2 * 128:c2 * 128 + rows], lb_.ap[0:rows, c2 * 128:(c2 + 1) * 128], idb[0:rows, 0:rows]), r=[lb_, IDB], w=[pv2])
            ld = latT[:, tr * rows:(tr + 1) * rows]
            op("act", lambda e, pv2=pv2, ld=ld: e.copy(out=ld.ap, in_=pv2.ap[:, 0:256].rearrange("p (k c) -> p k c", k=2)[:, :, 0:rows]), r=[pv2], w=[ld])
            t_idx = 16 if sample else tr
            krf = M["krf"].all()
            ra, rb = M["ra%d" % (tr % 2)].all(), M["rb%d" % (tr % 2)].all()
            cs_ = ROPC.ap[0:rows, t_idx * 32:(t_idx + 1) * 32]
            sn_ = ROPS.ap[0:rows, t_idx * 32:(t_idx + 1) * 32]
            x1 = pc.ap[0:rows, KVL:KVL + 32]
            x2 = pc.ap[0:rows, KVL + 32:KVL + 64]
            op("dve", lambda e, ra=ra, x1=x1, cs_=cs_: e.tensor_tensor(ra.ap[0:rows, 0:32], x1, cs_, ALU.mult), r=[pc, ROPC], w=[ra])
            op("dve", lambda e, rb=rb, x2=x2, sn_=sn_: e.tensor_tensor(rb.ap[0:rows, 0:32], x2, sn_, ALU.mult), r=[pc, ROPS], w=[rb])
            op("dve", lambda e, ra=ra, rb=rb, krf=krf: e.tensor_tensor(krf.ap[0:rows, 0:32], ra.ap[0:rows, 0:32], rb.ap[0:rows, 0:32], ALU.subtract), r=[ra, rb], w=[krf])
            op("dve", lambda e, ra=ra, x1=x1, sn_=sn_: e.tensor_tensor(ra.ap[0:rows, 32:64], x1, sn_, ALU.mult), r=[pc, ROPS], w=[ra])
            op("dve", lambda e, rb=rb, x2=x2, cs_=cs_: e.tensor_tensor(rb.ap[0:rows, 32:64], x2, cs_, ALU.mult), r=[pc, ROPC], w=[rb])
            op("dve", lambda e, ra=ra, rb=rb, krf=krf: e.tensor_tensor(krf.ap[0:rows, 32:64], ra.ap[0:rows, 32:64], rb.ap[0:rows, 32:64], ALU.add), r=[ra, rb], w=[krf])
            okey = "kr:%d:%d" % (u, tr)
            odst = O["kr_p"][u] if not sample else O["kr_s"]
            dma("sp", odst[tr * rows:(tr + 1) * rows, :], krf.ap[0:rows], r=[krf], w=[okey])
            fin_keys.append(okey)
            krb = M["krb"].all()
            op("dve", lambda e, krb=krb, krf=krf: e.tensor_copy(krb.ap[0:rows].rearrange("p (a b) -> p a b", a=2), krf.ap[0:rows].unsqueeze(1).to_broadcast([rows, 2, ROPE])), r=[krf], w=[krb])
            pv3 = PBb(next_pb([6, 7]))
            op("pe", lambda e, pv3=pv3, krb=krb: e.transpose(pv3.ap[:, 0:rows], krb.ap[0:rows, :], idb[0:rows, 0:rows]), r=[krb, IDB], w=[pv3])
            kd = krT2[tr * rows:(tr + 1) * rows]
            op("dve", lambda e, pv3=pv3, kd=kd: e.tensor_copy(kd.ap, pv3.ap[:, 0:rows]), r=[pv3], w=[kd])
        if not sample:
            wuk = ABuf(arena, WREG + 8192, [2, D], BF16)
            wuv = ABuf(arena, WREG + 12288, [2, D], BF16)
            dma("pool", wuk.ap, I["w_uk"].rearrange("(k p) c -> p k c", p=128), w=[wuk.all()])
            dma("pool", wuv.ap, I["w_uv"].rearrange("(k p) c -> p k c", p=128), w=[wuv.all()])
            KnT, V = M["KnT"], M["V"]
            for h in range(NH):
                for q in range(NT // 512):
                    cs = slice(q * 512, (q + 1) * 512)
                    pk = PB(next_pb([0, 1, 2, 3]))
                    for c2 in range(2):
                        op("pe", lambda e, pk=pk, c2=c2, h=h, cs=cs: e.matmul(pk.ap, wuk.ap[:, c2, h * 128:(h + 1) * 128], latT.ap[:, c2, cs], start=(c2 == 0), stop=(c2 == 1)),
                           r=[wuk.all(), latT[c2, cs]], w=[pk])
                    kd = KnT[h, cs]
                    if (h + q) % 2 == 0:
                        op("act", lambda e, pk=pk, kd=kd: e.copy(out=kd.ap, in_=pk.ap), r=[pk], w=[kd])
                    else:
                        op("dve", lambda e, pk=pk, kd=kd: e.tensor_copy(kd.ap, pk.ap), r=[pk], w=[kd])
            for r_ in range(16):
                for hc in range(2):
                    pk = PB(next_pb([4, 5, 6, 7]))
                    for c2 in range(2):
                        op("pe", lambda e, pk=pk, c2=c2, r_=r_, hc=hc: e.matmul(pk.ap, latT.ap[:, c2, r_ * 128:(r_ + 1) * 128], wuv.ap[:, c2, hc * 512:(hc + 1) * 512], start=(c2 == 0), stop=(c2 == 1)),
                           r=[wuv.all(), latT[c2, r_ * 128:(r_ + 1) * 128]], w=[pk])
                    vd = V[r_, hc * 512:(hc + 1) * 512]
                    if (r_ + hc) % 2 == 0:
                        op("act", lambda e, pk=pk, vd=vd: e.copy(out=vd.ap, in_=pk.ap), r=[pk], w=[vd])
                    else:
                        op("dve", lambda e, pk=pk, vd=vd: e.tensor_copy(vd.ap, pk.ap), r=[pk], w=[vd])

    def mla_layer(u, j, M):
        sample = (u == 2)
        NT = NTS if sample else SEQ
        rows = min(128, NT)
        CH = min(512, NT)
        NCH = NT // CH
        TPC = CH // rows
        li = 2 + j
        src, skey = x_src(u, li)
        dst, dkey = x_dst(u, li)
        hTc, cqT, sgT, qnT, qrT, ogT = M["hTc"], M["cqT"], M["sgT"], M["qnT"], M["qrT"], M["ogT"]
        gq = M["gq"].all()
        load_gain(I["norm_b"][j])
        dma("sp", gq.ap, I["norm_q"][j].partition_broadcast(128), w=[gq])
        wcq = ABuf(arena, WREG, [8, QL], BF16)
        wgt = [ABuf(arena, WREG + 6144 + i * 2048, [8, 128], BF16) for i in range(2)]
        wuq = ABuf(arena, WREG + 10240, [3, NH * 192], BF16)
        wob = ABuf(arena, WREG, [8, 512], BF16)
        if sample:
            sample_prep(M)
        for q in range(NCH):
            dma("pool", wcq.ap, I["w_in_b"][j][:, 0:QL].rearrange("(k p) c -> p k c", p=128), w=[wcq.all()])
            dma("pool", wuq.ap, I["w_uq"][j].rearrange("(k p) c -> p k c", p=128), w=[wuq.all()])
            build_hT(u, li, hTc, NT, ntiles=TPC, tile0=q * TPC)
            for t4 in range(TPC):
                ts_ = slice(t4 * rows, (t4 + 1) * rows)
                pc = PB(next_pb([2, 3]))
                for kt in range(8):
                    op("pe", lambda e, pc=pc, kt=kt, ts_=ts_: e.matmul(pc.ap[0:rows, 0:QL], hTc.ap[:, kt, ts_], wcq.ap[:, kt, :], start=(kt == 0), stop=(kt == 7)),
                       r=[hTc[kt, ts_], wcq.all()], w=[pc])
                cqb = M["cqb%d" % (t4 % 2)].all()
                rms_tile(T(pc.ap[:, 0:QL], pc.k), rows, QL, gq.ap, cqb, gain_T=gq)
                pv = PBb(next_pb([4, 5]))
                for c3 in range(3):
                    op("pe", lambda e, pv=pv, c3=c3, cqb=cqb: e.transpose(pv.ap[:, c3 * 128:c3 * 128 + rows], cqb.ap[0:rows, c3 * 128:(c3 + 1) * 128], idb[0:rows, 0:rows]), r=[cqb, IDB], w=[pv])
                cd = cqT[:, ts_]
                op("dve", lambda e, pv=pv, cd=cd: e.tensor_copy(cd.ap, pv.ap[:, 0:384].rearrange("p (k c) -> p k c", k=3)[:, :, 0:rows]), r=[pv], w=[cd])
            def load_gate(i):
                dma("pool", wgt[i % 2].ap, I["w_in_b"][j][:, QL + i * 128: QL + (i + 1) * 128].rearrange("(k p) c -> p k c", p=128), w=[wgt[i % 2].all()])
            load_gate(0)
            for i in range(NH):
                if i + 1 < NH:
                    load_gate(i + 1)
                pg = PB(next_pb([6, 7]))
                wg = wgt[i % 2]
                for kt in range(8):
                    op("pe", lambda e, pg=pg, kt=kt, wg=wg: e.matmul(pg.ap[:, 0:CH], wg.ap[:, kt, :], hTc.ap[:, kt, :], start=(kt == 0), stop=(kt == 7)), r=[wg.all(), hTc[kt]], w=[pg])
                sd = sgT[i]
                op("act", lambda e, pg=pg, sd=sd: e.activation(out=sd.ap, in_=pg.ap[:, 0:CH], func=AF.Silu), r=[pg], w=[sd])
            for h in range(NH):
                pq = PB(next_pb([0, 1]))
                for c3 in range(3):
                    op("pe", lambda e, pq=pq, c3=c3, h=h: e.matmul(pq.ap[:, 0:CH], wuq.ap[:, c3, h * 192:h * 192 + 128], cqT.ap[:, c3, :], start=(c3 == 0), stop=(c3 == 2)),
                       r=[wuq.all(), cqT[c3]], w=[pq])
                qd = qnT[h]
                op("dve", lambda e, pq=pq, qd=qd: e.tensor_copy(qd.ap, pq.ap[:, 0:CH]), r=[pq], w=[qd])
            wuq_r = wuq.ap.rearrange("p k (h d) -> p k h d", h=NH)
            for t4 in range(TPC):
                ts_ = slice(t4 * rows, (t4 + 1) * rows)
                pr_ = PB(next_pb([2, 3]))
                for c3 in range(3):
                    op("pe", lambda e, pr_=pr_, c3=c3, ts_=ts_: e.matmul(pr_.ap[0:rows, :].rearrange("p (h d) -> p h d", h=NH), cqT.ap[:, c3, ts_], wuq_r[:, c3, :, 128:192], start=(c3 == 0), stop=(c3 == 2)),
                       r=[wuq.all(), cqT[c3, ts_]], w=[pr_])
                t_idx = 16 if sample else (q * TPC + t4)
                ra, rb = M["ra%d" % (t4 % 2)].all(), M["rb%d" % (t4 % 2)].all()
                qrb = M["qrb%d" % (t4 % 2)].all()
                cosb = ROPC.ap[0:rows, t_idx * 32:(t_idx + 1) * 32].unsqueeze(1).to_broadcast([rows, NH, 32])
                sinb = ROPS.ap[0:rows, t_idx * 32:(t_idx + 1) * 32].unsqueeze(1).to_broadcast([rows, NH, 32])
                pr4 = pr_.ap[0:rows, :].rearrange("p (h d) -> p h d", h=NH)
                q1, q2 = pr4[:, :, 0:32], pr4[:, :, 32:64]
                rav = ra.ap[0:rows].rearrange("p (h d) -> p h d", h=NH)
                rbv = rb.ap[0:rows].rearrange("p (h d) -> p h d", h=NH)
                qv = qrb.ap[0:rows].rearrange("p (h d) -> p h d", h=NH)
                op("dve", lambda e, rav=rav, q1=q1, cosb=cosb: e.tensor_tensor(rav, q1, cosb, ALU.mult), r=[pr_, ROPC], w=[ra])
                op("dve", lambda e, rbv=rbv, q2=q2, sinb=sinb: e.tensor_tensor(rbv, q2, sinb, ALU.mult), r=[pr_, ROPS], w=[rb])
                op("pool", lambda e, qv=qv, rav=rav, rbv=rbv: e.tensor_tensor(qv[:, :, 0:32], rav, rbv, ALU.subtract), r=[ra, rb], w=[qrb])
                op("dve", lambda e, rav=rav, q1=q1, sinb=sinb: e.tensor_tensor(rav, q1, sinb, ALU.mult), r=[pr_, ROPS, qrb], w=[ra])
                op("dve", lambda e, rbv=rbv, q2=q2, cosb=cosb: e.tensor_tensor(rbv, q2, cosb, ALU.mult), r=[pr_, ROPC, qrb], w=[rb])
                op("pool", lambda e, qv=qv, rav=rav, rbv=rbv: e.tensor_tensor(qv[:, :, 32:64], rav, rbv, ALU.add), r=[ra, rb], w=[qrb])
                if not sample:
                    pv = PBb(next_pb([4, 5]))
                    for hp in range(4):
                        op("pe", lambda e, pv=pv, hp=hp, qrb=qrb: e.transpose(pv.ap[:, hp * 128:(hp + 1) * 128], qrb.ap[:, hp * 128:(hp + 1) * 128], idb[:]), r=[qrb, IDB], w=[pv])
                    qd = qrT[:, ts_]
                    op("act", lambda e, pv=pv, qd=qd: e.copy(out=qd.ap, in_=pv.ap[:, 0:512].rearrange("p (k c) -> p k c", k=4)), r=[pv], w=[qd])
                else:
                    pv = PBb(next_pb([4, 5]))
                    for h in range(NH):
                        op("pe", lambda e, pv=pv, h=h, qrb=qrb: e.transpose(pv.ap[0:64, h * 64:(h + 1) * 64], qrb.ap[0:64, h * 64:(h + 1) * 64], idb[0:64, 0:64]), r=[qrb, IDB], w=[pv])
                    qd = M["qrTs"].all()
                    op("act", lambda e, pv=pv, qd=qd: e.copy(out=qd.ap[0:64], in_=pv.ap[0:64, 0:512].rearrange("p (k c) -> p k c", k=NH)), r=[pv], w=[qd])
            if not sample:
                KnT, V, krT2 = M["KnT"], M["V"], M["krT2"]
                pti = 0
                for h in range(NH):
                    po = PB(next_pb([0, 1]))
                    pd = PB(next_pb([2, 3]))
                    nkt = 4 * q + 4
                    base = 64 * (h % 2)
                    def emit_pv(pt_, kt, c0, po=po, pd=pd, h=h, nkt=nkt):
                        op("pe", lambda e, po=po, pt_=pt_, h=h, kt=kt, c0=c0, nkt=nkt: e.matmul(po.ap[:, c0:512], V.ap[:, kt, h * 128:(h + 1) * 128], pt_.ap[:, c0:512], start=(kt == 0), stop=(kt == nkt - 1), skip_group_check=True),
                           r=[V[kt, h * 128:(h + 1) * 128], pt_], w=[po])
                        op("pe", lambda e, pd=pd, pt_=pt_, kt=kt, c0=c0, nkt=nkt: e.matmul(pd.ap[:, c0:512], onesb[:], pt_.ap[:, c0:512], start=(kt == 0), stop=(kt == nkt - 1), skip_group_check=True),
                           r=["onesb", pt_], w=[pd])
                    pend = None
                    for kt in range(nkt):
                        c0 = max(0, kt - 4 * q) * 128
                        ks = slice(kt * 128, (kt + 1) * 128)
                        ps_ = PB(next_pb([4, 5, 6, 7]))
                        op("pe", lambda e, ps_=ps_, h=h, ks=ks, c0=c0: e.matmul(ps_.ap[:, c0:512], KnT.ap[:, h, ks], qnT.ap[:, h, c0:512], start=True, stop=False),
                           r=[KnT[h, ks], qnT[h]], w=[ps_])
                        op("pe", lambda e, ps_=ps_, h=h, ks=ks, c0=c0, base=base: e.matmul(ps_.ap[:, c0:512], krT2.ap[base:base + 64, ks], qrT.ap[base:base + 64, h // 2, c0:512], start=False, stop=True),
                           r=[krT2[ks], qrT[h // 2]], w=[ps_])
                        pt_ = M["pT%d" % (pti % 4)].all()
                        pti += 1
                        op("act", lambda e, ps_=ps_, pt_=pt_, c0=c0: e.activation(out=pt_.ap[:, c0:512], in_=ps_.ap[:, c0:512], func=AF.Exp, scale=SCALE), r=[ps_], w=[pt_])
                        if kt >= 4 * q:
                            op("dve", lambda e, pt_=pt_, c0=c0: e.tensor_tensor(pt_.ap[:, c0:c0 + 128], pt_.ap[:, c0:c0 + 128], trib[:], ALU.mult), r=[pt_, "trib"], w=[pt_])
                        if pend is not None:
                            emit_pv(*pend)
                        pend = (pt_, kt, c0)
                    emit_pv(*pend)
                    rd = M["rden%d" % (h % 2)].all()
                    tg_ = M["tgs%d" % (h % 2)].all()
                    op("dve", lambda e, rd=rd, pd=pd: e.reciprocal(rd.ap, pd.ap), r=[pd], w=[rd])
                    op("pool", lambda e, tg_=tg_, rd=rd, h=h: e.tensor_tensor(tg_.ap, rd.ap, sgT.ap[:, h, :], ALU.mult), r=[rd, sgT[h]], w=[tg_])
                    od = ogT[h]
                    op("dve", lambda e, od=od, po=po, tg_=tg_: e.tensor_tensor(od.ap, po.ap, tg_.ap, ALU.mult), r=[po, tg_], w=[od])
            else:
                sample_attention(M, sgT, qnT, ogT)
            for hc in range(2):
                dma("pool", wob.ap, I["w_out_b"][j][:, hc * 512:(hc + 1) * 512].rearrange("(k p) c -> p k c", p=128), w=[wob.all()])
                for t4 in range(TPC):
                    tr = q * TPC + t4
                    ts_ = slice(t4 * rows, (t4 + 1) * rows)
                    xt = XT[(tr * 2 + hc) % 3][0:512]
                    dma("sp", xt.ap[0:rows], src[tr * rows:(tr + 1) * rows, hc * 512:(hc + 1) * 512], r=[skey + ":%d:%d" % (tr, hc)], w=[xt])
                    pp = PB(next_pb([6, 7]))
                    for h in range(NH):
                        op("pe", lambda e, pp=pp, h=h, ts_=ts_: e.matmul(pp.ap[0:rows, :], ogT.ap[:, h, ts_], wob.ap[:, h, :], start=(h == 0), stop=(h == NH - 1)),
                           r=[ogT[h, ts_], wob.all()], w=[pp])
                    op("dve", lambda e, pp=pp, xt=xt: e.tensor_tensor(xt.ap[0:rows], xt.ap[0:rows], pp.ap[0:rows, :], ALU.add), r=[xt, pp], w=[xt])
                    dma("sp", dst[tr * rows:(tr + 1) * rows, hc * 512:(hc + 1) * 512], xt.ap[0:rows], r=[xt], w=[dkey + ":%d:%d" % (tr, hc)])

    def sample_prep(M):
        if M.get("_prep"):
            return
        M["_prep"] = True
        wuk = ABuf(arena, WREG + 16384, [2, D], BF16)
        dma("pool", wuk.ap, I["w_uk"].rearrange("(k p) c -> p k c", p=128), w=[wuk.all()])
        dma("pool", M["wuv"].ap, I["w_uv"].rearrange("(k p) c -> p k c", p=128), w=[M["wuv"].all()])
        wukT = M["wukT"]
        for h in range(NH):
            pv = PBb(next_pb([4, 5]))
            for c2 in range(2):
                op("pe", lambda e, pv=pv, c2=c2, h=h: e.transpose(pv.ap[:, c2 * 128:(c2 + 1) * 128], wuk.ap[:, c2, h * 128:(h + 1) * 128], idb[:]), r=[wuk.all(), IDB], w=[pv])
            wd_ = wukT[h]
            op("dve", lambda e, pv=pv, wd_=wd_: e.tensor_copy(wd_.ap, pv.ap[:, 0:256]), r=[pv], w=[wd_])
        mn = M["mnew"].all()
        dma("sp", mn.ap[0:64], I["mnew"], w=[mn])

    def sample_attention(M, sgT, qnT, ogT):
        wukT, wuv, qlT, qrTs, latT, krT2, latb = M["wukT"], M["wuv"], M["qlT"], M["qrTs"], M["latT"], M["krT2"], M["latb"]
        Oall, Dall, olT = M["Oall"], M["Dall"], M["olT"]
        for h in range(NH):
            for c2 in range(2):
                pq = PB(next_pb([0, 1]))
                op("pe", lambda e, pq=pq, h=h, c2=c2: e.matmul(pq.ap[:, 0:NTS], wukT.ap[:, h, c2 * 128:(c2 + 1) * 128], qnT.ap[:, h, :], start=True, stop=True), r=[wukT[h], qnT[h]], w=[pq])
                qd = qlT[c2, h]
                op("dve", lambda e, pq=pq, qd=qd: e.tensor_copy(qd.ap, pq.ap[:, 0:NTS]), r=[pq], w=[qd])
        step = 0
        for s in range(NS_C):
            qs = slice(4 * s, 4 * s + 4)
            pacc = PB(2)
            pden = PB(3)
            for g in range(NPAGES // 4):
                b = step % 2
                step += 1
                pgl, pgk, cT, krT, pTs = M["pgl%d" % b], M["pgk%d" % b], M["cT%d" % b], M["krT%d" % b], M["pTs%d" % b]
                dma("sp", pgl.ap, PG_L[s, g * 4:(g + 1) * 4].rearrange("g t c -> t g c"), r=["pg_l:%d" % s], w=[pgl.all()])
                dma("sp", pgk.ap, PG_K[s, g * 4:(g + 1) * 4].rearrange("g t c -> t g c"), r=["pg_k:%d" % s], w=[pgk.all()])
                pva = PBb(4 + b)
                for i in range(4):
                    for c2 in range(2):
                        op("pe", lambda e, pva=pva, i=i, c2=c2, pgl=pgl: e.transpose(pva.ap[:, (c2 * 4 + i) * 128:(c2 * 4 + i + 1) * 128], pgl.ap[:, i, c2 * 128:(c2 + 1) * 128], idb[:]), r=[pgl[i], IDB], w=[pva])
                op("act", lambda e, pva=pva, cT=cT: e.copy(out=cT.ap, in_=pva.ap.rearrange("p (a b) -> p a b", a=2)), r=[pva], w=[cT.all()])
                pvb = PBb(6 + b)
                for i in range(4):
                    op("pe", lambda e, pvb=pvb, i=i, pgk=pgk: e.transpose(pvb.ap[0:64, i * 128:(i + 1) * 128], pgk.ap[:, i, :], idb[:]), r=[pgk[i], IDB], w=[pvb])
                op("dve", lambda e, pvb=pvb, krT=krT: e.tensor_copy(krT.ap[0:64], pvb.ap[0:64, 0:512]), r=[pvb], w=[krT.all()])
                psc = PB(next_pb([0, 1]))
                for i in range(4):
                    o_ = psc.ap[:, i * 32:(i + 1) * 32].rearrange("p (h t) -> p h t", h=NH)
                    for c2 in range(2):
                        op("pe", lambda e, o_=o_, i=i, c2=c2, cT=cT, qs=qs: e.matmul(o_, cT.ap[:, c2, i * 128:(i + 1) * 128], qlT.ap[:, c2, :, qs], start=(c2 == 0), stop=False),
                           r=[cT.all(), qlT[c2]], w=[psc])
                    op("pe", lambda e, o_=o_, i=i, krT=krT, qs=qs: e.matmul(o_, krT.ap[0:64, i * 128:(i + 1) * 128], qrTs.ap[0:64, :, qs], start=False, stop=True),
                       r=[krT.all(), qrTs.all()], w=[psc])
                op("act", lambda e, psc=psc, pTs=pTs: e.activation(out=pTs.ap, in_=psc.ap[:, 0:128], func=AF.Exp, scale=SCALE), r=[psc], w=[pTs.all()])
                for i in range(4):
                    for c2 in range(2):
                        first = (g == 0 and i == 0)
                        last = (g == NPAGES // 4 - 1 and i == 3)
                        op("pe", lambda e, pacc=pacc, i=i, c2=c2, pgl=pgl, pTs=pTs, first=first, last=last: e.matmul(pacc.ap[:, c2 * 32:(c2 + 1) * 32], pgl.ap[:, i, c2 * 128:(c2 + 1) * 128], pTs.ap[:, i * 32:(i + 1) * 32], start=first, stop=last, skip_group_check=True),
                           r=[pgl[i], pTs.all()], w=[pacc])
                    op("pe", lambda e, pden=pden, i=i, pTs=pTs, first=first, last=last: e.matmul(pden.ap[:, 0:32], onesb[:], pTs.ap[:, i * 32:(i + 1) * 32], start=first, stop=last, skip_group_check=True),
                       r=["onesb", pTs.all()], w=[pden])
            op("dve", lambda e, pacc=pacc, qs=qs: e.tensor_copy(Oall.ap[:, :, :, qs], pacc.ap[:, 0:64].rearrange("p (c h t) -> p c h t", c=2, h=NH)), r=[pacc], w=[Oall.all()])
            op("dve", lambda e, pden=pden, qs=qs: e.tensor_copy(Dall.ap[:, :, qs], pden.ap[:, 0:32].rearrange("p (h t) -> p h t", h=NH)), r=[pden], w=[Dall.all()])
        psn = PB(next_pb([0, 1]))
        for c2 in range(2):
            op("pe", lambda e, psn=psn, c2=c2: e.matmul(psn.ap[0:64, :], latT.ap[:, c2, 0:64], qlT.ap[:, c2].rearrange("p h t -> p (h t)"), start=(c2 == 0), stop=False), r=[latT[c2], qlT[c2]], w=[psn])
        op("pe", lambda e, psn=psn: e.matmul(psn.ap[0:64, :], krT2.ap[0:64, 0:64], qrTs.ap[0:64].rearrange("p h t -> p (h t)"), start=False, stop=True), r=[krT2.all(), qrTs.all()], w=[psn])
        pTn = M["pTn"].all()
        tgs = XT[0][0:512]
        op("act", lambda e, psn=psn: e.activation(out=tgs.ap[0:64], in_=psn.ap[0:64, :], func=AF.Exp, scale=SCALE), r=[psn], w=[tgs])
        mn = M["mnew"].all()
        op("dve", lambda e: e.tensor_tensor(pTn.ap[0:64].rearrange("p (h t) -> p h t", h=NH), tgs.ap[0:64].rearrange("p (h t) -> p h t", h=NH),
                                            mn.ap[0:64].unsqueeze(1).to_broadcast([64, NH, NTS]), ALU.mult), r=[tgs, mn], w=[pTn])
        for c2 in range(2):
            pn = PB(next_pb([4, 5]))
            op("pe", lambda e, pn=pn, c2=c2: e.matmul(pn.ap, latb.ap[0:64, c2 * 128:(c2 + 1) * 128], pTn.ap[0:64], start=True, stop=True), r=[latb.all(), pTn], w=[pn])
            od = Oall[c2]
            op("dve", lambda e, pn=pn, od=od: e.tensor_tensor(od.ap.rearrange("p h t -> p (h t)"), od.ap.rearrange("p h t -> p (h t)"), pn.ap, ALU.add), r=[pn, od], w=[od])
        pn = PB(next_pb([6, 7]))
        op("pe", lambda e, pn=pn: e.matmul(pn.ap, onesb[0:64, :], pTn.ap[0:64], start=True, stop=True), r=["onesb", pTn], w=[pn])
        Da = Dall.all()
        op("dve", lambda e, pn=pn: e.tensor_tensor(Dall.ap.rearrange("p h t -> p (h t)"), Dall.ap.rearrange("p h t -> p (h t)"), pn.ap, ALU.add), r=[pn, Da], w=[Da])
        op("dve", lambda e: e.reciprocal(Dall.ap, Dall.ap), r=[Da], w=[Da])
        for c2 in range(2):
            od = olT[c2]
            op("dve", lambda e, c2=c2, od=od: e.tensor_tensor(od.ap, Oall.ap[:, c2], Dall.ap, ALU.mult), r=[Oall[c2], Da], w=[od])
        for h in range(NH):
            pq = PB(next_pb([0, 1]))
            for c2 in range(2):
                op("pe", lambda e, pq=pq, h=h, c2=c2: e.matmul(pq.ap[:, 0:NTS], wuv.ap[:, c2, h * 128:(h + 1) * 128], olT.ap[:, c2, h, :], start=(c2 == 0), stop=(c2 == 1)), r=[wuv.all(), olT[c2]], w=[pq])
            od = ogT[h]
            op("dve", lambda e, pq=pq, od=od, h=h: e.tensor_tensor(od.ap, pq.ap[:, 0:NTS], sgT.ap[:, h, :], ALU.mult), r=[pq, sgT[h]], w=[od])

    def final_norm(u):
        sample = (u == 2)
        NT = NTS if sample else SEQ
        rows = min(128, NT)
        src, skey = x_src(u, 4)
        load_gain(I["norm_f"])
        for tr in range(NT // rows):
            xt = XT[tr % 3].all()
            dma("sp", xt.ap[0:rows], src[tr * rows:(tr + 1) * rows, :], r=[skey + ":%d:0" % tr, skey + ":%d:1" % tr], w=[xt])
            ot = XT[(tr + 1) % 3].all()
            rms_tile(xt, rows, D, gbuf[:], ot)
            key = "y:%d:%d" % (u, tr)
            odst = O["y_p"][u] if not sample else O["y_s"]
            dma("sp", odst[tr * rows:(tr + 1) * rows, :], ot.ap[0:rows], r=[ot], w=[key])
            fin_keys.append(key)

    PF["on"] = ("mla" in stages) and (2 in units)
    if "s5" in stages:
        for l in range(2):
            s5_tables(l)
            for u in units:
                s5_layer(u, l)

    prefetch_flush()
    if "lat" in stages:
        rope_tables()
        for u in units:
            M = mla_mem(u)
            latent(u, M)
            if "mla" in stages:
                for j in range(2):
                    mla_layer(u, j, M)
            if "fin" in stages:
                final_norm(u)

    if "dbg_e" in stages:
        s5_tables(0)
        for gp in range(4):
            eb = ABuf(arena, BIG, [2, 512], BF16)
            dma("sp", eb.ap, EB_S[0, gp], r=["eb_s0"], w=[eb.all()])
            xt = XT[gp % 3].all()
            op("dve", lambda e, xt=xt, eb=eb: e.tensor_copy(xt.ap, eb.ap.rearrange("p r c -> p (r c)")), r=[eb.all()], w=[xt])
            key = "dbg:%d" % gp
            dma("sp", O["y_p"][0][gp * 128:(gp + 1) * 128, :], xt.ap, r=[xt], w=[key])
            fin_keys.append(key)
        el = ABuf(arena, BIG + 4096, [NPAIR, 4], F32)
        dma("sp", el.ap, EL_S[0], r=["el_s0"], w=[el.all()])
        dma("sp", O["y_p"][1][0:128, 0:256], el.ap.rearrange("p a b -> p (a b)"), r=[el.all()], w=["dbg:el"])
        fin_keys.append("dbg:el")

    if "dbg_x" in stages:
        for u in units:
            NT = NTS if u == 2 else SEQ
            rows = min(128, NT)
            src, skey = x_src(u, 2)
            for tr in range(NT // rows):
                xt = XT[tr % 3].all()
                dma("sp", xt.ap[0:rows], src[tr * rows:(tr + 1) * rows, :], r=[skey + ":%d:0" % tr, skey + ":%d:1" % tr], w=[xt])
                key = "y:%d:%d" % (u, tr)
                dst = O["y_p"][u] if u < 2 else O["y_s"]
                dma("sp", dst[tr * rows:(tr + 1) * rows, :], xt.ap[0:rows], r=[xt], w=[key])
                fin_keys.append(key)

    S.emit(final_keys=fin_keys)
    global LAST_S
    LAST_S = S
    return nc


def _consts():
    ident = np.eye(128, dtype=np.float32)
    cst = np.zeros((128, 64), np.float32)
    g8 = (np.arange(128) // 16)
    cst[:, 0] = (g8 % 2 == 0)
    cst[:, 1] = (g8 % 2 == 1)
    cst[:, 2] = -cst[:, 0]
    cst[:, 3] = -cst[:, 1]
    mvec = np.concatenate([32.0 * np.arange(16), 1.0 + np.arange(32)]).astype(np.float32)
    cst[:, 8:56] = mvec[None, :]
    cst[:, 4] = np.arange(128)
    tri = (np.arange(128)[:, None] <= np.arange(128)[None, :]).astype(np.float32)
    pos = np.zeros((128, 17), np.float32)
    for t in range(16):
        pos[:, t] = t * 128 + np.arange(128)
    pos[:, 16] = 8192 + (np.arange(128) % 4)
    invf = (10000.0 ** (-np.arange(32, dtype=np.float32) / 32)).astype(np.float32)
    kk = np.arange(64)
    mnew = ((kk[:, None] // 4 == kk[None, :] // 4) & (kk[:, None] % 4 <= kk[None, :] % 4)).astype(np.float32)
    return dict(ident=ident, cst=cst, tri=tri, pos=pos, invf=invf, mnew=mnew)


_WEIGHT_KEYS = ["norm_a", "w_in_a", "a_re", "a_im", "log_dt", "b_re", "b_im", "c_re", "c_im", "d_skip", "w_glu", "b_glu",
                "w_out_a", "norm_kv", "w_dkv", "norm_latent", "w_uk", "w_uv", "norm_b", "w_in_b", "norm_q", "w_uq",
                "w_out_b", "norm_f"]


def make_in_maps(inputs, cores, npool=NPOOL):
    cs = _consts()
    f = lambda a: np.ascontiguousarray(np.asarray(a))
    cl = f(inputs["cache_latent"]).reshape(npool * 128, KVL)
    ck = f(inputs["cache_krope"]).reshape(npool * 128, ROPE)
    shared = {k: f(inputs[k]) for k in _WEIGHT_KEYS}
    maps = []
    for c in cores:
        m = dict(shared)
        m.update(cs)
        m["xp"] = f(inputs["x_prompt"][NSEQ_C * c:NSEQ_C * (c + 1)])
        m["xs"] = f(inputs["x_sample"][NS_C * c:NS_C * (c + 1)]).reshape(NTS, D)
        m["cl"] = cl
        m["ck"] = ck
        m["pt"] = f(inputs["page_table"][NS_C * c:NS_C * (c + 1)]).astype(np.int32)
        m["sre"] = f(inputs["state_ssm_re"][:, NS_C * c:NS_C * (c + 1)])
        m["sim"] = f(inputs["state_ssm_im"][:, NS_C * c:NS_C * (c + 1)])
        maps.append(m)
    return maps


def assemble(results):
    y_p = np.concatenate([r["y_p"] for r in results], axis=0)
    y_s = np.concatenate([r["y_s"].reshape(NS_C, DSEQ, D) for r in results], axis=0)
    lat_p = np.concatenate([r["lat_p"] for r in results], axis=0)
    kr_p = np.concatenate([r["kr_p"] for r in results], axis=0)
    lat_s = np.concatenate([r["lat_s"].reshape(NS_C, DSEQ, KVL) for r in results], axis=0)
    kr_s = np.concatenate([r["kr_s"].reshape(NS_C, DSEQ, ROPE) for r in results], axis=0)
    hre_p = np.concatenate([r["hre_p"] for r in results], axis=1)
    him_p = np.concatenate([r["him_p"] for r in results], axis=1)
    hre_s = np.concatenate([r["hre_s"] for r in results], axis=1)
    him_s = np.concatenate([r["him_s"] for r in results], axis=1)
    return tuple(np.ascontiguousarray(a, dtype=np.float32) for a in
                 (y_p, y_s, lat_p, kr_p, lat_s, kr_s, hre_p, him_p, hre_s, him_s))


def kernel(**inputs):
    nc = build_program()
    maps = make_in_maps(inputs, list(range(NCORES)))
    res = run_bass_kernel_spmd(nc, maps, core_ids=list(range(NCORES)))
    return assemble(res.results)
```

```python
import contextlib
import math
import numpy as np
import concourse.bass as bass
import concourse.mybir as mybir
from concourse.bass_utils import run_bass_kernel_spmd

F32 = mybir.dt.float32
BF16 = mybir.dt.bfloat16
I32 = mybir.dt.int32
AF = mybir.ActivationFunctionType
ALU = mybir.AluOpType

NCORES = 8
D = 1024
SEQ = 2048
NSEQ_C = 2
NS_C = 16
DSEQ = 4
NTS = NS_C * DSEQ
W = 2048
NPAIR = 64
KVL = 256
ROPE = 64
QL = 384
NH = 8
NPOOL = 10240
NPAGES = 64
EPS = 1e-6
SCALE = 1.0 / math.sqrt(192.0)
TWO_PI = 2.0 * math.pi
GRAN = 256


class Sched:
    NDS = 24

    def __init__(self, nc):
        self.nc = nc
        self.ops = []
        self.stack = contextlib.ExitStack()
        self.last_w = {}
        self.readers = {}
        self.n_dma = 0

    def sb(self, name, shape, dtype):
        return self.stack.enter_context(self.nc.sbuf_tensor(name, list(shape), dtype))

    def ps(self, name, shape, dtype):
        return self.stack.enter_context(self.nc.psum_tensor(name, list(shape), dtype))

    def _add(self, eng, fn, reads, writes, dma):
        idx = len(self.ops)
        deps = set()
        for k in reads:
            d = self.last_w.get(k)
            if d is not None:
                deps.add(d)
        for k in writes:
            d = self.last_w.get(k)
            if d is not None:
                deps.add(d)
            deps.update(self.readers.get(k, ()))
        deps.discard(idx)
        for k in reads:
            self.readers.setdefault(k, []).append(idx)
        for k in writes:
            self.last_w[k] = idx
            self.readers[k] = []
        o = dict(eng=eng, fn=fn, deps=deps, dma=dma)
        if dma:
            o["dma_i"] = self.n_dma
            self.n_dma += 1
        self.ops.append(o)
        return idx

    def emit(self, final_keys=()):
        nc = self.nc
        ops = self.ops
        engs = ["pe", "act", "dve", "pool", "sp"]
        need_sig = [False] * len(ops)
        for i, o in enumerate(ops):
            for d in o["deps"]:
                od = ops[d]
                if od["dma"]:
                    continue
                if od["eng"] == "pe" and o["eng"] == "pe" and not o["dma"]:
                    continue
                need_sig[d] = True
        final_deps = set()
        for k in final_keys:
            if k in self.last_w:
                final_deps.add(self.last_w[k])
        for d in final_deps:
            if not ops[d]["dma"]:
                need_sig[d] = True
        cnt = {e: 0 for e in engs}
        sig_val = [None] * len(ops)
        nds = self.NDS
        dma_uses = [0] * nds
        for i, o in enumerate(ops):
            if o["dma"]:
                s = o["dma_i"] % nds
                dma_uses[s] += 1
                sig_val[i] = ("d", s, 16 * dma_uses[s])
            elif need_sig[i]:
                cnt[o["eng"]] += 1
                sig_val[i] = ("e", o["eng"], cnt[o["eng"]])
        with contextlib.ExitStack() as st:
            esem = {e: st.enter_context(nc.semaphore("es_" + e)) for e in engs}
            dsem = [st.enter_context(nc.semaphore("ds_%d" % j)) for j in range(nds)]
            block = st.enter_context(nc.Block())
            per_eng = {e: [] for e in engs}
            for i, o in enumerate(ops):
                per_eng[o["eng"]].append(i)

            def semof(sv):
                return dsem[sv[1]] if sv[0] == "d" else esem[sv[1]]

            def body(ename):
                def _f(eng):
                    waited = {}
                    for i in per_eng[ename]:
                        o = ops[i]
                        waits = {}
                        for d in o["deps"]:
                            od = ops[d]
                            if (not od["dma"]) and od["eng"] == "pe" and ename == "pe" and not o["dma"]:
                                continue
                            sv = sig_val[d]
                            key = (sv[0], sv[1])
                            if waits.get(key, 0) < sv[2]:
                                waits[key] = sv[2]
                        if o["dma"]:
                            sv = sig_val[i]
                            if sv[2] > 16:
                                key = (sv[0], sv[1])
                                if waits.get(key, 0) < sv[2] - 16:
                                    waits[key] = sv[2] - 16
                        for key, v in waits.items():
                            if waited.get(key, 0) >= v:
                                continue
                            waited[key] = v
                            eng.wait_ge(dsem[key[1]] if key[0] == "d" else esem[key[1]], v)
                        ins = o["fn"](eng)
                        sv = sig_val[i]
                        if sv is not None:
                            ins.then_inc(semof(sv), 16 if sv[0] == "d" else 1)
                    if ename == "sp":
                        for d in sorted(final_deps):
                            sv = sig_val[d]
                            key = (sv[0], sv[1])
                            if waited.get(key, 0) >= sv[2]:
                                continue
                            waited[key] = sv[2]
                            eng.wait_ge(semof(sv), sv[2])
                return _f

            block.tensor(body("pe"))
            block.scalar(body("act"))
            block.vector(body("dve"))
            block.gpsimd(body("pool"))
            block.sync(body("sp"))
        self.stack.close()


class T:
    __slots__ = ("ap", "k")

    def __init__(self, ap, k):
        self.ap = ap
        self.k = k


class ABuf:
    def __init__(self, arena, off, shape, dtype, tag="A"):
        es = 2 if dtype == BF16 else 4
        n = int(np.prod(shape))
        assert off % 4 == 0
        ap = arena[:, off // 2: off // 2 + n * es // 2]
        if dtype != BF16:
            ap = ap.bitcast(dtype)
        if len(shape) == 2:
            ap = ap.rearrange("p (a b) -> p a b", a=shape[0])
        elif len(shape) == 3:
            ap = ap.rearrange("p (a b c) -> p a b c", a=shape[0], b=shape[1])
        elif len(shape) == 4:
            ap = ap.rearrange("p (a b c d) -> p a b c d", a=shape[0], b=shape[1], c=shape[2])
        self.ap = ap
        self.off = off
        self.shape = tuple(shape)
        self.es = es
        self.tag = tag
        self.nbytes = n * es
        self._offs = None

    def keys(self, index=()):
        if self._offs is None:
            self._offs = (np.arange(int(np.prod(self.shape)), dtype=np.int64).reshape(self.shape) * self.es + self.off)
        sel = self._offs[index] if index else self._offs
        sel = np.asarray(sel).ravel()
        g = np.unique(np.concatenate([sel // GRAN, (sel + self.es - 1) // GRAN]))
        return ["%s%d" % (self.tag, int(x)) for x in g]

    def __getitem__(self, index):
        if not isinstance(index, tuple):
            index = (index,)
        return T(self.ap[(slice(None),) + index], self.keys(index))

    def all(self):
        return T(self.ap, self.keys())

    def rows(self, lo, hi, index=()):
        if not isinstance(index, tuple):
            index = (index,)
        return T(self.ap[(slice(lo, hi),) + index], self.keys(index))


def _flat(xs):
    out = []
    for x in xs:
        if isinstance(x, T):
            out.extend(x.k)
        elif isinstance(x, str):
            out.append(x)
        else:
            out.extend(_flat(x))
    return out


def build_program(stages=("s5", "lat", "mla", "fin"), units=(0, 1, 2), npool=NPOOL):
    nc = bass.Bass("TRN2", target_bir_lowering=False)

    def din(name, shape, dt=F32):
        return nc.dram_tensor(name, list(shape), dt, kind="ExternalInput").ap()

    def dout(name, shape, dt=F32):
        return nc.dram_tensor(name, list(shape), dt, kind="ExternalOutput").ap()

    def dscr(name, shape, dt=F32):
        return nc.dram_tensor(name, list(shape), dt, kind="Internal").ap()

    I = {}
    I["xp"] = din("xp", [NSEQ_C, SEQ, D])
    I["xs"] = din("xs", [NTS, D])
    I["cl"] = din("cl", [npool * 128, KVL])
    I["ck"] = din("ck", [npool * 128, ROPE])
    I["pt"] = din("pt", [NS_C, NPAGES], I32)
    I["sre"] = din("sre", [2, NS_C, 128, 64])
    I["sim"] = din("sim", [2, NS_C, 128, 64])
    I["norm_a"] = din("norm_a", [2, D])
    I["w_in_a"] = din("w_in_a", [2, D, 2 * W])
    I["a_re"] = din("a_re", [2, 128, 64])
    I["a_im"] = din("a_im", [2, 128, 64])
    I["log_dt"] = din("log_dt", [2, 128])
    I["b_re"] = din("b_re", [2, 128, 64, 16])
    I["b_im"] = din("b_im", [2, 128, 64, 16])
    I["c_re"] = din("c_re", [2, 128, 16, 64])
    I["c_im"] = din("c_im", [2, 128, 16, 64])
    I["d_skip"] = din("d_skip", [2, W])
    I["w_glu"] = din("w_glu", [2, W, 2 * W])
    I["b_glu"] = din("b_glu", [2, 2 * W])
    I["w_out_a"] = din("w_out_a", [2, W, D])
    I["norm_kv"] = din("norm_kv", [D])
    I["w_dkv"] = din("w_dkv", [D, KVL + ROPE])
    I["norm_latent"] = din("norm_latent", [KVL])
    I["w_uk"] = din("w_uk", [KVL, D])
    I["w_uv"] = din("w_uv", [KVL, D])
    I["norm_b"] = din("norm_b", [2, D])
    I["w_in_b"] = din("w_in_b", [2, D, QL + D])
    I["norm_q"] = din("norm_q", [2, QL])
    I["w_uq"] = din("w_uq", [2, QL, NH * 192])
    I["w_out_b"] = din("w_out_b", [2, D, D])
    I["norm_f"] = din("norm_f", [D])
    I["ident"] = din("ident", [128, 128])
    I["cst"] = din("cst", [128, 64])
    I["tri"] = din("tri", [128, 128])
    I["pos"] = din("pos", [128, 17])
    I["invf"] = din("invf", [32])
    I["mnew"] = din("mnew", [64, 64])

    O = {}
    O["y_p"] = dout("y_p", [NSEQ_C, SEQ, D])
    O["y_s"] = dout("y_s", [NTS, D])
    O["lat_p"] = dout("lat_p", [NSEQ_C, SEQ, KVL])
    O["kr_p"] = dout("kr_p", [NSEQ_C, SEQ, ROPE])
    O["lat_s"] = dout("lat_s", [NTS, KVL])
    O["kr_s"] = dout("kr_s", [NTS, ROPE])
    O["hre_p"] = dout("hre_p", [2, NSEQ_C, 128, 64])
    O["him_p"] = dout("him_p", [2, NSEQ_C, 128, 64])
    O["hre_s"] = dout("hre_s", [2, NS_C, 128, 64])
    O["him_s"] = dout("him_s", [2, NS_C, 128, 64])

    XS = [[dscr("xs_%d_%d" % (u, i), [SEQ if u < 2 else NTS, D]) for i in range(2)] for u in range(3)]
    BT_S = dscr("bt_s", [2, NPAIR, 2, 128, 128], BF16)
    CT_S = dscr("ct_s", [2, NPAIR, 2, 128, 128], BF16)
    HL_S = dscr("hl_s", [2, 128, NPAIR, 2, 48])
    EB_S = dscr("eb_s", [2, NPAIR, 128, 2, 512], BF16)
    EL_S = dscr("el_s", [2, 128, NPAIR, 4])
    PG_L = dscr("pg_l", [NS_C, NPAGES, 128, KVL], BF16)
    PG_K = dscr("pg_k", [NS_C, NPAGES, 128, ROPE], BF16)

    S = Sched(nc)

    def op(eng, fn, r=(), w=()):
        S._add(eng, fn, _flat(r), _flat(w), False)

    def dma(q, out, in_, r=(), w=(), **kw):
        S._add(q, lambda e: e.dma_start(out=out, in_=in_, **kw), _flat(r), _flat(w), True)

    ARENA_BYTES = 188 * 1024
    arena = S.sb("arena", [128, ARENA_BYTES // 2], BF16)
    pbs = [S.ps("pb%d" % i, [128, 512], F32) for i in range(8)]

    def PB(i):
        return T(pbs[i][:, :], ["pb%d" % i])

    def PBb(i):
        return T(pbs[i][:, :].bitcast(BF16), ["pb%d" % i])

    idf = S.sb("idf_sb", [128, 128], F32)
    idb = S.sb("idb", [128, 128], BF16)
    onesb = S.sb("onesb", [128, 128], BF16)
    cst = S.sb("cst_sb", [128, 64], F32)
    trib = S.sb("trib", [128, 128], BF16)
    small = S.sb("small", [128, 2048], F32)
    sm_off = [0]

    def SM(n, name):
        o = sm_off[0]
        sm_off[0] += n
        assert sm_off[0] <= 2048
        return T(small[:, o:o + n], [name])

    dma("sp", idf[:], I["ident"], w=["idf"])
    dma("pool", idb[:], I["ident"], w=["idb"])
    dma("sp", cst[:], I["cst"], w=["cst"])
    dma("pool", trib[:], I["tri"], w=["trib"])
    op("dve", lambda e: e.memset(onesb[:], 1.0), w=["onesb"])
    IDF = T(idf[:], ["idf"])
    IDB = T(idb[:], ["idb"])

    gbuf = S.sb("gbuf", [128, D], F32)
    GB = T(gbuf[:], ["gbuf"])

    A = {}
    off = 0

    def AL(name, shape, dtype, at=None):
        nonlocal off
        o = off if at is None else at
        b = ABuf(arena, o, shape, dtype)
        if at is None:
            off = o + b.nbytes
            off = (off + 255) // 256 * 256
        assert o + b.nbytes <= ARENA_BYTES, (name, o, b.nbytes)
        A[name] = b
        return b

    XT = [AL("xt%d" % i, [D], F32) for i in range(3)]
    HB = [AL("hb%d" % i, [D], BF16) for i in range(2)]
    WREG = off
    off += 24 * 1024
    BIG = off
    BIGSZ = ARENA_BYTES - BIG

    stat = [SM(4, "stat%d" % i) for i in range(3)]
    rr_i = [0]

    def range_reduce(dst, src, shift, tmp_i, tmp_f, tmp_m, eng="dve"):
        if shift != 0.0:
            op(eng, lambda e: e.tensor_scalar(dst.ap, src.ap, float(shift), None, ALU.add), r=[src], w=[dst])
            s2 = dst
        else:
            s2 = src
        op(eng, lambda e: e.tensor_scalar(tmp_i.ap, s2.ap, 1.0 / TWO_PI, None, ALU.mult), r=[s2], w=[tmp_i])
        op(eng, lambda e: e.tensor_copy(tmp_f.ap, tmp_i.ap), r=[tmp_i], w=[tmp_f])
        op(eng, lambda e: e.scalar_tensor_tensor(dst.ap, tmp_f.ap, -TWO_PI, s2.ap, ALU.mult, ALU.add), r=[tmp_f, s2], w=[dst])
        op(eng, lambda e: e.tensor_single_scalar(tmp_m.ap, dst.ap, math.pi, ALU.is_gt), r=[dst], w=[tmp_m])
        op(eng, lambda e: e.scalar_tensor_tensor(dst.ap, tmp_m.ap, -TWO_PI, dst.ap, ALU.mult, ALU.add), r=[tmp_m, dst], w=[dst])
        op(eng, lambda e: e.tensor_single_scalar(tmp_m.ap, dst.ap, -math.pi, ALU.is_lt), r=[dst], w=[tmp_m])
        op(eng, lambda e: e.scalar_tensor_tensor(dst.ap, tmp_m.ap, TWO_PI, dst.ap, ALU.mult, ALU.add), r=[tmp_m, dst], w=[dst])

    def load_gain(vec_ap):
        dma("sp", gbuf[:], vec_ap.partition_broadcast(128), w=[GB])

    def rms_tile(xt, rows, ncols, gain_ap, out_t, gain_T=None):
        st = stat[rr_i[0] % 3]
        rr_i[0] += 1
        xa = xt.ap[0:rows] if rows < 128 else xt.ap
        oa = out_t.ap[0:rows] if rows < 128 else out_t.ap
        sa = st.ap[0:rows] if rows < 128 else st.ap
        ga = gain_ap[0:rows] if rows < 128 else gain_ap
        op("act", lambda e: e.activation(out=oa, in_=xa, func=AF.Square, accum_out=sa[:, 0:1]), r=[xt], w=[out_t, st])
        op("dve", lambda e: e.tensor_scalar(sa[:, 1:2], sa[:, 0:1], 1.0 / ncols, EPS, ALU.mult, ALU.add), r=[st], w=[st])
        op("act", lambda e: e.activation(out=sa[:, 1:2], in_=sa[:, 1:2], func=AF.Sqrt), r=[st], w=[st])
        op("dve", lambda e: e.reciprocal(sa[:, 2:3], sa[:, 1:2]), r=[st], w=[st])
        op("dve", lambda e: e.scalar_tensor_tensor(oa, xa, sa[:, 2:3], ga, ALU.mult, ALU.mult), r=[xt, st, GB if gain_T is None else gain_T], w=[out_t])

    pb_rot = [0]
    cvst = SM(128, "cvst")

    def load_colvec(dst, vec_ap, n):
        dma("sp", cvst.ap[0:n, :], vec_ap.rearrange("(t p) -> t p", p=128), w=[cvst])
        pv = PB(7)
        op("pe", lambda e: e.transpose(pv.ap[:, 0:n], cvst.ap[0:n, :], idf[0:n, 0:n]), r=[cvst, IDF], w=[pv])
        op("dve", lambda e: e.tensor_copy(dst.ap, pv.ap[:, 0:n]), r=[pv], w=[dst])

    def next_pb(cands):
        i = cands[pb_rot[0] % len(cands)]
        pb_rot[0] += 1
        return i

    def x_src(u, li):
        if li == 0:
            return (I["xp"][u] if u < 2 else I["xs"]), "xin%d" % u
        return XS[u][(li - 1) % 2], "xs_%d_%d" % (u, (li - 1) % 2)

    def x_dst(u, li):
        return XS[u][li % 2], "xs_%d_%d" % (u, li % 2)

    def build_hT(u, li, hT, NT, ncols_tile=128, col0=0, ntiles=None, tile0=0, keep_x=None):
        src, skey = x_src(u, li)
        rows = min(128, NT)
        nt = (NT // rows) if ntiles is None else ntiles
        for r in range(nt):
            tr = tile0 + r
            xt = XT[(tr) % 3].all() if keep_x is None else keep_x[r]
            dma("sp", xt.ap[0:rows], src[tr * rows:(tr + 1) * rows, :], r=[skey + ":%d:0" % tr, skey + ":%d:1" % tr], w=[xt])
            hb = HB[tr % 2].all()
            rms_tile(xt, rows, D, gbuf[:], hb)
            pbi = next_pb([0, 1])
            pv = PBb(pbi)
            for kt in range(8):
                op("pe", lambda e, kt=kt, pv=pv, hb=hb: e.transpose(pv.ap[:, kt * 128:kt * 128 + rows], hb.ap[0:rows, kt * 128:(kt + 1) * 128], idb[0:rows, 0:rows]),
                   r=[hb, IDB], w=[pv])
            dst = hT[:, col0 + r * rows: col0 + (r + 1) * rows]
            op("act", lambda e, pv=pv, dst=dst: e.copy(out=dst.ap, in_=pv.ap.rearrange("p (k c) -> p k c", k=8)[:, :, 0:rows]), r=[pv], w=[dst])

    fin_keys = []


    pfl = [S.sb("pfl%d" % i, [128, KVL], BF16) for i in range(2)]
    pfk = [S.sb("pfk%d" % i, [128, ROPE], BF16) for i in range(2)]
    pidx = S.sb("pidx", [128, 4 * NPAGES], I32)
    pptb = S.sb("pptb", [128, 4 * NPAGES], I32)
    PF = {"n": 0, "pending": None, "on": False}

    def prefetch_step():
        if not PF["on"]:
            return
        n = PF["n"]
        if n < NS_C * NPAGES:
            s_, pg = divmod(n, NPAGES)
            if n % (4 * NPAGES) == 0:
                blk = n // (4 * NPAGES)
                dma("sp", pptb[:], I["pt"][blk * 4:(blk + 1) * 4].rearrange("s g -> (s g)").partition_broadcast(128), w=["pptb"])
                op("dve", lambda e: e.tensor_scalar(pidx[:], pptb[:], 128.0, cst[:, 4:5], ALU.mult, ALU.add), r=["pptb", "cst"], w=["pidx"])
            col = n % (4 * NPAGES)
            b = n % 2
            S._add("pool", (lambda e, b=b, col=col: e.indirect_dma_start(
                out=pfl[b][:], out_offset=None, in_=I["cl"], in_offset=bass.IndirectOffsetOnAxis(ap=pidx[:, col:col + 1], axis=0))),
                ["pidx"], ["pfl%d" % b], True)
            S._add("pool", (lambda e, b=b, col=col: e.indirect_dma_start(
                out=pfk[b][:], out_offset=None, in_=I["ck"], in_offset=bass.IndirectOffsetOnAxis(ap=pidx[:, col:col + 1], axis=0))),
                ["pidx"], ["pfk%d" % b], True)
        if PF["pending"] is not None:
            s_, pg, b = PF["pending"]
            dma("sp", PG_L[s_, pg], pfl[b][:], r=["pfl%d" % b], w=["pg_l:%d" % s_])
            dma("sp", PG_K[s_, pg], pfk[b][:], r=["pfk%d" % b], w=["pg_k:%d" % s_])
            PF["pending"] = None
        if n < NS_C * NPAGES:
            PF["pending"] = (n // NPAGES, n % NPAGES, n % 2)
            PF["n"] = n + 1

    def prefetch_flush():
        while PF["on"] and (PF["n"] < NS_C * NPAGES or PF["pending"] is not None):
            prefetch_step()

    def s5_tables(l):
        nonlocal off
        base = BIG
        cur = [base]

        def TL(shape, dtype):
            b = ABuf(arena, cur[0], shape, dtype)
            cur[0] += (b.nbytes + 255) // 256 * 256
            assert cur[0] <= ARENA_BYTES
            return b

        nat = TL([3, 128], F32)
        ld2 = TL([2], F32)
        at = TL([3, 64], F32)
        dt_ = TL([64], F32)
        er = TL([64], F32)
        th = TL([64], F32)
        rr = TL([64], F32)
        t1 = TL([64], F32)
        t2 = TL([64], F32)
        t3 = TL([64], F32)
        ti = TL([64], I32)
        s1 = TL([64], F32)
        c1 = TL([64], F32)
        fr = TL([64], F32)
        fi = TL([64], F32)
        dma("sp", nat.ap[0:64, 0, :], I["a_re"][l].rearrange("(gp g2) p -> gp (g2 p)", g2=2), w=[nat[0]])
        dma("sp", nat.ap[0:64, 1, :], I["a_im"][l].rearrange("(gp g2) p -> gp (g2 p)", g2=2), w=[nat[1]])
        dma("sp", ld2.ap[0:64, :], I["log_dt"][l].rearrange("(gp g2) -> gp g2", g2=2), w=[ld2.all()])
        op("dve", lambda e: e.tensor_copy(nat.ap[0:64, 2, :].rearrange("q (g p) -> q g p", g=2),
                                          ld2.ap[0:64, :].unsqueeze(2).to_broadcast([64, 2, 64])), r=[ld2.all()], w=[nat[2]])
        pv = PB(7)
        for j in range(3):
            op("pe", lambda e, j=j, pv=pv: e.transpose(pv.ap[:, j * 64:(j + 1) * 64], nat.ap[0:64, j, :], idf[0:64, 0:64]), r=[nat[j], IDF], w=[pv])
        op("dve", lambda e, pv=pv: e.tensor_copy(at.ap, pv.ap[:, 0:192].rearrange("p (a b) -> p a b", a=3)), r=[pv], w=[at.all()])
        op("act", lambda e: e.activation(out=dt_.ap, in_=at.ap[:, 2, :], func=AF.Exp), r=[at[2]], w=[dt_.all()])
        op("dve", lambda e: e.tensor_tensor(er.ap, at.ap[:, 0, :], dt_.ap, ALU.mult), r=[at[0], dt_.all()], w=[er.all()])
        op("dve", lambda e: e.tensor_tensor(th.ap, at.ap[:, 1, :], dt_.ap, ALU.mult), r=[at[1], dt_.all()], w=[th.all()])
        op("act", lambda e: e.activation(out=rr.ap, in_=er.ap, func=AF.Exp), r=[er.all()], w=[rr.all()])
        range_reduce(t1.all(), th.all(), 0.0, ti.all(), t2.all(), t3.all())
        op("act", lambda e: e.activation(out=s1.ap, in_=t1.ap, func=AF.Sin), r=[t1.all()], w=[s1.all()])
        range_reduce(t1.all(), th.all(), math.pi / 2, ti.all(), t2.all(), t3.all())
        op("act", lambda e: e.activation(out=c1.ap, in_=t1.ap, func=AF.Sin), r=[t1.all()], w=[c1.all()])
        lbr, lbi = c1, s1
        op("dve", lambda e: e.tensor_tensor(lbr.ap, c1.ap, rr.ap, ALU.mult), r=[c1.all(), rr.all()], w=[lbr.all()])
        op("dve", lambda e: e.tensor_tensor(lbi.ap, s1.ap, rr.ap, ALU.mult), r=[s1.all(), rr.all()], w=[lbi.all()])
        op("dve", lambda e: e.tensor_scalar(lbr.ap, lbr.ap, -1.0, None, ALU.add), r=[lbr.all()], w=[lbr.all()])
        op("dve", lambda e: e.tensor_tensor(t1.ap, at.ap[:, 0, :], at.ap[:, 0, :], ALU.mult), r=[at[0]], w=[t1.all()])
        op("dve", lambda e: e.tensor_tensor(t2.ap, at.ap[:, 1, :], at.ap[:, 1, :], ALU.mult), r=[at[1]], w=[t2.all()])
        op("dve", lambda e: e.tensor_tensor(t1.ap, t1.ap, t2.ap, ALU.add), r=[t1.all(), t2.all()], w=[t1.all()])
        op("dve", lambda e: e.reciprocal(t3.ap, t1.ap), r=[t1.all()], w=[t3.all()])
        op("dve", lambda e: e.tensor_tensor(t1.ap, lbr.ap, at.ap[:, 0, :], ALU.mult), r=[lbr.all(), at[0]], w=[t1.all()])
        op("dve", lambda e: e.tensor_tensor(t2.ap, lbi.ap, at.ap[:, 1, :], ALU.mult), r=[lbi.all(), at[1]], w=[t2.all()])
        op("dve", lambda e: e.tensor_tensor(t1.ap, t1.ap, t2.ap, ALU.add), r=[t1.all(), t2.all()], w=[t1.all()])
        op("dve", lambda e: e.tensor_tensor(fr.ap, t1.ap, t3.ap, ALU.mult), r=[t1.all(), t3.all()], w=[fr.all()])
        op("dve", lambda e: e.tensor_tensor(t1.ap, lbi.ap, at.ap[:, 0, :], ALU.mult), r=[lbi.all(), at[0]], w=[t1.all()])
        op("dve", lambda e: e.tensor_tensor(t2.ap, lbr.ap, at.ap[:, 1, :], ALU.mult), r=[lbr.all(), at[1]], w=[t2.all()])
        op("dve", lambda e: e.tensor_tensor(t1.ap, t1.ap, t2.ap, ALU.subtract), r=[t1.all(), t2.all()], w=[t1.all()])
        op("dve", lambda e: e.tensor_tensor(fi.ap, t1.ap, t3.ap, ALU.mult), r=[t1.all(), t3.all()], w=[fi.all()])
        RRl = RR[l]
        op("dve", lambda e: e.tensor_copy(RRl.ap, rr.ap), r=[rr.all()], w=[RRl])
        NHL = NPAIR * 48
        arg = TL([NPAIR, 48], F32)
        red = TL([NPAIR, 48], F32)
        tf = TL([NPAIR, 48], F32)
        tm = TL([NPAIR, 48], F32)
        tii = TL([NPAIR, 48], I32)
        hl = TL([NPAIR, 2, 48], F32)
        CST = T(cst[:], ["cst"])
        op("dve", lambda e: e.tensor_tensor(arg.ap, th.ap.unsqueeze(2).to_broadcast([128, NPAIR, 48]),
                                            cst[:, 8:56].unsqueeze(1).to_broadcast([128, NPAIR, 48]), ALU.mult), r=[th.all(), CST], w=[arg.all()])
        range_reduce(red.all(), arg.all(), 0.0, tii.all(), tf.all(), tm.all())
        op("act", lambda e: e.activation(out=hl.ap[:, :, 0, :], in_=red.ap, func=AF.Sin), r=[red.all()], w=[hl.all()])
        range_reduce(red.all(), arg.all(), math.pi / 2, tii.all(), tf.all(), tm.all())
        op("act", lambda e: e.activation(out=hl.ap[:, :, 1, :], in_=red.ap, func=AF.Sin), r=[red.all()], w=[hl.all()])
        dma("sp", HL_S[l], hl.ap, r=[hl.all()], w=["hl_s%d" % l])
        cur[0] = arg.off
        bn = TL([2, NPAIR, 16], F32)
        bp = TL([2, NPAIR, 16], F32)
        tb = TL([NPAIR, 16], F32)
        n4 = TL([2, NPAIR, 128], BF16)
        for ri, nm in enumerate(["b_re", "b_im"]):
            v = I[nm][l].rearrange("(gp g2) p c -> g2 p gp c", g2=2)
            for g2 in range(2):
                dma("sp", bn.ap[64 * g2:64 * (g2 + 1), ri], v[g2], w=[bn[ri]])
        frb = fr.ap.unsqueeze(2).to_broadcast([128, NPAIR, 16])
        fib = fi.ap.unsqueeze(2).to_broadcast([128, NPAIR, 16])
        op("dve", lambda e: e.tensor_tensor(bp.ap[:, 0], bn.ap[:, 0], frb, ALU.mult), r=[bn[0], fr.all()], w=[bp[0]])
        op("dve", lambda e: e.tensor_tensor(tb.ap, bn.ap[:, 1], fib, ALU.mult), r=[bn[1], fi.all()], w=[tb.all()])
        op("dve", lambda e: e.tensor_tensor(bp.ap[:, 0], bp.ap[:, 0], tb.ap, ALU.subtract), r=[bp[0], tb.all()], w=[bp[0]])
        op("dve", lambda e: e.tensor_tensor(bp.ap[:, 1], bn.ap[:, 0], fib, ALU.mult), r=[bn[0], fi.all()], w=[bp[1]])
        op("dve", lambda e: e.tensor_tensor(tb.ap, bn.ap[:, 1], frb, ALU.mult), r=[bn[1], fr.all()], w=[tb.all()])
        op("dve", lambda e: e.tensor_tensor(bp.ap[:, 1], bp.ap[:, 1], tb.ap, ALU.add), r=[bp[1], tb.all()], w=[bp[1]])
        op("pool", lambda e: e.memset(n4.ap, 0.0), w=[n4.all()])
        for ri in range(2):
            n4v = n4.ap[:, ri].rearrange("p (t k) c -> p t k c", k=4)
            bpv = bp.ap[:, ri].rearrange("p (t k) c -> p t k c", k=4)
            for k in range(4):
                for g2 in range(2):
                    c0 = 32 * k + 16 * g2
                    op("dve", lambda e, n4v=n4v, bpv=bpv, k=k, g2=g2, c0=c0: e.tensor_copy(
                        n4v[64 * g2:64 * (g2 + 1), :, k, c0:c0 + 16], bpv[64 * g2:64 * (g2 + 1), :, k, :]),
                        r=[bp[ri]], w=[n4[ri]])
        stg = [TL([8, 128], BF16) for _ in range(2)]
        for t8 in range(16):
            pv = PBb(next_pb([2, 3]))
            for k in range(4):
                for ri in range(2):
                    j = k * 2 + ri
                    op("pe", lambda e, pv=pv, j=j, ri=ri, gp=t8 * 4 + k: e.transpose(pv.ap[:, j * 128:(j + 1) * 128], n4.ap[:, ri, gp, :], idb[:]),
                       r=[n4[ri], IDB], w=[pv])
            sg = stg[t8 % 2]
            op("act", lambda e, pv=pv, sg=sg: e.copy(out=sg.ap, in_=pv.ap.rearrange("p (a b) -> p a b", a=8)), r=[pv], w=[sg.all()])
            dma("sp", BT_S[l, t8 * 4:(t8 + 1) * 4].rearrange("k r p c -> p (k r) c"), sg.ap, r=[sg.all()], w=["bt_s%d" % l])
        cur[0] = bn.off
        cn = TL([2, 16, 64], F32)
        mt = TL([2, 16, 128], BF16)
        ctp = [TL([2, 768], BF16) for _ in range(2)]
        for ri, nm in enumerate(["c_re", "c_im"]):
            dma("sp", cn.ap[:, ri], I[nm][l].rearrange("(t g) c p -> (g c) t p", g=8), w=[cn[ri]])
        for ri in range(2):
            op("dve", lambda e, ri=ri: e.tensor_scalar(mt.ap[:, ri, :, 0:64], cn.ap[:, ri], cst[:, 2 * ri:2 * ri + 1], None, ALU.mult), r=[cn[ri], CST], w=[mt[ri]])
            op("dve", lambda e, ri=ri: e.tensor_scalar(mt.ap[:, ri, :, 64:128], cn.ap[:, ri], cst[:, 2 * ri + 1:2 * ri + 2], None, ALU.mult), r=[cn[ri], CST], w=[mt[ri]])
        for c in ctp:
            op("pool", lambda e, c=c: e.memset(c.ap, 0.0), w=[c.all()])
        for t8 in range(16):
            pv = PBb(next_pb([2, 3]))
            for ri in range(2):
                op("pe", lambda e, pv=pv, ri=ri, t8=t8: e.transpose(pv.ap[:, ri * 128:(ri + 1) * 128], mt.ap[:, ri, t8, :], idb[:]), r=[mt[ri], IDB], w=[pv])
            c = ctp[t8 % 2]
            for ri in range(2):
                op("act", lambda e, pv=pv, ri=ri, c=c: e.copy(out=c.ap[:, ri, :].rearrange("p (k s) -> p k s", s=192)[:, :, 0:32],
                                                             in_=pv.ap[:, ri * 128:(ri + 1) * 128].rearrange("p (k j) -> p k j", k=4)), r=[pv], w=[c.all()])
            for ri in range(2):
                dma("sp", CT_S[l, t8 * 4:(t8 + 1) * 4, ri].rearrange("k p c -> p k c"),
                    c.ap[:, ri, 0:640].rearrange("p (k s) -> p k s", s=160)[:, :, 0:128], r=[c.all()], w=["ct_s%d" % l])

        cur[0] = hl.off + (hl.nbytes + 255) // 256 * 256
        ef = [[TL([512], F32) for _ in range(2)] for _ in range(2)]
        tab2 = [TL([512], F32) for _ in range(2)]
        ebs = [TL([2, 512], BF16) for _ in range(2)]
        elt = TL([NPAIR, 4], F32)
        for gp in range(NPAIR):
            er_, ei_ = ef[gp % 2][0].all(), ef[gp % 2][1].all()
            ta_, tb_ = tab2[0].all(), tab2[1].all()
            v3 = lambda t: t.ap.rearrange("p (a b) -> p a b", a=16)
            hi = lambda sc, gp=gp: hl.ap[:, gp, sc, 0:16].unsqueeze(2).to_broadcast([128, 16, 32])
            lo = lambda sc, gp=gp: hl.ap[:, gp, sc, 16:48].unsqueeze(1).to_broadcast([128, 16, 32])
            HLT = hl.all()
            op("dve", lambda e, hi=hi, lo=lo, ta_=ta_: e.tensor_tensor(v3(ta_), hi(1), lo(1), ALU.mult), r=[HLT], w=[ta_])
            op("dve", lambda e, hi=hi, lo=lo, tb_=tb_: e.tensor_tensor(v3(tb_), hi(0), lo(0), ALU.mult), r=[HLT], w=[tb_])
            op("dve", lambda e, er_=er_, ta_=ta_, tb_=tb_: e.tensor_tensor(er_.ap, ta_.ap, tb_.ap, ALU.subtract), r=[ta_, tb_], w=[er_])
            op("dve", lambda e, hi=hi, lo=lo, ta_=ta_: e.tensor_tensor(v3(ta_), hi(0), lo(1), ALU.mult), r=[HLT, er_], w=[ta_])
            op("dve", lambda e, hi=hi, lo=lo, tb_=tb_: e.tensor_tensor(v3(tb_), hi(1), lo(0), ALU.mult), r=[HLT, er_], w=[tb_])
            op("dve", lambda e, ei_=ei_, ta_=ta_, tb_=tb_: e.tensor_tensor(ei_.ap, ta_.ap, tb_.ap, ALU.add), r=[ta_, tb_], w=[ei_])
            eb_ = ebs[gp % 2]
            op("act", lambda e, eb_=eb_, er_=er_: e.copy(out=eb_.ap[:, 0, :], in_=er_.ap), r=[er_], w=[eb_.all()])
            op("act", lambda e, eb_=eb_, ei_=ei_: e.copy(out=eb_.ap[:, 1, :], in_=ei_.ap), r=[ei_], w=[eb_.all()])
            EL_ = elt.all()
            op("act", lambda e, gp=gp, er_=er_: e.copy(out=elt.ap[:, gp, 0:1], in_=er_.ap[:, 511:512]), r=[er_], w=[EL_])
            op("act", lambda e, gp=gp, ei_=ei_: e.copy(out=elt.ap[:, gp, 1:2], in_=ei_.ap[:, 511:512]), r=[ei_], w=[EL_])
            op("act", lambda e, gp=gp, ei_=ei_: e.mul(elt.ap[:, gp, 2:3], ei_.ap[:, 511:512], -1.0), r=[ei_], w=[EL_])
            dma("sp", EB_S[l, gp], eb_.ap, r=[eb_.all()], w=["eb_s%d" % l])
        dma("sp", EL_S[l], elt.ap, r=[elt.all()], w=["el_s%d" % l])

    RR = [SM(64, "rr%d" % l) for l in range(2)]
    DSK = [SM(16, "dsk%d" % l) for l in range(2)]
    BGL = [SM(32, "bgl%d" % l) for l in range(2)]
    CR = SM(64, "cr")
    CI = SM(64, "ci")


    def s5_step2_prompt(u, l, hT, gy, base):
        NT, CH, NCH, CW = SEQ, 512, SEQ // 512, 512
        cur = [base]

        def TL(shape, dtype):
            b = ABuf(arena, cur[0], shape, dtype)
            cur[0] += (b.nbytes + 255) // 256 * 256
            assert cur[0] <= ARENA_BYTES, cur[0]
            return b
        xreg = [XT[0].off]

        def TX(shape, dtype):
            b = ABuf(arena, xreg[0], shape, dtype)
            xreg[0] += (b.nbytes + 255) // 256 * 256
            assert xreg[0] <= WREG, xreg[0]
            return b
        bub = [[TL([CW], BF16) for _ in range(2)] for _ in range(2)]
        prod = [[TL([CW], BF16) for _ in range(4)] for _ in range(3)]
        wbf = [[TL([CW], BF16) for _ in range(2)] for _ in range(2)]
        wf = [[TL([CW], F32) for _ in range(2)] for _ in range(2)]
        wb = [[TL([CW], BF16) for _ in range(2)] for _ in range(2)]
        nctb = [TL([4, 128], BF16) for _ in range(2)]
        Ebs = [TL([4, 2, CW], BF16), TX([4, 2, CW], BF16)]
        Els = [TX([4, 4], F32) for _ in range(2)]
        ctm = [TL([2], F32) for _ in range(2)]
        yt = [TL([CH], F32) for _ in range(2)]
        uT = [TL([CH], BF16) for _ in range(3)]
        wreg = [ABuf(arena, WREG + i * 2048, [8, 128], BF16) for i in range(3)]
        btb = [ABuf(arena, WREG + 6144 + i * 2048, [8, 128], BF16) for i in range(2)]
        ctb = [ABuf(arena, WREG + 10240 + i * 2048, [8, 128], BF16) for i in range(2)]
        RRl = RR[l]
        DS = DSK[l]
        CRk = [T(CR.ap, ["cr%d" % kk]) for kk in range(4)]
        CIk = [T(CI.ap, ["ci%d" % kk]) for kk in range(4)]

        def load_tile_weights(t8):
            wt = wreg[t8 % 3]
            dma("pool", wt.ap, I["w_in_a"][l][:, t8 * 128:(t8 + 1) * 128].rearrange("(k p) c -> p k c", p=128), w=[wt.all()])
            bt = btb[t8 % 2]
            dma("sp", bt.ap, BT_S[l, t8 * 4:(t8 + 1) * 4].rearrange("k r p c -> p (k r) c"), r=["bt_s%d" % l], w=[bt.all()])
            ct = ctb[t8 % 2]
            dma("sp", ct.ap, CT_S[l, t8 * 4:(t8 + 1) * 4].rearrange("k r p c -> p (k r) c"), r=["ct_s%d" % l], w=[ct.all()])
            eb = Ebs[t8 % 2]
            dma("sp", eb.ap.rearrange("p k r c -> p k (r c)"), EB_S[l, t8 * 4:(t8 + 1) * 4].rearrange("k p r c -> p k (r c)"), r=["eb_s%d" % l], w=[eb.all()])
            el = Els[t8 % 2]
            dma("sp", el.ap, EL_S[l][:, t8 * 4:(t8 + 1) * 4], r=["el_s%d" % l], w=[el.all()])

        def stageA(it):
            k, uc, st, bt = it["k"], it["uc"], it["st"], it["bt"]
            pr, pi = PB(2 + st), PB(4 + st)
            op("pe", lambda e, pr=pr, k=k, uc=uc, bt=bt: e.matmul(pr.ap, bt.ap[:, 2 * k, :], uc.ap, start=True, stop=True), r=[bt.all(), uc], w=[pr])
            op("pe", lambda e, pi=pi, k=k, uc=uc, bt=bt: e.matmul(pi.ap, bt.ap[:, 2 * k + 1, :], uc.ap, start=True, stop=True), r=[bt.all(), uc], w=[pi])
            br, bi = bub[st][0].all(), bub[st][1].all()
            op("act", lambda e, br=br, pr=pr: e.copy(out=br.ap, in_=pr.ap), r=[pr], w=[br])
            op("act", lambda e, bi=bi, pi=pi: e.copy(out=bi.ap, in_=pi.ap), r=[pi], w=[bi])

        def stageB(it):
            q, k, st, s3, t8, Eb, El = it["q"], it["k"], it["st"], it["s3"], it["t8"], it["Eb"], it["El"]
            gp = t8 * 4 + k
            br, bi = bub[st][0].all(), bub[st][1].all()
            m1, m2, m3, m4 = [x.all() for x in prod[s3]]
            wri, wii = wbf[st][0].all(), wbf[st][1].all()
            wr, wi = wf[st][0].all(), wf[st][1].all()
            wbr, wbi = wb[st][0].all(), wb[st][1].all()
            Er, Ei = Eb[k, 0], Eb[k, 1]
            op("dve", lambda e, a=m1, Er=Er, br=br: e.tensor_tensor(a.ap, Er.ap, br.ap, ALU.mult), r=[Er, br], w=[m1])
            op("dve", lambda e, a=m2, Ei=Ei, bi=bi: e.tensor_tensor(a.ap, Ei.ap, bi.ap, ALU.mult), r=[Ei, bi], w=[m2])
            op("dve", lambda e, a=m3, Er=Er, bi=bi: e.tensor_tensor(a.ap, Er.ap, bi.ap, ALU.mult), r=[Er, bi], w=[m3])
            op("dve", lambda e, a=m4, Ei=Ei, br=br: e.tensor_tensor(a.ap, Ei.ap, br.ap, ALU.mult), r=[Ei, br], w=[m4])
            op("dve", lambda e, o=wri, a=m1, b=m2: e.tensor_tensor(o.ap, a.ap, b.ap, ALU.add), r=[m1, m2], w=[wri])
            op("dve", lambda e, o=wii, a=m3, b=m4: e.tensor_tensor(o.ap, a.ap, b.ap, ALU.subtract), r=[m3, m4], w=[wii])
            d0 = RRl.ap[:, gp:gp + 1].to_broadcast([128, CW])
            ini_r = 0.0 if q == 0 else CR.ap[:, gp:gp + 1]
            ini_i = 0.0 if q == 0 else CI.ap[:, gp:gp + 1]
            rdeps = [RRl] + ([] if q == 0 else [CRk[k], CIk[k]])
            op("dve", lambda e, wr=wr, wri=wri, d0=d0, ini=ini_r: e.tensor_tensor_scan(wr.ap, d0, wri.ap, ini, ALU.mult, ALU.add), r=[wri] + rdeps, w=[wr])
            op("dve", lambda e, wi=wi, wii=wii, d0=d0, ini=ini_i: e.tensor_tensor_scan(wi.ap, d0, wii.ap, ini, ALU.mult, ALU.add), r=[wii] + rdeps, w=[wi])
            op("act", lambda e, wbr=wbr, wr=wr: e.copy(out=wbr.ap, in_=wr.ap), r=[wr], w=[wbr])
            op("act", lambda e, wbi=wbi, wi=wi: e.copy(out=wbi.ap, in_=wi.ap), r=[wi], w=[wbi])
            elk = El[k]
            c_ = ctm[st].all()
            L = CW - 1
            op("act", lambda e, c_=c_, wi=wi, elk=elk: e.activation(out=c_.ap[:, 0:1], in_=wi.ap[:, L:L + 1], func=AF.Copy, scale=elk.ap[:, 2:3]), r=[wi, elk], w=[c_])
            op("act", lambda e, c_=c_, wr=wr, elk=elk: e.activation(out=c_.ap[:, 1:2], in_=wr.ap[:, L:L + 1], func=AF.Copy, scale=elk.ap[:, 1:2]), r=[wr, elk, c_], w=[c_])
            op("act", lambda e, c_=c_, wr=wr, elk=elk, gp=gp: e.activation(out=CR.ap[:, gp:gp + 1], in_=wr.ap[:, L:L + 1], func=AF.Identity, bias=c_.ap[:, 0:1], scale=elk.ap[:, 0:1]), r=[wr, elk, c_], w=[CRk[k]])
            op("act", lambda e, c_=c_, wi=wi, elk=elk, gp=gp: e.activation(out=CI.ap[:, gp:gp + 1], in_=wi.ap[:, L:L + 1], func=AF.Identity, bias=c_.ap[:, 1:2], scale=elk.ap[:, 0:1]), r=[wi, elk, c_], w=[CIk[k]])

        def stageC(it):
            q, k, uc, st, s3, pacc, t8, Eb, ct = it["q"], it["k"], it["uc"], it["st"], it["s3"], it["pacc"], it["t8"], it["Eb"], it["ct"]
            d1, d2, d3, d4 = [x.all() for x in prod[s3]]
            wbr, wbi = wb[st][0].all(), wb[st][1].all()
            Er, Ei = Eb[k, 0], Eb[k, 1]
            op("dve", lambda e, a=d1, Er=Er, w_=wbr: e.tensor_tensor(a.ap, Er.ap, w_.ap, ALU.mult), r=[Er, wbr], w=[d1])
            op("dve", lambda e, a=d2, Ei=Ei, w_=wbi: e.tensor_tensor(a.ap, Ei.ap, w_.ap, ALU.mult), r=[Ei, wbi], w=[d2])
            op("dve", lambda e, a=d3, Er=Er, w_=wbi: e.tensor_tensor(a.ap, Er.ap, w_.ap, ALU.mult), r=[Er, wbi], w=[d3])
            op("dve", lambda e, a=d4, Ei=Ei, w_=wbr: e.tensor_tensor(a.ap, Ei.ap, w_.ap, ALU.mult), r=[Ei, wbr], w=[d4])
            nct = it["nct"]
            op("pe", lambda e, pacc=pacc, k=k, d=d1, ct=ct: e.matmul(pacc.ap, ct.ap[:, 2 * k, :], d.ap, start=(k == 0), stop=False), r=[ct.all(), d1], w=[pacc])
            op("pe", lambda e, pacc=pacc, k=k, d=d2, nct=nct: e.matmul(pacc.ap, nct.ap[:, k, :], d.ap, start=False, stop=False), r=[nct.all(), d2], w=[pacc])
            op("pe", lambda e, pacc=pacc, k=k, d=d3, ct=ct: e.matmul(pacc.ap, ct.ap[:, 2 * k + 1, :], d.ap, start=False, stop=False), r=[ct.all(), d3], w=[pacc])
            op("pe", lambda e, pacc=pacc, k=k, d=d4, ct=ct: e.matmul(pacc.ap, ct.ap[:, 2 * k + 1, :], d.ap, start=False, stop=(k == 3)), r=[ct.all(), d4], w=[pacc])
            if k == 3:
                y_ = yt[q % 2].all()
                op("dve", lambda e, y_=y_, uc=uc, pacc=pacc, t8=t8: e.scalar_tensor_tensor(y_.ap, uc.ap, DS.ap[:, t8:t8 + 1], pacc.ap, ALU.mult, ALU.add), r=[uc, DS, pacc], w=[y_])
                gdst = gy[t8, slice(q * CH, (q + 1) * CH)]
                op("act", lambda e, gdst=gdst, y_=y_: e.activation(out=gdst.ap, in_=y_.ap, func=AF.Gelu), r=[y_], w=[gdst])

        items = []
        rot = 0
        for t8 in range(16):
            for q in range(NCH):
                for k in range(4):
                    items.append(dict(t8=t8, q=q, k=k, st=rot % 2, s3=rot % 3, bt=btb[t8 % 2], ct=ctb[t8 % 2], Eb=Ebs[t8 % 2], El=Els[t8 % 2], nct=nctb[t8 % 2]))
                    rot += 1
        ucs, paccs = {}, {}

        def prep_chunk(t8, q):
            wt = wreg[t8 % 3]
            cs = slice(q * CH, (q + 1) * CH)
            pu = PB(next_pb([0, 1]))
            for kt in range(8):
                op("pe", lambda e, pu=pu, kt=kt, wt=wt, cs=cs: e.matmul(pu.ap, wt.ap[:, kt, :], hT.ap[:, kt, cs], start=(kt == 0), stop=(kt == 7)),
                   r=[wt.all(), hT[kt, cs]], w=[pu])
            uc = uT[(t8 * NCH + q) % 3].all()
            op("act", lambda e, pu=pu, uc=uc: e.copy(out=uc.ap, in_=pu.ap), r=[pu], w=[uc])
            ucs[(t8, q)] = uc
            paccs[(t8, q)] = PB(6 + (t8 * NCH + q) % 2)

        load_tile_weights(0)
        n = len(items)
        for i in range(n + 2):
            if i < n:
                it = items[i]
                if it["k"] == 0 and it["q"] == 2 and it["t8"] + 1 < 16:
                    load_tile_weights(it["t8"] + 1)
                if it["k"] == 0 and it["q"] == 0:
                    ct_, nc_ = it["ct"], it["nct"]
                    op("act", lambda e, ct_=ct_, nc_=nc_: e.mul(nc_.ap, ct_.ap.rearrange("p (k r) c -> p k r c", r=2)[:, :, 0, :], -1.0), r=[ct_.all()], w=[nc_.all()])
                if it["k"] == 0:
                    prep_chunk(it["t8"], it["q"])
                it["uc"] = ucs[(it["t8"], it["q"])]
                it["pacc"] = paccs[(it["t8"], it["q"])]
                stageA(it)
            if 1 <= i <= n:
                stageB(items[i - 1])
            if i >= 2:
                stageC(items[i - 2])
            prefetch_step()
        for ri, (Ct, oname) in enumerate(((CR, "hre_p"), (CI, "him_p"))):
            pv = PB(next_pb([2, 3]))
            Cta = T(Ct.ap, ["cr%d" % kk for kk in range(4)] if ri == 0 else ["ci%d" % kk for kk in range(4)])
            op("pe", lambda e, pv=pv, Ct=Ct: e.transpose(pv.ap[0:64, 0:128], Ct.ap, idf[:]), r=[Cta, IDF], w=[pv])
            so = wf[0][ri].all()
            op("dve", lambda e, pv=pv, so=so: e.tensor_copy(so.ap[0:64, 0:128], pv.ap[0:64, 0:128]), r=[pv], w=[so])
            key = "%s:%d:%d" % (oname, l, u)
            dma("sp", O[oname][l, u].rearrange("(gp g2) p -> gp (g2 p)", g2=2), so.ap[0:64, 0:128], r=[so], w=[key])
            fin_keys.append(key)

    def s5_layer(u, l):
        sample = (u == 2)
        NT = NTS if sample else SEQ
        CH = 64 if sample else 512
        NCH = NT // CH
        CW = 80 if sample else 512
        cur = [BIG]

        def TL(shape, dtype):
            b = ABuf(arena, cur[0], shape, dtype)
            cur[0] += (b.nbytes + 255) // 256 * 256
            assert cur[0] <= ARENA_BYTES, cur[0]
            return b

        hT = TL([8, NT], BF16)
        gy = TL([16, NT], BF16)
        w2 = cur[0]
        load_gain(I["norm_a"][l])
        load_colvec(DSK[l], I["d_skip"][l], 16)
        load_colvec(BGL[l], I["b_glu"][l], 32)
        build_hT(u, l, hT, NT)
        if not sample:
            s5_step2_prompt(u, l, hT, gy, cur[0])
        if sample:
            E = TL([4, 2, CW], F32)
            uT = [TL([CH], BF16) for _ in range(3)]
            tmp = [[TL([CW], F32) for _ in range(4)] for _ in range(2)]
            tmp.append([ABuf(arena, XT[0].off + i * 2048, [CW], F32) for i in range(4)])
            wv = [[TL([CW], F32) for _ in range(2)] for _ in range(2)]
            xb = [[TL([CW], BF16) for _ in range(2)] for _ in range(2)]
            yt = [TL([CH], F32) for _ in range(2)]
            hlb = [TL([4, 2, 48], F32) for _ in range(2)]
            if sample:
                rs = TL([NS_C, 5], F32)
                h0 = TL([2, NPAIR, NS_C], F32)
                xsf = TL([2, NPAIR, NS_C], F32)
                snat = [TL([128], F32) for _ in range(2)]
                op("dve", lambda e: e.memset(rs.ap, 0.0), w=[rs.all()])
                op("dve", lambda e: e.memset(E.ap[:, :, 0, :], 1.0), w=[E.all()])
                op("dve", lambda e: e.memset(E.ap[:, :, 1, :], 0.0), w=[E.all()])
                for ri, nm in enumerate(["sre", "sim"]):
                    for s in range(NS_C):
                        sn = snat[(ri * NS_C + s) % 2]
                        dma("sp", sn.ap[0:64, :], I[nm][l, s].rearrange("(gp g2) p -> gp (g2 p)", g2=2), w=[sn.all()])
                        pv = PB(next_pb([2, 3]))
                        op("pe", lambda e, pv=pv, sn=sn: e.transpose(pv.ap[:, 0:64], sn.ap[0:64, :], idf[0:64, 0:64]), r=[sn.all(), IDF], w=[pv])
                        op("dve", lambda e, pv=pv, ri=ri, s=s: e.tensor_copy(h0.ap[:, ri, :, s], pv.ap[:, 0:64]), r=[pv], w=[h0[ri]])
            wreg = [ABuf(arena, WREG + i * 2048, [8, 128], BF16) for i in range(3)]
            btb = [ABuf(arena, WREG + 6144 + i * 2048, [8, 128], BF16) for i in range(2)]
            ctb = [ABuf(arena, WREG + 10240 + i * 2048, [8, 128], BF16) for i in range(2)]
            wk = "w_in_a"

            def load_tile_weights(t8):
                wt = wreg[t8 % 3]
                dma("pool", wt.ap, I["w_in_a"][l][:, t8 * 128:(t8 + 1) * 128].rearrange("(k p) c -> p k c", p=128), w=[wt.all()])
                bt = btb[t8 % 2]
                dma("sp", bt.ap, BT_S[l, t8 * 4:(t8 + 1) * 4].rearrange("k r p c -> p (k r) c"), r=["bt_s%d" % l], w=[bt.all()])
                ct = ctb[t8 % 2]
                dma("sp", ct.ap, CT_S[l, t8 * 4:(t8 + 1) * 4].rearrange("k r p c -> p (k r) c"), r=["ct_s%d" % l], w=[ct.all()])
                hb_ = hlb[t8 % 2]
                dma("sp", hb_.ap, HL_S[l][:, t8 * 4:(t8 + 1) * 4], r=["hl_s%d" % l], w=[hb_.all()])

            load_tile_weights(0)
            rot = [0]
            for t8 in range(16):
                if t8 + 1 < 16:
                    load_tile_weights(t8 + 1)
                wt = wreg[t8 % 3]
                bt = btb[t8 % 2]
                ct = ctb[t8 % 2]
                hl = hlb[t8 % 2]
                ut = uT[t8 % 2]
                for k in range(4):
                    if not sample:
                        ev = lambda ri, k=k: E.ap[:, k, ri, :].rearrange("p (a b) -> p a b", a=16)
                        hi = lambda sc, k=k, hl=hl: hl.ap[:, k, sc, 0:16].unsqueeze(2).to_broadcast([128, 16, 32])
                        lo = lambda sc, k=k, hl=hl: hl.ap[:, k, sc, 16:48].unsqueeze(1).to_broadcast([128, 16, 32])
                        ta = tmp[0][0].ap.rearrange("p (a b) -> p a b", a=16)
                        tb_ = tmp[0][1].ap.rearrange("p (a b) -> p a b", a=16)
                        Tta, Ttb = tmp[0][0].all(), tmp[0][1].all()
                        op("dve", lambda e, hi=hi, lo=lo, ta=ta: e.tensor_tensor(ta, hi(1), lo(1), ALU.mult), r=[hl.all()], w=[Tta])
                        op("dve", lambda e, hi=hi, lo=lo, tb_=tb_: e.tensor_tensor(tb_, hi(0), lo(0), ALU.mult), r=[hl.all()], w=[Ttb])
                        op("dve", lambda e, ev=ev, ta=ta, tb_=tb_: e.tensor_tensor(ev(0), ta, tb_, ALU.subtract), r=[Tta, Ttb], w=[E[k, 0]])
                        op("dve", lambda e, hi=hi, lo=lo, ta=ta: e.tensor_tensor(ta, hi(0), lo(1), ALU.mult), r=[hl.all()], w=[Tta])
                        op("dve", lambda e, hi=hi, lo=lo, tb_=tb_: e.tensor_tensor(tb_, hi(1), lo(0), ALU.mult), r=[hl.all()], w=[Ttb])
                        op("dve", lambda e, ev=ev, ta=ta, tb_=tb_: e.tensor_tensor(ev(1), ta, tb_, ALU.add), r=[Tta, Ttb], w=[E[k, 1]])
                    else:
                        for ri, sc in ((0, 1), (1, 0)):
                            op("dve", lambda e, k=k, ri=ri, sc=sc, hl=hl: e.tensor_copy(
                                E.ap[:, k, ri, :].rearrange("p (s t) -> p s t", t=5)[:, :, 1:5],
                                hl.ap[:, k, sc, 16:20].unsqueeze(1).to_broadcast([128, NS_C, 4])), r=[hl.all()], w=[E[k, ri]])
                RRl = RR[l]
                DS = DSK[l]
                if sample:
                    v5 = lambda t: t.ap.rearrange("p (s t) -> p s t", t=5)[:, :, 1:5]
                    p4 = lambda p: p.ap[:, 0:CH].rearrange("p (s t) -> p s t", t=4)
                    s0 = lambda t: t.ap.rearrange("p (s t) -> p s t", t=5)[:, :, 0]
                    l5 = lambda t: t.ap.rearrange("p (s t) -> p s t", t=5)[:, :, 4]
                else:
                    v5 = lambda t: t.ap
                    p4 = lambda p: p.ap

                def stage1(it):
                    q, k, uc, st = it["q"], it["k"], it["uc"], it["st"]
                    gp = t8 * 4 + k
                    pr = PB(2 + st)
                    pi = PB(4 + st)
                    op("pe", lambda e, pr=pr, k=k, uc=uc, bt=bt: e.matmul(pr.ap[:, 0:CH], bt.ap[:, 2 * k, :], uc.ap, start=True, stop=True), r=[bt.all(), uc], w=[pr])
                    op("pe", lambda e, pi=pi, k=k, uc=uc, bt=bt: e.matmul(pi.ap[:, 0:CH], bt.ap[:, 2 * k + 1, :], uc.ap, start=True, stop=True), r=[bt.all(), uc], w=[pi])
                    t1_, t2_, t3_, t4_ = [x.all() for x in tmp[it["s3"]]]
                    wri, wii = [x.all() for x in wv[st]]
                    Er, Ei = E[k, 0], E[k, 1]
                    op("dve", lambda e, a=t1_, Er=Er, pr=pr: e.tensor_tensor(v5(a), v5(Er), p4(pr), ALU.mult), r=[Er, pr], w=[t1_])
                    op("dve", lambda e, a=t2_, Ei=Ei, pi=pi: e.tensor_tensor(v5(a), v5(Ei), p4(pi), ALU.mult), r=[Ei, pi], w=[t2_])
                    op("dve", lambda e, a=t3_, Er=Er, pi=pi: e.tensor_tensor(v5(a), v5(Er), p4(pi), ALU.mult), r=[Er, pi], w=[t3_])
                    op("dve", lambda e, a=t4_, Ei=Ei, pr=pr: e.tensor_tensor(v5(a), v5(Ei), p4(pr), ALU.mult), r=[Ei, pr], w=[t4_])
                    op("dve", lambda e, o=wri, a=t1_, b=t2_: e.tensor_tensor(v5(o), v5(a), v5(b), ALU.add), r=[t1_, t2_], w=[wri])
                    op("dve", lambda e, o=wii, a=t3_, b=t4_: e.tensor_tensor(v5(o), v5(a), v5(b), ALU.subtract), r=[t3_, t4_], w=[wii])
                    if sample:
                        op("dve", lambda e, o=wri, gp=gp: e.tensor_copy(s0(o), h0.ap[:, 0, gp, :]), r=[h0[0]], w=[wri])
                        op("dve", lambda e, o=wii, gp=gp: e.tensor_copy(s0(o), h0.ap[:, 1, gp, :]), r=[h0[1]], w=[wii])

                def stage2(it):
                    q, k, uc, st, pacc = it["q"], it["k"], it["uc"], it["st"], it["pacc"]
                    gp = t8 * 4 + k
                    t1_, t2_, t3_, t4_ = [x.all() for x in tmp[it["s3"]]]
                    wri, wii = [x.all() for x in wv[st]]
                    wr, wi = wri, wii
                    xr, xi = [x.all() for x in xb[st]]
                    Er, Ei = E[k, 0], E[k, 1]
                    if sample:
                        op("dve", lambda e, gp=gp: e.tensor_copy(rs.ap[:, :, 1:5], RRl.ap[:, gp:gp + 1].unsqueeze(2).to_broadcast([128, NS_C, 4])), r=[RRl], w=[rs.all()])
                        d0 = rs.ap.rearrange("p s t -> p (s t)")
                        ini_r = ini_i = 0.0
                        rdeps = [rs.all()]
                    else:
                        d0 = RRl.ap[:, gp:gp + 1].to_broadcast([128, CW])
                        ini_r = 0.0 if q == 0 else CR.ap[:, gp:gp + 1]
                        ini_i = 0.0 if q == 0 else CI.ap[:, gp:gp + 1]
                        rdeps = [RRl] + ([] if q == 0 else [CRk[k], CIk[k]])
                    op("dve", lambda e, wr=wr, wri=wri, d0=d0, ini=ini_r: e.tensor_tensor_scan(wr.ap, d0, wri.ap, ini, ALU.mult, ALU.add), r=[wri] + rdeps, w=[wr])
                    op("dve", lambda e, wi=wi, wii=wii, d0=d0, ini=ini_i: e.tensor_tensor_scan(wi.ap, d0, wii.ap, ini, ALU.mult, ALU.add), r=[wii] + rdeps, w=[wi])
                    op("dve", lambda e, a=t1_, Er=Er, wr=wr: e.tensor_tensor(a.ap, Er.ap, wr.ap, ALU.mult), r=[Er, wr], w=[t1_])
                    op("dve", lambda e, a=t2_, Ei=Ei, wi=wi: e.tensor_tensor(a.ap, Ei.ap, wi.ap, ALU.mult), r=[Ei, wi], w=[t2_])
                    op("dve", lambda e, a=t3_, Er=Er, wi=wi: e.tensor_tensor(a.ap, Er.ap, wi.ap, ALU.mult), r=[Er, wi], w=[t3_])
                    op("dve", lambda e, a=t4_, Ei=Ei, wr=wr: e.tensor_tensor(a.ap, Ei.ap, wr.ap, ALU.mult), r=[Ei, wr], w=[t4_])
                    op("dve", lambda e, xr=xr, a=t1_, b=t2_: e.tensor_tensor(xr.ap, a.ap, b.ap, ALU.subtract), r=[t1_, t2_], w=[xr])
                    op("dve", lambda e, xi=xi, a=t3_, b=t4_: e.tensor_tensor(xi.ap, a.ap, b.ap, ALU.add), r=[t3_, t4_], w=[xi])
                    if sample:
                        op("dve", lambda e, gp=gp, a=t1_, b=t2_: e.tensor_tensor(xsf.ap[:, 0, gp, :], l5(a), l5(b), ALU.subtract), r=[t1_, t2_], w=[xsf[0]])
                        op("dve", lambda e, gp=gp, a=t3_, b=t4_: e.tensor_tensor(xsf.ap[:, 1, gp, :], l5(a), l5(b), ALU.add), r=[t3_, t4_], w=[xsf[1]])
                    else:
                        op("dve", lambda e, gp=gp, a=t1_, b=t2_: e.tensor_tensor(CR.ap[:, gp:gp + 1], a.ap[:, CW - 1:CW], b.ap[:, CW - 1:CW], ALU.subtract), r=[t1_, t2_], w=[CRk[k]])
                        op("dve", lambda e, gp=gp, a=t3_, b=t4_: e.tensor_tensor(CI.ap[:, gp:gp + 1], a.ap[:, CW - 1:CW], b.ap[:, CW - 1:CW], ALU.add), r=[t3_, t4_], w=[CIk[k]])
                    op("pe", lambda e, pacc=pacc, k=k, xr=xr, ct=ct: e.matmul(pacc.ap[:, 0:CW], ct.ap[:, 2 * k, :], xr.ap, start=(k == 0), stop=False), r=[ct.all(), xr], w=[pacc])
                    op("pe", lambda e, pacc=pacc, k=k, xi=xi, ct=ct: e.matmul(pacc.ap[:, 0:CW], ct.ap[:, 2 * k + 1, :], xi.ap, start=False, stop=(k == 3)), r=[ct.all(), xi], w=[pacc])
                    if k == 3:
                        y_ = yt[q % 2].all()
                        if sample:
                            pav = pacc.ap[:, 0:CW].rearrange("p (s t) -> p s t", t=5)[:, :, 1:5]
                            ucv = uc.ap.rearrange("p (s t) -> p s t", t=4)
                            yv = y_.ap.rearrange("p (s t) -> p s t", t=4)
                        else:
                            pav, ucv, yv = pacc.ap, uc.ap, y_.ap
                        op("dve", lambda e, yv=yv, ucv=ucv, pav=pav, t8=t8: e.scalar_tensor_tensor(yv, ucv, DS.ap[:, t8:t8 + 1], pav, ALU.mult, ALU.add), r=[uc, DS, pacc], w=[y_])
                        gdst = gy[t8, slice(q * CH, (q + 1) * CH)]
                        op("act", lambda e, gdst=gdst, y_=y_: e.activation(out=gdst.ap, in_=y_.ap, func=AF.Gelu), r=[y_], w=[gdst])

                CRk = [T(CR.ap, ["cr%d" % kk]) for kk in range(4)]
                CIk = [T(CI.ap, ["ci%d" % kk]) for kk in range(4)]
                prev = None
                for q in range(NCH):
                    cs = slice(q * CH, (q + 1) * CH)
                    pu = PB(next_pb([0, 1]))
                    for kt in range(8):
                        op("pe", lambda e, pu=pu, kt=kt, wt=wt, cs=cs: e.matmul(pu.ap[:, 0:CH], wt.ap[:, kt, :], hT.ap[:, kt, cs], start=(kt == 0), stop=(kt == 7)),
                           r=[wt.all(), hT[kt, cs]], w=[pu])
                    uc = uT[(t8 * NCH + q) % 3].all()
                    op("act", lambda e, pu=pu, uc=uc: e.copy(out=uc.ap, in_=pu.ap[:, 0:CH]), r=[pu], w=[uc])
                    pacc = PB(next_pb([6, 7]))
                    for k in range(4):
                        it = dict(q=q, k=k, uc=uc, st=rot[0] % 2, s3=rot[0] % 3, pacc=pacc)
                        rot[0] += 1
                        stage1(it)
                        if prev is not None:
                            stage2(prev)
                        prev = it
                stage2(prev)
        if False:
            for ri, (Ct, oname) in enumerate(((CR, "hre_p"), (CI, "him_p"))):
                pv = PB(next_pb([2, 3]))
                op("pe", lambda e, pv=pv, Ct=Ct: e.transpose(pv.ap[0:64, 0:128], Ct.ap, idf[:]), r=[Ct, IDF], w=[pv])
                so = tmp[0][ri].all()
                op("dve", lambda e, pv=pv, so=so: e.tensor_copy(so.ap[0:64, 0:128], pv.ap[0:64, 0:128]), r=[pv], w=[so])
                key = "%s:%d:%d" % (oname, l, u)
                dma("sp", O[oname][l, u].rearrange("(gp g2) p -> gp (g2 p)", g2=2), so.ap[0:64, 0:128], r=[so], w=[key])
                fin_keys.append(key)
        if sample:
            for ri, oname in enumerate(("hre_s", "him_s")):
                for s in range(NS_C):
                    pv = PB(next_pb([2, 3]))
                    op("pe", lambda e, pv=pv, ri=ri, s=s: e.transpose(pv.ap[0:64, 0:128], xsf.ap[:, ri, :, s], idf[:]), r=[xsf[ri], IDF], w=[pv])
                    so = XT[s % 3][0:128]
                    op("dve", lambda e, pv=pv, so=so: e.tensor_copy(so.ap[0:64, 0:128], pv.ap[0:64, 0:128]), r=[pv], w=[so])
                    key = "%s:%d:%d" % (oname, l, s)
                    dma("sp", O[oname][l, s].rearrange("(gp g2) p -> gp (g2 p)", g2=2), so.ap[0:64, 0:128], r=[so], w=[key])
                    fin_keys.append(key)
        cur[0] = w2
        HF = min(1024, NT)
        NHF = NT // HF
        HC = min(512, HF)
        NHC = HF // HC
        vT = TL([16, HF], BF16)
        wo = ABuf(arena, WREG, [16, 512], BF16)
        sz = [TL([HC], F32) for _ in range(2)]
        sg = [TL([HC], F32) for _ in range(2)]
        tg = [TL([HC], F32) for _ in range(2)]
        wz = [ABuf(arena, WREG + i * 2048, [8, 128], BF16) for i in range(2)]
        wga = [ABuf(arena, WREG + 4096 + i * 4096, [16, 128], BF16) for i in range(2)]
        wgb = [ABuf(arena, WREG + 12288 + i * 4096, [16, 128], BF16) for i in range(2)]

        def load_glu_w(j):
            dma("pool", wz[j % 2].ap, I["w_in_a"][l][:, W + j * 128: W + (j + 1) * 128].rearrange("(k p) c -> p k c", p=128), w=[wz[j % 2].all()])
            dma("pool", wga[j % 2].ap, I["w_glu"][l][:, j * 128:(j + 1) * 128].rearrange("(k p) c -> p k c", p=128), w=[wga[j % 2].all()])
            dma("pool", wgb[j % 2].ap, I["w_glu"][l][:, W + j * 128: W + (j + 1) * 128].rearrange("(k p) c -> p k c", p=128), w=[wgb[j % 2].all()])

        src, skey = x_src(u, l)
        dst, dkey = x_dst(u, l)
        rows = min(128, NT)
        for hf in range(NHF):
            load_glu_w(0)
            for j in range(16):
                if j + 1 < 16:
                    load_glu_w(j + 1)
                for c in range(NHC):
                    cs = slice(hf * HF + c * HC, hf * HF + (c + 1) * HC)
                    ls = slice(c * HC, (c + 1) * HC)
                    pz = PB(next_pb([0, 1]))
                    pa = PB(next_pb([2, 3]))
                    pg = PB(next_pb([4, 5]))
                    wzj, waj, wbj = wz[j % 2], wga[j % 2], wgb[j % 2]
                    for kt in range(8):
                        op("pe", lambda e, pz=pz, kt=kt, wzj=wzj, cs=cs: e.matmul(pz.ap[:, 0:HC], wzj.ap[:, kt, :], hT.ap[:, kt, cs], start=(kt == 0), stop=(kt == 7)),
                           r=[wzj.all(), hT[kt, cs]], w=[pz])
                    for kt in range(16):
                        op("pe", lambda e, pa=pa, kt=kt, waj=waj, cs=cs: e.matmul(pa.ap[:, 0:HC], waj.ap[:, kt, :], gy.ap[:, kt, cs], start=(kt == 0), stop=(kt == 15)),
                           r=[waj.all(), gy[kt, cs]], w=[pa])
                    for kt in range(16):
                        op("pe", lambda e, pg=pg, kt=kt, wbj=wbj, cs=cs: e.matmul(pg.ap[:, 0:HC], wbj.ap[:, kt, :], gy.ap[:, kt, cs], start=(kt == 0), stop=(kt == 15)),
                           r=[wbj.all(), gy[kt, cs]], w=[pg])
                    i2 = (j * NHC + c) % 2
                    sz_, sg_, tg_ = sz[i2].all(), sg[i2].all(), tg[i2].all()
                    BG = BGL[l]
                    op("act", lambda e, sz_=sz_, pz=pz: e.activation(out=sz_.ap, in_=pz.ap[:, 0:HC], func=AF.Silu), r=[pz], w=[sz_])
                    op("act", lambda e, sg_=sg_, pg=pg, j=j: e.activation(out=sg_.ap, in_=pg.ap[:, 0:HC], func=AF.Sigmoid, bias=BG.ap[:, 16 + j:17 + j]), r=[pg, BG], w=[sg_])
                    op("dve", lambda e, tg_=tg_, pa=pa, sg_=sg_, j=j: e.scalar_tensor_tensor(tg_.ap, pa.ap[:, 0:HC], BG.ap[:, j:j + 1], sg_.ap, ALU.add, ALU.mult), r=[pa, BG, sg_], w=[tg_])
                    vd = vT[j, ls]
                    op("pool", lambda e, vd=vd, tg_=tg_, sz_=sz_: e.tensor_tensor(vd.ap, tg_.ap, sz_.ap, ALU.mult), r=[tg_, sz_], w=[vd])
            ntile = HF // rows
            for hc in range(2):
                dma("pool", wo.ap, I["w_out_a"][l][:, hc * 512:(hc + 1) * 512].rearrange("(k p) c -> p k c", p=128), w=[wo.all()])
                for r_ in range(ntile):
                    tr = hf * ntile + r_
                    xt = XT[(tr * 2 + hc) % 3][0:512]
                    dma("sp", xt.ap[0:rows], src[tr * rows:(tr + 1) * rows, hc * 512:(hc + 1) * 512], r=[skey + ":%d:%d" % (tr, hc)], w=[xt])
                    po = PB(next_pb([6, 7]))
                    for kt in range(16):
                        op("pe", lambda e, po=po, kt=kt, r_=r_: e.matmul(po.ap[0:rows, :], vT.ap[:, kt, r_ * rows:(r_ + 1) * rows], wo.ap[:, kt, :],
                                                                        start=(kt == 0), stop=(kt == 15)),
                           r=[vT[kt, r_ * rows:(r_ + 1) * rows], wo.all()], w=[po])
                    op("dve", lambda e, po=po, xt=xt: e.tensor_tensor(xt.ap[0:rows], xt.ap[0:rows], po.ap[0:rows, :], ALU.add), r=[xt, po], w=[xt])
                    dma("sp", dst[tr * rows:(tr + 1) * rows, hc * 512:(hc + 1) * 512], xt.ap[0:rows], r=[xt], w=[dkey + ":%d:%d" % (tr, hc)])


    ROPC = SM(17 * 32, "ropc")
    ROPS = SM(17 * 32, "rops")

    def rope_tables():
        cur = [BIG]

        def TL(shape, dtype):
            b = ABuf(arena, cur[0], shape, dtype)
            cur[0] += (b.nbytes + 255) // 256 * 256
            return b
        posb = TL([17], F32)
        inv = TL([32], F32)
        ang = TL([17, 32], F32)
        red = TL([17, 32], F32)
        tf = TL([17, 32], F32)
        tm = TL([17, 32], F32)
        ti = TL([17, 32], I32)
        dma("sp", posb.ap, I["pos"], w=[posb.all()])
        dma("sp", inv.ap, I["invf"].partition_broadcast(128), w=[inv.all()])
        op("dve", lambda e: e.tensor_tensor(ang.ap, posb.ap.unsqueeze(2).to_broadcast([128, 17, 32]), inv.ap.unsqueeze(1).to_broadcast([128, 17, 32]), ALU.mult),
           r=[posb.all(), inv.all()], w=[ang.all()])
        range_reduce(red.all(), ang.all(), 0.0, ti.all(), tf.all(), tm.all())
        op("act", lambda e: e.activation(out=ROPS.ap.rearrange("p (a b) -> p a b", a=17), in_=red.ap, func=AF.Sin), r=[red.all()], w=[ROPS])
        range_reduce(red.all(), ang.all(), math.pi / 2, ti.all(), tf.all(), tm.all())
        op("act", lambda e: e.activation(out=ROPC.ap.rearrange("p (a b) -> p a b", a=17), in_=red.ap, func=AF.Sin), r=[red.all()], w=[ROPC])

    def rope_apply(src4, dst4, rows, t_idx, nh, tA, tB):
        cosb = ROPC.ap[0:rows, t_idx * 32:(t_idx + 1) * 32].unsqueeze(1).to_broadcast([rows, nh, 32])
        sinb = ROPS.ap[0:rows, t_idx * 32:(t_idx + 1) * 32].unsqueeze(1).to_broadcast([rows, nh, 32])
        a = tA.ap[0:rows, 0:nh * 32].rearrange("p (h j) -> p h j", h=nh)
        b = tB.ap[0:rows, 0:nh * 32].rearrange("p (h j) -> p h j", h=nh)
        return cosb, sinb, a, b

    def mla_mem(u):
        sample = (u == 2)
        NT = NTS if sample else SEQ
        cur = [BIG]
        M = {}

        def TL(name, shape, dtype):
            b = ABuf(arena, cur[0], shape, dtype)
            cur[0] += (b.nbytes + 255) // 256 * 256
            assert cur[0] <= ARENA_BYTES, (name, cur[0])
            M[name] = b
            return b
        TL("latT", [2, NT], BF16)
        TL("krT2", [NT], BF16)
        TL("glat", [KVL], F32)
        TL("gq", [QL], F32)
        if not sample:
            TL("KnT", [NH, NT], BF16)
            TL("V", [16, D], BF16)
        else:
            TL("latb", [KVL], BF16)
            TL("wukT", [NH, KVL], BF16)
            TL("wuv", [2, D], BF16)
            TL("qlT", [2, NH, NTS], BF16)
            TL("qrTs", [NH, NTS], BF16)
            TL("Oall", [2, NH, NTS], F32)
            TL("Dall", [NH, NTS], F32)
            TL("olT", [2, NH, NTS], BF16)
            TL("mnew", [NTS], F32)
            for i in range(2):
                TL("pgl%d" % i, [4, KVL], BF16)
                TL("pgk%d" % i, [4, ROPE], BF16)
                TL("cT%d" % i, [2, 512], BF16)
                TL("krT%d" % i, [512], BF16)
                TL("pTs%d" % i, [128], BF16)
            TL("pTn", [512], BF16)
        CH = min(512, NT)
        TL("hTc", [8, CH], BF16)
        TL("hTt0", [8, 128], BF16)
        TL("hTt1", [8, 128], BF16)
        TL("latf", [KVL], F32)
        TL("krf", [ROPE], F32)
        TL("krb", [128], BF16)
        TL("cqb0", [QL], BF16)
        TL("cqb1", [QL], BF16)
        TL("cqT", [3, CH], BF16)
        TL("sgT", [NH, CH], BF16)
        TL("qnT", [NH, CH], BF16)
        TL("qrb0", [512], BF16)
        TL("qrb1", [512], BF16)
        TL("qrT", [4, CH], BF16)
        for i in range(4):
            TL("pT%d" % i, [CH], BF16)
        for i in range(2):
            TL("rden%d" % i, [CH], F32)
            TL("tgs%d" % i, [CH], F32)
            TL("ra%d" % i, [256], F32)
            TL("rb%d" % i, [256], F32)
        TL("ogT", [NH, CH], BF16)
        return M

    def latent(u, M):
        sample = (u == 2)
        NT = NTS if sample else SEQ
        rows = min(128, NT)
        ntile = NT // rows
        src, skey = x_src(u, 2)
        latT, krT2 = M["latT"], M["krT2"]
        wd = ABuf(arena, WREG, [8, KVL + ROPE], BF16)
        dma("pool", wd.ap, I["w_dkv"].rearrange("(k p) c -> p k c", p=128), w=[wd.all()])
        load_gain(I["norm_kv"])
        glat = M["glat"].all()
        dma("sp", glat.ap, I["norm_latent"].partition_broadcast(128), w=[glat])
        for tr in range(ntile):
            xt = XT[tr % 3].all()
            dma("sp", xt.ap[0:rows], src[tr * rows:(tr + 1) * rows, :], r=[skey + ":%d:0" % tr, skey + ":%d:1" % tr], w=[xt])
            hb = HB[tr % 2].all()
            rms_tile(xt, rows, D, gbuf[:], hb)
            pv = PBb(next_pb([0, 1]))
            for kt in range(8):
                op("pe", lambda e, kt=kt, pv=pv, hb=hb: e.transpose(pv.ap[:, kt * 128:kt * 128 + rows], hb.ap[0:rows, kt * 128:(kt + 1) * 128], idb[0:rows, 0:rows]), r=[hb, IDB], w=[pv])
            ht = M["hTt%d" % (tr % 2)]
            op("act", lambda e, pv=pv, ht=ht: e.copy(out=ht.ap[:, :, 0:rows], in_=pv.ap.rearrange("p (k c) -> p k c", k=8)[:, :, 0:rows]), r=[pv], w=[ht.all()])
            pc = PB(next_pb([2, 3]))
            for kt in range(8):
                op("pe", lambda e, kt=kt, pc=pc, ht=ht: e.matmul(pc.ap[0:rows, 0:KVL + ROPE], ht.ap[:, kt, 0:rows], wd.ap[:, kt, :], start=(kt == 0), stop=(kt == 7)),
                   r=[ht.all(), wd.all()], w=[pc])
            latf = M["latf"].all()
            pcl = T(pc.ap[:, 0:KVL], pc.k)
            rms_tile(pcl, rows, KVL, glat.ap, latf, gain_T=glat)
            okey = "lat:%d:%d" % (u, tr)
            odst = O["lat_p"][u] if not sample else O["lat_s"]
            dma("sp", odst[tr * rows:(tr + 1) * rows, :], latf.ap[0:rows], r=[latf], w=[okey])
            fin_keys.append(okey)
            lb_ = HB[tr % 2][0:KVL] if not sample else M["latb"].all()
            op("dve", lambda e, lb_=lb_, latf=latf: e.tensor_copy(lb_.ap[0:rows], latf.ap[0:rows]), r=[latf], w=[lb_])
            pv2 = PBb(next_pb([4, 5]))
            for c2 in range(2):
                op("pe", lambda e, c2=c2, pv2=pv2, lb_=lb_: e.transpose(pv2.ap[:, c2 * 128:c2 * 128 + rows], lb_.ap[0:rows, c2 * 128:(c2 + 1) * 128], idb[0:rows, 0:rows]), r=[lb_, IDB], w=[pv2])
            ld = latT[:, tr * rows:(tr + 1) * rows]
            op("act", lambda e, pv2=pv2, ld=ld: e.copy(out=ld.ap, in_=pv2.ap[:, 0:256].rearrange("p (k c) -> p k c", k=2)[:, :, 0:rows]), r=[pv2], w=[ld])
            t_idx = 16 if sample else tr
            krf = M["krf"].all()
            ra, rb = M["ra%d" % (tr % 2)].all(), M["rb%d" % (tr % 2)].all()
            cs_ = ROPC.ap[0:rows, t_idx * 32:(t_idx + 1) * 32]
            sn_ = ROPS.ap[0:rows, t_idx * 32:(t_idx + 1) * 32]
            x1 = pc.ap[0:rows, KVL:KVL + 32]
            x2 = pc.ap[0:rows, KVL + 32:KVL + 64]
            op("dve", lambda e, ra=ra, x1=x1, cs_=cs_: e.tensor_tensor(ra.ap[0:rows, 0:32], x1, cs_, ALU.mult), r=[pc, ROPC], w=[ra])
            op("dve", lambda e, rb=rb, x2=x2, sn_=sn_: e.tensor_tensor(rb.ap[0:rows, 0:32], x2, sn_, ALU.mult), r=[pc, ROPS], w=[rb])
            op("dve", lambda e, ra=ra, rb=rb, krf=krf: e.tensor_tensor(krf.ap[0:rows, 0:32], ra.ap[0:rows, 0:32], rb.ap[0:rows, 0:32], ALU.subtract), r=[ra, rb], w=[krf])
            op("dve", lambda e, ra=ra, x1=x1, sn_=sn_: e.tensor_tensor(ra.ap[0:rows, 32:64], x1, sn_, ALU.mult), r=[pc, ROPS], w=[ra])
            op("dve", lambda e, rb=rb, x2=x2, cs_=cs_: e.tensor_tensor(rb.ap[0:rows, 32:64], x2, cs_, ALU.mult), r=[pc, ROPC], w=[rb])
            op("dve", lambda e, ra=ra, rb=rb, krf=krf: e.tensor_tensor(krf.ap[0:rows, 32:64], ra.ap[0:rows, 32:64], rb.ap[0:rows, 32:64], ALU.add), r=[ra, rb], w=[krf])
            okey = "kr:%d:%d" % (u, tr)
            odst = O["kr_p"][u] if not sample else O["kr_s"]
            dma("sp", odst[tr * rows:(tr + 1) * rows, :], krf.ap[0:rows], r=[krf], w=[okey])
            fin_keys.append(okey)
            krb = M["krb"].all()
            op("dve", lambda e, krb=krb, krf=krf: e.tensor_copy(krb.ap[0:rows].rearrange("p (a b) -> p a b", a=2), krf.ap[0:rows].unsqueeze(1).to_broadcast([rows, 2, ROPE])), r=[krf], w=[krb])
            pv3 = PBb(next_pb([6, 7]))
            op("pe", lambda e, pv3=pv3, krb=krb: e.transpose(pv3.ap[:, 0:rows], krb.ap[0:rows, :], idb[0:rows, 0:rows]), r=[krb, IDB], w=[pv3])
            kd = krT2[tr * rows:(tr + 1) * rows]
            op("dve", lambda e, pv3=pv3, kd=kd: e.tensor_copy(kd.ap, pv3.ap[:, 0:rows]), r=[pv3], w=[kd])
        if not sample:
            wuk = ABuf(arena, WREG + 8192, [2, D], BF16)
            wuv = ABuf(arena, WREG + 12288, [2, D], BF16)
            dma("pool", wuk.ap, I["w_uk"].rearrange("(k p) c -> p k c", p=128), w=[wuk.all()])
            dma("pool", wuv.ap, I["w_uv"].rearrange("(k p) c -> p k c", p=128), w=[wuv.all()])
            KnT, V = M["KnT"], M["V"]
            for h in range(NH):
                for q in range(NT // 512):
                    cs = slice(q * 512, (q + 1) * 512)
                    pk = PB(next_pb([0, 1, 2, 3]))
                    for c2 in range(2):
                        op("pe", lambda e, pk=pk, c2=c2, h=h, cs=cs: e.matmul(pk.ap, wuk.ap[:, c2, h * 128:(h + 1) * 128], latT.ap[:, c2, cs], start=(c2 == 0), stop=(c2 == 1)),
                           r=[wuk.all(), latT[c2, cs]], w=[pk])
                    kd = KnT[h, cs]
                    if (h + q) % 2 == 0:
                        op("act", lambda e, pk=pk, kd=kd: e.copy(out=kd.ap, in_=pk.ap), r=[pk], w=[kd])
                    else:
                        op("dve", lambda e, pk=pk, kd=kd: e.tensor_copy(kd.ap, pk.ap), r=[pk], w=[kd])
            for r_ in range(16):
                for hc in range(2):
                    pk = PB(next_pb([4, 5, 6, 7]))
                    for c2 in range(2):
                        op("pe", lambda e, pk=pk, c2=c2, r_=r_, hc=hc: e.matmul(pk.ap, latT.ap[:, c2, r_ * 128:(r_ + 1) * 128], wuv.ap[:, c2, hc * 512:(hc + 1) * 512], start=(c2 == 0), stop=(c2 == 1)),
                           r=[wuv.all(), latT[c2, r_ * 128:(r_ + 1) * 128]], w=[pk])
                    vd = V[r_, hc * 512:(hc + 1) * 512]
                    if (r_ + hc) % 2 == 0:
                        op("act", lambda e, pk=pk, vd=vd: e.copy(out=vd.ap, in_=pk.ap), r=[pk], w=[vd])
                    else:
                        op("dve", lambda e, pk=pk, vd=vd: e.tensor_copy(vd.ap, pk.ap), r=[pk], w=[vd])

    def mla_layer(u, j, M):
        sample = (u == 2)
        NT = NTS if sample else SEQ
        rows = min(128, NT)
        CH = min(512, NT)
        NCH = NT // CH
        TPC = CH // rows
        li = 2 + j
        src, skey = x_src(u, li)
        dst, dkey = x_dst(u, li)
        hTc, cqT, sgT, qnT, qrT, ogT = M["hTc"], M["cqT"], M["sgT"], M["qnT"], M["qrT"], M["ogT"]
        gq = M["gq"].all()
        load_gain(I["norm_b"][j])
        dma("sp", gq.ap, I["norm_q"][j].partition_broadcast(128), w=[gq])
        wcq = ABuf(arena, WREG, [8, QL], BF16)
        wgt = [ABuf(arena, WREG + 6144 + i * 2048, [8, 128], BF16) for i in range(2)]
        wuq = ABuf(arena, WREG + 10240, [3, NH * 192], BF16)
        wob = ABuf(arena, WREG, [8, 512], BF16)
        if sample:
            sample_prep(M)
        for q in range(NCH):
            dma("pool", wcq.ap, I["w_in_b"][j][:, 0:QL].rearrange("(k p) c -> p k c", p=128), w=[wcq.all()])
            dma("pool", wuq.ap, I["w_uq"][j].rearrange("(k p) c -> p k c", p=128), w=[wuq.all()])
            build_hT(u, li, hTc, NT, ntiles=TPC, tile0=q * TPC)
            for t4 in range(TPC):
                ts_ = slice(t4 * rows, (t4 + 1) * rows)
                pc = PB(next_pb([2, 3]))
                for kt in range(8):
                    op("pe", lambda e, pc=pc, kt=kt, ts_=ts_: e.matmul(pc.ap[0:rows, 0:QL], hTc.ap[:, kt, ts_], wcq.ap[:, kt, :], start=(kt == 0), stop=(kt == 7)),
                       r=[hTc[kt, ts_], wcq.all()], w=[pc])
                cqb = M["cqb%d" % (t4 % 2)].all()
                rms_tile(T(pc.ap[:, 0:QL], pc.k), rows, QL, gq.ap, cqb, gain_T=gq)
                pv = PBb(next_pb([4, 5]))
                for c3 in range(3):
                    op("pe", lambda e, pv=pv, c3=c3, cqb=cqb: e.transpose(pv.ap[:, c3 * 128:c3 * 128 + rows], cqb.ap[0:rows, c3 * 128:(c3 + 1) * 128], idb[0:rows, 0:rows]), r=[cqb, IDB], w=[pv])
                cd = cqT[:, ts_]
                op("dve", lambda e, pv=pv, cd=cd: e.tensor_copy(cd.ap, pv.ap[:, 0:384].rearrange("p (k c) -> p k c", k=3)[:, :, 0:rows]), r=[pv], w=[cd])
            def load_gate(i):
                dma("pool", wgt[i % 2].ap, I["w_in_b"][j][:, QL + i * 128: QL + (i + 1) * 128].rearrange("(k p) c -> p k c", p=128), w=[wgt[i % 2].all()])
            load_gate(0)
            for i in range(NH):
                if i + 1 < NH:
                    load_gate(i + 1)
                pg = PB(next_pb([6, 7]))
                wg = wgt[i % 2]
                for kt in range(8):
                    op("pe", lambda e, pg=pg, kt=kt, wg=wg: e.matmul(pg.ap[:, 0:CH], wg.ap[:, kt, :], hTc.ap[:, kt, :], start=(kt == 0), stop=(kt == 7)), r=[wg.all(), hTc[kt]], w=[pg])
                sd = sgT[i]
                op("act", lambda e, pg=pg, sd=sd: e.activation(out=sd.ap, in_=pg.ap[:, 0:CH], func=AF.Silu), r=[pg], w=[sd])
            for h in range(NH):
                pq = PB(next_pb([0, 1]))
                for c3 in range(3):
                    op("pe", lambda e, pq=pq, c3=c3, h=h: e.matmul(pq.ap[:, 0:CH], wuq.ap[:, c3, h * 192:h * 192 + 128], cqT.ap[:, c3, :], start=(c3 == 0), stop=(c3 == 2)),
                       r=[wuq.all(), cqT[c3]], w=[pq])
                qd = qnT[h]
                op("dve", lambda e, pq=pq, qd=qd: e.tensor_copy(qd.ap, pq.ap[:, 0:CH]), r=[pq], w=[qd])
            wuq_r = wuq.ap.rearrange("p k (h d) -> p k h d", h=NH)
            for t4 in range(TPC):
                ts_ = slice(t4 * rows, (t4 + 1) * rows)
                pr_ = PB(next_pb([2, 3]))
                for c3 in range(3):
                    op("pe", lambda e, pr_=pr_, c3=c3, ts_=ts_: e.matmul(pr_.ap[0:rows, :].rearrange("p (h d) -> p h d", h=NH), cqT.ap[:, c3, ts_], wuq_r[:, c3, :, 128:192], start=(c3 == 0), stop=(c3 == 2)),
                       r=[wuq.all(), cqT[c3, ts_]], w=[pr_])
                t_idx = 16 if sample else (q * TPC + t4)
                ra, rb = M["ra%d" % (t4 % 2)].all(), M["rb%d" % (t4 % 2)].all()
                qrb = M["qrb%d" % (t4 % 2)].all()
                cosb = ROPC.ap[0:rows, t_idx * 32:(t_idx + 1) * 32].unsqueeze(1).to_broadcast([rows, NH, 32])
                sinb = ROPS.ap[0:rows, t_idx * 32:(t_idx + 1) * 32].unsqueeze(1).to_broadcast([rows, NH, 32])
                pr4 = pr_.ap[0:rows, :].rearrange("p (h d) -> p h d", h=NH)
                q1, q2 = pr4[:, :, 0:32], pr4[:, :, 32:64]
                rav = ra.ap[0:rows].rearrange("p (h d) -> p h d", h=NH)
                rbv = rb.ap[0:rows].rearrange("p (h d) -> p h d", h=NH)
                qv = qrb.ap[0:rows].rearrange("p (h d) -> p h d", h=NH)
                op("dve", lambda e, rav=rav, q1=q1, cosb=cosb: e.tensor_tensor(rav, q1, cosb, ALU.mult), r=[pr_, ROPC], w=[ra])
                op("dve", lambda e, rbv=rbv, q2=q2, sinb=sinb: e.tensor_tensor(rbv, q2, sinb, ALU.mult), r=[pr_, ROPS], w=[rb])
                op("pool", lambda e, qv=qv, rav=rav, rbv=rbv: e.tensor_tensor(qv[:, :, 0:32], rav, rbv, ALU.subtract), r=[ra, rb], w=[qrb])
                op("dve", lambda e, rav=rav, q1=q1, sinb=sinb: e.tensor_tensor(rav, q1, sinb, ALU.mult), r=[pr_, ROPS, qrb], w=[ra])
                op("dve", lambda e, rbv=rbv, q2=q2, cosb=cosb: e.tensor_tensor(rbv, q2, cosb, ALU.mult), r=[pr_, ROPC, qrb], w=[rb])
                op("pool", lambda e, qv=qv, rav=rav, rbv=rbv: e.tensor_tensor(qv[:, :, 32:64], rav, rbv, ALU.add), r=[ra, rb], w=[qrb])
                if not sample:
                    pv = PBb(next_pb([4, 5]))
                    for hp in range(4):
                        op("pe", lambda e, pv=pv, hp=hp, qrb=qrb: e.transpose(pv.ap[:, hp * 128:(hp + 1) * 128], qrb.ap[:, hp * 128:(hp + 1) * 128], idb[:]), r=[qrb, IDB], w=[pv])
                    qd = qrT[:, ts_]
                    op("act", lambda e, pv=pv, qd=qd: e.copy(out=qd.ap, in_=pv.ap[:, 0:512].rearrange("p (k c) -> p k c", k=4)), r=[pv], w=[qd])
                else:
                    pv = PBb(next_pb([4, 5]))
                    for h in range(NH):
                        op("pe", lambda e, pv=pv, h=h, qrb=qrb: e.transpose(pv.ap[0:64, h * 64:(h + 1) * 64], qrb.ap[0:64, h * 64:(h + 1) * 64], idb[0:64, 0:64]), r=[qrb, IDB], w=[pv])
                    qd = M["qrTs"].all()
                    op("act", lambda e, pv=pv, qd=qd: e.copy(out=qd.ap[0:64], in_=pv.ap[0:64, 0:512].rearrange("p (k c) -> p k c", k=NH)), r=[pv], w=[qd])
            if not sample:
                KnT, V, krT2 = M["KnT"], M["V"], M["krT2"]
                pti = 0
                for h in range(NH):
                    po = PB(next_pb([0, 1]))
                    pd = PB(next_pb([2, 3]))
                    nkt = 4 * q + 4
                    base = 64 * (h % 2)
                    def emit_pv(pt_, kt, c0, po=po, pd=pd, h=h, nkt=nkt):
                        op("pe", lambda e, po=po, pt_=pt_, h=h, kt=kt, c0=c0, nkt=nkt: e.matmul(po.ap[:, c0:512], V.ap[:, kt, h * 128:(h + 1) * 128], pt_.ap[:, c0:512], start=(kt == 0), stop=(kt == nkt - 1), skip_group_check=True),
                           r=[V[kt, h * 128:(h + 1) * 128], pt_], w=[po])
                        op("pe", lambda e, pd=pd, pt_=pt_, kt=kt, c0=c0, nkt=nkt: e.matmul(pd.ap[:, c0:512], onesb[:], pt_.ap[:, c0:512], start=(kt == 0), stop=(kt == nkt - 1), skip_group_check=True),
                           r=["onesb", pt_], w=[pd])
                    pend = None
                    for kt in range(nkt):
                        c0 = max(0, kt - 4 * q) * 128
                        ks = slice(kt * 128, (kt + 1) * 128)
                        ps_ = PB(next_pb([4, 5, 6, 7]))
                        op("pe", lambda e, ps_=ps_, h=h, ks=ks, c0=c0: e.matmul(ps_.ap[:, c0:512], KnT.ap[:, h, ks], qnT.ap[:, h, c0:512], start=True, stop=False),
                           r=[KnT[h, ks], qnT[h]], w=[ps_])
                        op("pe", lambda e, ps_=ps_, h=h, ks=ks, c0=c0, base=base: e.matmul(ps_.ap[:, c0:512], krT2.ap[base:base + 64, ks], qrT.ap[base:base + 64, h // 2, c0:512], start=False, stop=True),
                           r=[krT2[ks], qrT[h // 2]], w=[ps_])
                        pt_ = M["pT%d" % (pti % 4)].all()
                        pti += 1
                        op("act", lambda e, ps_=ps_, pt_=pt_, c0=c0: e.activation(out=pt_.ap[:, c0:512], in_=ps_.ap[:, c0:512], func=AF.Exp, scale=SCALE), r=[ps_], w=[pt_])
                        if kt >= 4 * q:
                            op("dve", lambda e, pt_=pt_, c0=c0: e.tensor_tensor(pt_.ap[:, c0:c0 + 128], pt_.ap[:, c0:c0 + 128], trib[:], ALU.mult), r=[pt_, "trib"], w=[pt_])
                        if pend is not None:
                            emit_pv(*pend)
                        pend = (pt_, kt, c0)
                    emit_pv(*pend)
                    rd = M["rden%d" % (h % 2)].all()
                    tg_ = M["tgs%d" % (h % 2)].all()
                    op("dve", lambda e, rd=rd, pd=pd: e.reciprocal(rd.ap, pd.ap), r=[pd], w=[rd])
                    op("pool", lambda e, tg_=tg_, rd=rd, h=h: e.tensor_tensor(tg_.ap, rd.ap, sgT.ap[:, h, :], ALU.mult), r=[rd, sgT[h]], w=[tg_])
                    od = ogT[h]
                    op("dve", lambda e, od=od, po=po, tg_=tg_: e.tensor_tensor(od.ap, po.ap, tg_.ap, ALU.mult), r=[po, tg_], w=[od])
            else:
                sample_attention(M, sgT, qnT, ogT)
            for hc in range(2):
                dma("pool", wob.ap, I["w_out_b"][j][:, hc * 512:(hc + 1) * 512].rearrange("(k p) c -> p k c", p=128), w=[wob.all()])
                for t4 in range(TPC):
                    tr = q * TPC + t4
                    ts_ = slice(t4 * rows, (t4 + 1) * rows)
                    xt = XT[(tr * 2 + hc) % 3][0:512]
                    dma("sp", xt.ap[0:rows], src[tr * rows:(tr + 1) * rows, hc * 512:(hc + 1) * 512], r=[skey + ":%d:%d" % (tr, hc)], w=[xt])
                    pp = PB(next_pb([6, 7]))
                    for h in range(NH):
                        op("pe", lambda e, pp=pp, h=h, ts_=ts_: e.matmul(pp.ap[0:rows, :], ogT.ap[:, h, ts_], wob.ap[:, h, :], start=(h == 0), stop=(h == NH - 1)),
                           r=[ogT[h, ts_], wob.all()], w=[pp])
                    op("dve", lambda e, pp=pp, xt=xt: e.tensor_tensor(xt.ap[0:rows], xt.ap[0:rows], pp.ap[0:rows, :], ALU.add), r=[xt, pp], w=[xt])
                    dma("sp", dst[tr * rows:(tr + 1) * rows, hc * 512:(hc + 1) * 512], xt.ap[0:rows], r=[xt], w=[dkey + ":%d:%d" % (tr, hc)])

    def sample_prep(M):
        if M.get("_prep"):
            return
        M["_prep"] = True
        wuk = ABuf(arena, WREG + 16384, [2, D], BF16)
        dma("pool", wuk.ap, I["w_uk"].rearrange("(k p) c -> p k c", p=128), w=[wuk.all()])
        dma("pool", M["wuv"].ap, I["w_uv"].rearrange("(k p) c -> p k c", p=128), w=[M["wuv"].all()])
        wukT = M["wukT"]
        for h in range(NH):
            pv = PBb(next_pb([4, 5]))
            for c2 in range(2):
                op("pe", lambda e, pv=pv, c2=c2, h=h: e.transpose(pv.ap[:, c2 * 128:(c2 + 1) * 128], wuk.ap[:, c2, h * 128:(h + 1) * 128], idb[:]), r=[wuk.all(), IDB], w=[pv])
            wd_ = wukT[h]
            op("dve", lambda e, pv=pv, wd_=wd_: e.tensor_copy(wd_.ap, pv.ap[:, 0:256]), r=[pv], w=[wd_])
        mn = M["mnew"].all()
        dma("sp", mn.ap[0:64], I["mnew"], w=[mn])

    def sample_attention(M, sgT, qnT, ogT):
        wukT, wuv, qlT, qrTs, latT, krT2, latb = M["wukT"], M["wuv"], M["qlT"], M["qrTs"], M["latT"], M["krT2"], M["latb"]
        Oall, Dall, olT = M["Oall"], M["Dall"], M["olT"]
        for h in range(NH):
            for c2 in range(2):
                pq = PB(next_pb([0, 1]))
                op("pe", lambda e, pq=pq, h=h, c2=c2: e.matmul(pq.ap[:, 0:NTS], wukT.ap[:, h, c2 * 128:(c2 + 1) * 128], qnT.ap[:, h, :], start=True, stop=True), r=[wukT[h], qnT[h]], w=[pq])
                qd = qlT[c2, h]
                op("dve", lambda e, pq=pq, qd=qd: e.tensor_copy(qd.ap, pq.ap[:, 0:NTS]), r=[pq], w=[qd])
        step = 0
        for s in range(NS_C):
            qs = slice(4 * s, 4 * s + 4)
            pacc = PB(2)
            pden = PB(3)
            for g in range(NPAGES // 4):
                b = step % 2
                step += 1
                pgl, pgk, cT, krT, pTs = M["pgl%d" % b], M["pgk%d" % b], M["cT%d" % b], M["krT%d" % b], M["pTs%d" % b]
                dma("sp", pgl.ap, PG_L[s, g * 4:(g + 1) * 4].rearrange("g t c -> t g c"), r=["pg_l:%d" % s], w=[pgl.all()])
                dma("sp", pgk.ap, PG_K[s, g * 4:(g + 1) * 4].rearrange("g t c -> t g c"), r=["pg_k:%d" % s], w=[pgk.all()])
                pva = PBb(4 + b)
                for i in range(4):
                    for c2 in range(2):
                        op("pe", lambda e, pva=pva, i=i, c2=c2, pgl=pgl: e.transpose(pva.ap[:, (c2 * 4 + i) * 128:(c2 * 4 + i + 1) * 128], pgl.ap[:, i, c2 * 128:(c2 + 1) * 128], idb[:]), r=[pgl[i], IDB], w=[pva])
                op("act", lambda e, pva=pva, cT=cT: e.copy(out=cT.ap, in_=pva.ap.rearrange("p (a b) -> p a b", a=2)), r=[pva], w=[cT.all()])
                pvb = PBb(6 + b)
                for i in range(4):
                    op("pe", lambda e, pvb=pvb, i=i, pgk=pgk: e.transpose(pvb.ap[0:64, i * 128:(i + 1) * 128], pgk.ap[:, i, :], idb[:]), r=[pgk[i], IDB], w=[pvb])
                op("dve", lambda e, pvb=pvb, krT=krT: e.tensor_copy(krT.ap[0:64], pvb.ap[0:64, 0:512]), r=[pvb], w=[krT.all()])
                psc = PB(next_pb([0, 1]))
                for i in range(4):
                    o_ = psc.ap[:, i * 32:(i + 1) * 32].rearrange("p (h t) -> p h t", h=NH)
                    for c2 in range(2):
                        op("pe", lambda e, o_=o_, i=i, c2=c2, cT=cT, qs=qs: e.matmul(o_, cT.ap[:, c2, i * 128:(i + 1) * 128], qlT.ap[:, c2, :, qs], start=(c2 == 0), stop=False),
                           r=[cT.all(), qlT[c2]], w=[psc])
                    op("pe", lambda e, o_=o_, i=i, krT=krT, qs=qs: e.matmul(o_, krT.ap[0:64, i * 128:(i + 1) * 128], qrTs.ap[0:64, :, qs], start=False, stop=True),
                       r=[krT.all(), qrTs.all()], w=[psc])
                op("act", lambda e, psc=psc, pTs=pTs: e.activation(out=pTs.ap, in_=psc.ap[:, 0:128], func=AF.Exp, scale=SCALE), r=[psc], w=[pTs.all()])
                for i in range(4):
                    for c2 in range(2):
                        first = (g == 0 and i == 0)
                        last = (g == NPAGES // 4 - 1 and i == 3)
                        op("pe", lambda e, pacc=pacc, i=i, c2=c2, pgl=pgl, pTs=pTs, first=first, last=last: e.matmul(pacc.ap[:, c2 * 32:(c2 + 1) * 32], pgl.ap[:, i, c2 * 128:(c2 + 1) * 128], pTs.ap[:, i * 32:(i + 1) * 32], start=first, stop=last, skip_group_check=True),
                           r=[pgl[i], pTs.all()], w=[pacc])
                    op("pe", lambda e, pden=pden, i=i, pTs=pTs, first=first, last=last: e.matmul(pden.ap[:, 0:32], onesb[:], pTs.ap[:, i * 32:(i + 1) * 32], start=first, stop=last, skip_group_check=True),
                       r=["onesb", pTs.all()], w=[pden])
            op("dve", lambda e, pacc=pacc, qs=qs: e.tensor_copy(Oall.ap[:, :, :, qs], pacc.ap[:, 0:64].rearrange("p (c h t) -> p c h t", c=2, h=NH)), r=[pacc], w=[Oall.all()])
            op("dve", lambda e, pden=pden, qs=qs: e.tensor_copy(Dall.ap[:, :, qs], pden.ap[:, 0:32].rearrange("p (h t) -> p h t", h=NH)), r=[pden], w=[Dall.all()])
        psn = PB(next_pb([0, 1]))
        for c2 in range(2):
            op("pe", lambda e, psn=psn, c2=c2: e.matmul(psn.ap[0:64, :], latT.ap[:, c2, 0:64], qlT.ap[:, c2].rearrange("p h t -> p (h t)"), start=(c2 == 0), stop=False), r=[latT[c2], qlT[c2]], w=[psn])
        op("pe", lambda e, psn=psn: e.matmul(psn.ap[0:64, :], krT2.ap[0:64, 0:64], qrTs.ap[0:64].rearrange("p h t -> p (h t)"), start=False, stop=True), r=[krT2.all(), qrTs.all()], w=[psn])
        pTn = M["pTn"].all()
        tgs = XT[0][0:512]
        op("act", lambda e, psn=psn: e.activation(out=tgs.ap[0:64], in_=psn.ap[0:64, :], func=AF.Exp, scale=SCALE), r=[psn], w=[tgs])
        mn = M["mnew"].all()
        op("dve", lambda e: e.tensor_tensor(pTn.ap[0:64].rearrange("p (h t) -> p h t", h=NH), tgs.ap[0:64].rearrange("p (h t) -> p h t", h=NH),
                                            mn.ap[0:64].unsqueeze(1).to_broadcast([64, NH, NTS]), ALU.mult), r=[tgs, mn], w=[pTn])
        for c2 in range(2):
            pn = PB(next_pb([4, 5]))
            op("pe", lambda e, pn=pn, c2=c2: e.matmul(pn.ap, latb.ap[0:64, c2 * 128:(c2 + 1) * 128], pTn.ap[0:64], start=True, stop=True), r=[latb.all(), pTn], w=[pn])
            od = Oall[c2]
            op("dve", lambda e, pn=pn, od=od: e.tensor_tensor(od.ap.rearrange("p h t -> p (h t)"), od.ap.rearrange("p h t -> p (h t)"), pn.ap, ALU.add), r=[pn, od], w=[od])
        pn = PB(next_pb([6, 7]))
        op("pe", lambda e, pn=pn: e.matmul(pn.ap, onesb[0:64, :], pTn.ap[0:64], start=True, stop=True), r=["onesb", pTn], w=[pn])
        Da = Dall.all()
        op("dve", lambda e, pn=pn: e.tensor_tensor(Dall.ap.rearrange("p h t -> p (h t)"), Dall.ap.rearrange("p h t -> p (h t)"), pn.ap, ALU.add), r=[pn, Da], w=[Da])
        op("dve", lambda e: e.reciprocal(Dall.ap, Dall.ap), r=[Da], w=[Da])
        for c2 in range(2):
            od = olT[c2]
            op("dve", lambda e, c2=c2, od=od: e.tensor_tensor(od.ap, Oall.ap[:, c2], Dall.ap, ALU.mult), r=[Oall[c2], Da], w=[od])
        for h in range(NH):
            pq = PB(next_pb([0, 1]))
            for c2 in range(2):
                op("pe", lambda e, pq=pq, h=h, c2=c2: e.matmul(pq.ap[:, 0:NTS], wuv.ap[:, c2, h * 128:(h + 1) * 128], olT.ap[:, c2, h, :], start=(c2 == 0), stop=(c2 == 1)), r=[wuv.all(), olT[c2]], w=[pq])
            od = ogT[h]
            op("dve", lambda e, pq=pq, od=od, h=h: e.tensor_tensor(od.ap, pq.ap[:, 0:NTS], sgT.ap[:, h, :], ALU.mult), r=[pq, sgT[h]], w=[od])

    def final_norm(u):
        sample = (u == 2)
        NT = NTS if sample else SEQ
        rows = min(128, NT)
        src, skey = x_src(u, 4)
        load_gain(I["norm_f"])
        for tr in range(NT // rows):
            xt = XT[tr % 3].all()
            dma("sp", xt.ap[0:rows], src[tr * rows:(tr + 1) * rows, :], r=[skey + ":%d:0" % tr, skey + ":%d:1" % tr], w=[xt])
            ot = XT[(tr + 1) % 3].all()
            rms_tile(xt, rows, D, gbuf[:], ot)
            key = "y:%d:%d" % (u, tr)
            odst = O["y_p"][u] if not sample else O["y_s"]
            dma("sp", odst[tr * rows:(tr + 1) * rows, :], ot.ap[0:rows], r=[ot], w=[key])
            fin_keys.append(key)

    PF["on"] = ("mla" in stages) and (2 in units)
    if "s5" in stages:
        for l in range(2):
            s5_tables(l)
            for u in units:
                s5_layer(u, l)

    prefetch_flush()
    if "lat" in stages:
        rope_tables()
        for u in units:
            M = mla_mem(u)
            latent(u, M)
            if "mla" in stages:
                for j in range(2):
                    mla_layer(u, j, M)
            if "fin" in stages:
                final_norm(u)

    if "dbg_e" in stages:
        s5_tables(0)
        for gp in range(4):
            eb = ABuf(arena, BIG, [2, 512], BF16)
            dma("sp", eb.ap, EB_S[0, gp], r=["eb_s0"], w=[eb.all()])
            xt = XT[gp % 3].all()
            op("dve", lambda e, xt=xt, eb=eb: e.tensor_copy(xt.ap, eb.ap.rearrange("p r c -> p (r c)")), r=[eb.all()], w=[xt])
            key = "dbg:%d" % gp
            dma("sp", O["y_p"][0][gp * 128:(gp + 1) * 128, :], xt.ap, r=[xt], w=[key])
            fin_keys.append(key)
        el = ABuf(arena, BIG + 4096, [NPAIR, 4], F32)
        dma("sp", el.ap, EL_S[0], r=["el_s0"], w=[el.all()])
        dma("sp", O["y_p"][1][0:128, 0:256], el.ap.rearrange("p a b -> p (a b)"), r=[el.all()], w=["dbg:el"])
        fin_keys.append("dbg:el")

    if "dbg_x" in stages:
        for u in units:
            NT = NTS if u == 2 else SEQ
            rows = min(128, NT)
            src, skey = x_src(u, 2)
            for tr in range(NT // rows):
                xt = XT[tr % 3].all()
                dma("sp", xt.ap[0:rows], src[tr * rows:(tr + 1) * rows, :], r=[skey + ":%d:0" % tr, skey + ":%d:1" % tr], w=[xt])
                key = "y:%d:%d" % (u, tr)
                dst = O["y_p"][u] if u < 2 else O["y_s"]
                dma("sp", dst[tr * rows:(tr + 1) * rows, :], xt.ap[0:rows], r=[xt], w=[key])
                fin_keys.append(key)

    S.emit(final_keys=fin_keys)
    global LAST_S
    LAST_S = S
    return nc


def _consts():
    ident = np.eye(128, dtype=np.float32)
    cst = np.zeros((128, 64), np.float32)
    g8 = (np.arange(128) // 16)
    cst[:, 0] = (g8 % 2 == 0)
    cst[:, 1] = (g8 % 2 == 1)
    cst[:, 2] = -cst[:, 0]
    cst[:, 3] = -cst[:, 1]
    mvec = np.concatenate([32.0 * np.arange(16), 1.0 + np.arange(32)]).astype(np.float32)
    cst[:, 8:56] = mvec[None, :]
    cst[:, 4] = np.arange(128)
    tri = (np.arange(128)[:, None] <= np.arange(128)[None, :]).astype(np.float32)
    pos = np.zeros((128, 17), np.float32)
    for t in range(16):
        pos[:, t] = t * 128 + np.arange(128)
    pos[:, 16] = 8192 + (np.arange(128) % 4)
    invf = (10000.0 ** (-np.arange(32, dtype=np.float32) / 32)).astype(np.float32)
    kk = np.arange(64)
    mnew = ((kk[:, None] // 4 == kk[None, :] // 4) & (kk[:, None] % 4 <= kk[None, :] % 4)).astype(np.float32)
    return dict(ident=ident, cst=cst, tri=tri, pos=pos, invf=invf, mnew=mnew)


_WEIGHT_KEYS = ["norm_a", "w_in_a", "a_re", "a_im", "log_dt", "b_re", "b_im", "c_re", "c_im", "d_skip", "w_glu", "b_glu",
                "w_out_a", "norm_kv", "w_dkv", "norm_latent", "w_uk", "w_uv", "norm_b", "w_in_b", "norm_q", "w_uq",
                "w_out_b", "norm_f"]


def make_in_maps(inputs, cores, npool=NPOOL):
    cs = _consts()
    f = lambda a: np.ascontiguousarray(np.asarray(a))
    cl = f(inputs["cache_latent"]).reshape(npool * 128, KVL)
    ck = f(inputs["cache_krope"]).reshape(npool * 128, ROPE)
    shared = {k: f(inputs[k]) for k in _WEIGHT_KEYS}
    maps = []
    for c in cores:
        m = dict(shared)
        m.update(cs)
        m["xp"] = f(inputs["x_prompt"][NSEQ_C * c:NSEQ_C * (c + 1)])
        m["xs"] = f(inputs["x_sample"][NS_C * c:NS_C * (c + 1)]).reshape(NTS, D)
        m["cl"] = cl
        m["ck"] = ck
        m["pt"] = f(inputs["page_table"][NS_C * c:NS_C * (c + 1)]).astype(np.int32)
        m["sre"] = f(inputs["state_ssm_re"][:, NS_C * c:NS_C * (c + 1)])
        m["sim"] = f(inputs["state_ssm_im"][:, NS_C * c:NS_C * (c + 1)])
        maps.append(m)
    return maps


def assemble(results):
    y_p = np.concatenate([r["y_p"] for r in results], axis=0)
    y_s = np.concatenate([r["y_s"].reshape(NS_C, DSEQ, D) for r in results], axis=0)
    lat_p = np.concatenate([r["lat_p"] for r in results], axis=0)
    kr_p = np.concatenate([r["kr_p"] for r in results], axis=0)
    lat_s = np.concatenate([r["lat_s"].reshape(NS_C, DSEQ, KVL) for r in results], axis=0)
    kr_s = np.concatenate([r["kr_s"].reshape(NS_C, DSEQ, ROPE) for r in results], axis=0)
    hre_p = np.concatenate([r["hre_p"] for r in results], axis=1)
    him_p = np.concatenate([r["him_p"] for r in results], axis=1)
    hre_s = np.concatenate([r["hre_s"] for r in results], axis=1)
    him_s = np.concatenate([r["him_s"] for r in results], axis=1)
    return tuple(np.ascontiguousarray(a, dtype=np.float32) for a in
                 (y_p, y_s, lat_p, kr_p, lat_s, kr_s, hre_p, him_p, hre_s, him_s))


def kernel(**inputs):
    nc = build_program()
    maps = make_in_maps(inputs, list(range(NCORES)))
    res = run_bass_kernel_spmd(nc, maps, core_ids=list(range(NCORES)))
    return assemble(res.results)
```
